# Optimizing a Trainium2 kernel written in Bass

```python
import math
import jax
import jax.numpy as jnp
from jax import lax
import numpy as np

D_MODEL = 4096
BATCH = 2
SEQ = 8192
DEPTH = 2

EPS = 1e-5
HEAD_DIM = 128
BLK = 128

SWA_Q_HEADS = 16
SWA_KV_HEADS = 4
SWA_WINDOW = 128
A_Q = SWA_Q_HEADS * HEAD_DIM
A_KV = SWA_KV_HEADS * HEAD_DIM
A_IN = A_Q + 2 * A_KV

DIL_PATTERNS = ((128, 1), (512, 4), (2048, 16))
DIL_HEADS = 12
B_W = DIL_HEADS * HEAD_DIM
B_IN = len(DIL_PATTERNS) * 3 * B_W

SSM_HEADS = 32
SSM_HEAD_DIM = 64
SSM_D_INNER = SSM_HEADS * SSM_HEAD_DIM
SSM_GROUPS = 8
SSM_STATE = 128
SSM_CONV = 4
SSM_CHUNK = 128
SSM_CONV_DIM = SSM_D_INNER + 2 * SSM_GROUPS * SSM_STATE
C_IN = SSM_D_INNER + SSM_CONV_DIM + SSM_HEADS

N_BRANCH = 3
GATE_IN = N_BRANCH * D_MODEL
IN_TOTAL = A_IN + B_IN + C_IN + GATE_IN
IN_SPLITS = [A_IN, A_IN + B_IN, A_IN + B_IN + C_IN]
BR_A, BR_B, BR_C = A_Q, B_W, SSM_D_INNER
MIX_WIDTH = BR_A + BR_B + BR_C

MEM_LEN = 256
XATTN_HEADS = 4
XATTN_DIM = XATTN_HEADS * HEAD_DIM

D_FF = -(-(8 * D_MODEL) // (3 * 256)) * 256

kernel_name = "hybrid_gated_swa_dilated_ssd_block"


def rms_norm(x, g):
    xf = x.astype(jnp.float32)
    y = xf * lax.rsqrt(jnp.mean(xf * xf, axis=-1, keepdims=True) + EPS)
    return (y * g.astype(jnp.float32)).astype(x.dtype)


def banded_attention(q, k, v, max_dist, sink=None):
    n_, seq, kvh, g, d = q.shape
    nb = seq // BLK
    qb = q.reshape(n_, nb, BLK, kvh, g, d)
    kb = k.reshape(n_, nb, BLK, kvh, d)
    vb = v.reshape(n_, nb, BLK, kvh, d)
    shift = lambda t: jnp.concatenate([jnp.zeros_like(t[:, :1]), t[:, :-1]], axis=1)
    kband = jnp.concatenate([shift(kb), kb], axis=2)
    vband = jnp.concatenate([shift(vb), vb], axis=2)
    s = jnp.einsum('nbqhgd,nbkhd->nbhgqk', qb, kband,
                   preferred_element_type=jnp.float32) * (d ** -0.5)
    qi = jnp.arange(BLK)[:, None]
    kj = jnp.arange(2 * BLK)[None, :]
    dist = BLK + qi - kj
    allowed = (dist >= 0) & (dist <= max_dist)
    not_first = jnp.arange(nb)[:, None, None] > 0
    allowed = allowed[None] & (not_first | (kj >= BLK)[None])
    s = jnp.where(allowed[None, :, None, None], s, -jnp.inf)
    if sink is None:
        lse = jax.nn.logsumexp(s, axis=-1)
    else:
        sk = jnp.broadcast_to(sink.astype(jnp.float32)[None, None, :, :, None, None],
                              s.shape[:-1] + (1,))
        lse = jax.nn.logsumexp(jnp.concatenate([s, sk], axis=-1), axis=-1)
    p = jnp.exp(s - lse[..., None])
    out = jnp.einsum('nbhgqk,nbkhd->nbqhgd', p.astype(v.dtype), vband)
    out = out.reshape(n_, seq, kvh, g, d)
    lse = jnp.transpose(lse, (0, 1, 4, 2, 3)).reshape(n_, seq, kvh, g)
    return out, lse


def swa_sink_attention(q, k, v, sink):
    b, seq, _ = q.shape
    g = SWA_Q_HEADS // SWA_KV_HEADS
    out, _ = banded_attention(q.reshape(b, seq, SWA_KV_HEADS, g, HEAD_DIM),
                              k.reshape(b, seq, SWA_KV_HEADS, HEAD_DIM),
                              v.reshape(b, seq, SWA_KV_HEADS, HEAD_DIM),
                              SWA_WINDOW - 1, sink.reshape(SWA_KV_HEADS, g))
    return out.reshape(b, seq, A_Q)


def dilated_attention(q, k, v, window, dilation):
    b, seq, h, d = q.shape
    ls = seq // dilation
    lp = -(-ls // BLK) * BLK

    def to_sub(t):
        t = t.reshape(b, ls, dilation, h, d).transpose(0, 2, 1, 3, 4).reshape(b * dilation, ls, h, d)
        return jnp.pad(t, ((0, 0), (0, lp - ls), (0, 0), (0, 0)))

    out, lse = banded_attention(to_sub(q)[:, :, :, None], to_sub(k), to_sub(v), window // dilation)
    out = out[:, :ls, :, 0].reshape(b, dilation, ls, h, d).transpose(0, 2, 1, 3, 4).reshape(b, seq, h, d)
    lse = lse[:, :ls, :, 0].reshape(b, dilation, ls, h).transpose(0, 2, 1, 3).reshape(b, seq, h)
    return out, lse


def dilated_mixture_attention(qkv):
    b, seq, _ = qkv.shape
    parts = qkv.reshape(b, seq, len(DIL_PATTERNS), 3, DIL_HEADS, HEAD_DIM)
    outs, lses = [], []
    for gi, (window, dilation) in enumerate(DIL_PATTERNS):
        o, l = dilated_attention(parts[:, :, gi, 0], parts[:, :, gi, 1], parts[:, :, gi, 2],
                                 window, dilation)
        outs.append(o)
        lses.append(l)
    alpha = jax.nn.softmax(jnp.stack(lses), axis=0)
    out = jnp.sum(alpha[..., None] * jnp.stack(outs).astype(jnp.float32), axis=0)
    return out.reshape(b, seq, B_W).astype(qkv.dtype)


def ssd_scan(x, dt, a, bm, cm):
    b, seq, g, r, p = x.shape
    n = bm.shape[-1]
    c = SSM_CHUNK
    nc = seq // c
    xc = (x * dt[..., None]).reshape(b, nc, c, g, r, p)
    bc = bm.reshape(b, nc, c, g, n)
    cc = cm.reshape(b, nc, c, g, n)
    a_cum = jnp.cumsum((dt * a).reshape(b, nc, c, g, r), axis=2)
    seg = a_cum[:, :, :, None] - a_cum[:, :, None]
    causal = jnp.tril(jnp.ones((c, c), dtype=bool))
    decay = jnp.exp(jnp.where(causal[None, None, :, :, None, None], seg, -jnp.inf))
    cb = jnp.einsum('bzlgn,bzsgn->bzlsg', cc, bc)
    y_diag = jnp.einsum('bzlsg,bzlsgr,bzsgrp->bzlgrp', cb, decay, xc)
    decay_states = jnp.exp(a_cum[:, :, -1:] - a_cum)
    states = jnp.einsum('bzsgn,bzsgr,bzsgrp->bzgrpn', bc, decay_states, xc)
    chunk_decay = jnp.exp(a_cum[:, :, -1])

    def step(h, inp):
        s_c, dec_c = inp
        return dec_c[..., None, None] * h + s_c, h

    h0 = jnp.zeros((b, g, r, p, n), states.dtype)
    _, prev = lax.scan(step, h0, (jnp.moveaxis(states, 1, 0), jnp.moveaxis(chunk_decay, 1, 0)))
    prev = jnp.moveaxis(prev, 0, 1)
    y_off = jnp.einsum('bzlgn,bzgrpn,bzlgr->bzlgrp', cc, prev, jnp.exp(a_cum))
    return (y_diag + y_off).reshape(b, seq, g, r, p)


def mamba2_mixer(zxbcdt, conv_w, conv_b, dt_bias, a_log, d_skip, norm_w):
    b, seq, _ = zxbcdt.shape
    f32 = jnp.float32
    z, xbc, dt = jnp.split(zxbcdt, [SSM_D_INNER, SSM_D_INNER + SSM_CONV_DIM], axis=-1)
    xbc = lax.conv_general_dilated(
        xbc, conv_w.astype(xbc.dtype)[:, None, :], window_strides=(1,),
        padding=[(SSM_CONV - 1, 0)], dimension_numbers=('NWC', 'WIO', 'NWC'),
        feature_group_count=SSM_CONV_DIM) + conv_b
    xbc = jax.nn.silu(xbc)
    xs, bm, cm = jnp.split(xbc, [SSM_D_INNER, SSM_D_INNER + SSM_GROUPS * SSM_STATE], axis=-1)
    r = SSM_HEADS // SSM_GROUPS
    xs = xs.astype(f32).reshape(b, seq, SSM_GROUPS, r, SSM_HEAD_DIM)
    bm = bm.astype(f32).reshape(b, seq, SSM_GROUPS, SSM_STATE)
    cm = cm.astype(f32).reshape(b, seq, SSM_GROUPS, SSM_STATE)
    dt = jax.nn.softplus(dt.astype(f32) + dt_bias.astype(f32)).reshape(b, seq, SSM_GROUPS, r)
    a = -jnp.exp(a_log.astype(f32)).reshape(SSM_GROUPS, r)
    y = ssd_scan(xs, dt, a, bm, cm) + d_skip.astype(f32).reshape(SSM_GROUPS, r)[..., None] * xs
    y = y.reshape(b, seq, SSM_D_INNER) * jax.nn.silu(z.astype(f32))
    yg = y.reshape(b, seq, SSM_GROUPS, SSM_D_INNER // SSM_GROUPS)
    yg = yg * lax.rsqrt(jnp.mean(yg * yg, axis=-1, keepdims=True) + EPS)
    return (yg.reshape(b, seq, SSM_D_INNER) * norm_w.astype(f32)).astype(zxbcdt.dtype)


def hybrid_mixer(h, w_in, attn_sink, conv_w, conv_b, dt_bias, a_log, d_skip, ssm_norm,
                 w_branch, w_out):
    b, seq, _ = h.shape
    proj = h @ w_in
    a_qkv, b_qkv, c_in, gate_logits = jnp.split(proj, IN_SPLITS, axis=-1)
    aq, ak, av = jnp.split(a_qkv, [A_Q, A_Q + A_KV], axis=-1)
    y_a = swa_sink_attention(aq, ak, av, attn_sink)
    y_b = dilated_mixture_attention(b_qkv)
    y_c = mamba2_mixer(c_in, conv_w, conv_b, dt_bias, a_log, d_skip, ssm_norm)
    wa, wb, wc = jnp.split(w_branch, [BR_A, BR_A + BR_B], axis=0)
    gates = jax.nn.sigmoid(gate_logits.astype(jnp.float32)).reshape(b, seq, N_BRANCH, D_MODEL)
    merged = (gates[:, :, 0] * (y_a @ wa) + gates[:, :, 1] * (y_b @ wb)
              + gates[:, :, 2] * (y_c @ wc))
    return merged.astype(h.dtype) @ w_out


def memory_cross_attention(h, m, wq, wk, wv, wo):
    b, seq, _ = h.shape
    q = (h @ wq).reshape(b, seq, XATTN_HEADS, HEAD_DIM)
    k = (m @ wk).reshape(b, m.shape[1], XATTN_HEADS, HEAD_DIM)
    v = (m @ wv).reshape(b, m.shape[1], XATTN_HEADS, HEAD_DIM)
    s = jnp.einsum('blhd,bmhd->bhlm', q, k, preferred_element_type=jnp.float32) * (HEAD_DIM ** -0.5)
    p = jax.nn.softmax(s, axis=-1)
    o = jnp.einsum('bhlm,bmhd->blhd', p.astype(v.dtype), v).reshape(b, seq, XATTN_DIM)
    return o @ wo


def swiglu(h, w_gate, w_up, w_down):
    return (jax.nn.silu(h @ w_gate) * (h @ w_up)) @ w_down


def setup_inputs(seed: int = 0) -> dict:
    key = jax.random.key(seed)
    ks = iter(jax.random.split(key, 32))
    f32 = jnp.float32

    def dense(shape, fan_in):
        return jax.random.normal(next(ks), shape, f32) * (fan_in ** -0.5)

    def gain(shape):
        return 1.0 + 0.05 * jax.random.normal(next(ks), shape, f32)

    x = jax.random.normal(next(ks), (BATCH, SEQ, D_MODEL), f32)
    mem = jax.random.normal(next(ks), (BATCH, MEM_LEN, D_MODEL), f32)
    norm_mix = gain((DEPTH, D_MODEL))
    w_in = dense((DEPTH, D_MODEL, IN_TOTAL), D_MODEL)
    attn_sink = 0.5 * jax.random.normal(next(ks), (DEPTH, SWA_Q_HEADS), f32)
    conv_w = dense((DEPTH, SSM_CONV, SSM_CONV_DIM), SSM_CONV)
    conv_b = 0.02 * jax.random.normal(next(ks), (DEPTH, SSM_CONV_DIM), f32)
    u = jax.random.uniform(next(ks), (DEPTH, SSM_HEADS), f32)
    dt = jnp.exp(u * (math.log(0.1) - math.log(0.001)) + math.log(0.001))
    dt_bias = dt + jnp.log(-jnp.expm1(-dt))
    a_log = jnp.log(jax.random.uniform(next(ks), (DEPTH, SSM_HEADS), f32, minval=1.0, maxval=16.0))
    d_skip = 1.0 + 0.1 * jax.random.normal(next(ks), (DEPTH, SSM_HEADS), f32)
    ssm_norm = gain((DEPTH, SSM_D_INNER))
    w_branch = dense((DEPTH, MIX_WIDTH, D_MODEL), MIX_WIDTH)
    w_out = dense((DEPTH, D_MODEL, D_MODEL), D_MODEL)
    norm_x = gain((DEPTH, D_MODEL))
    norm_mem = gain((DEPTH, D_MODEL))
    xattn_q = dense((DEPTH, D_MODEL, XATTN_DIM), D_MODEL)
    xattn_k = dense((DEPTH, D_MODEL, XATTN_DIM), D_MODEL)
    xattn_v = dense((DEPTH, D_MODEL, XATTN_DIM), D_MODEL)
    xattn_o = dense((DEPTH, XATTN_DIM, D_MODEL), XATTN_DIM)
    norm_ffn = gain((DEPTH, D_MODEL))
    ffn_gate = dense((DEPTH, D_MODEL, D_FF), D_MODEL)
    ffn_up = dense((DEPTH, D_MODEL, D_FF), D_MODEL)
    ffn_down = dense((DEPTH, D_FF, D_MODEL), D_FF)
    norm_final = gain((D_MODEL,))
    return {"x": x, "mem": mem, "norm_mix": norm_mix, "w_in": w_in, "attn_sink": attn_sink,
            "conv_w": conv_w, "conv_b": conv_b, "dt_bias": dt_bias, "a_log": a_log,
            "d_skip": d_skip, "ssm_norm": ssm_norm, "w_branch": w_branch, "w_out": w_out,
            "norm_x": norm_x, "norm_mem": norm_mem, "xattn_q": xattn_q, "xattn_k": xattn_k,
            "xattn_v": xattn_v, "xattn_o": xattn_o, "norm_ffn": norm_ffn, "ffn_gate": ffn_gate,
            "ffn_up": ffn_up, "ffn_down": ffn_down, "norm_final": norm_final}


def reference(x, mem, norm_mix, w_in, attn_sink, conv_w, conv_b, dt_bias, a_log, d_skip,
              ssm_norm, w_branch, w_out, norm_x, norm_mem, xattn_q, xattn_k, xattn_v, xattn_o,
              norm_ffn, ffn_gate, ffn_up, ffn_down, norm_final):
    for i in range(DEPTH):
        h = rms_norm(x, norm_mix[i])
        x = x + hybrid_mixer(h, w_in[i], attn_sink[i], conv_w[i], conv_b[i], dt_bias[i],
                             a_log[i], d_skip[i], ssm_norm[i], w_branch[i], w_out[i])
        m = rms_norm(mem, norm_mem[i])
        x = x + memory_cross_attention(rms_norm(x, norm_x[i]), m, xattn_q[i], xattn_k[i],
                                       xattn_v[i], xattn_o[i])
        x = x + swiglu(rms_norm(x, norm_ffn[i]), ffn_gate[i], ffn_up[i], ffn_down[i])
    return rms_norm(x, norm_final)
```

```python
import contextlib
import numpy as np
import ml_dtypes
import concourse.bass as bass
import concourse.mybir as mybir
from concourse.bass_utils import run_bass_kernel_spmd

F32 = mybir.dt.float32
BF16 = mybir.dt.bfloat16
AF = mybir.ActivationFunctionType
ALU = mybir.AluOpType
NPBF = ml_dtypes.bfloat16

EPS = 1e-5
NCORES = 8
SAME_ENG_SYNC = True
WAIT_SORT = True
DEBUG_OUT = False


class Cfg:
    def __init__(self, D=4096, S=8192, B=2, F=None):
        self.D, self.S, self.B = D, S, B
        self.F = F if F is not None else -(-(8 * D) // (3 * 256)) * 256
        self.KC = D // 128
        self.FC = self.F // 128
        self.NT = S // 4
        self.MEM = 256
        self.NA = 33 * 128 + 12 * 128 + 8


class Buf:
    __slots__ = ("name", "w", "r")

    def __init__(self, name=""):
        self.name, self.w, self.r = name, None, []


class Prog:
    ENGS = ("pe", "dve", "act", "pool", "sp")

    def __init__(self, nc):
        self.nc = nc
        self.streams = {e: [] for e in self.ENGS}
        self.cnt = {e: 0 for e in self.ENGS}
        self.waited = {e: {} for e in self.ENGS}
        self.dma_cnt = {}
        self.semkeys = list(self.ENGS)
        self.base = contextlib.ExitStack()
        self.stack = self.base
        self.nb = 0
        self.uid = 0

    def sbuf(self, name, shape, dtype):
        self.uid += 1
        return self.stack.enter_context(self.nc.sbuf_tensor(f"{name}_{self.uid}", list(shape), dtype))

    def psum(self, name, shape, dtype=F32):
        self.uid += 1
        return self.stack.enter_context(self.nc.psum_tensor(f"{name}_{self.uid}", list(shape), dtype))

    def buf(self, name=""):
        self.nb += 1
        return Buf(name or f"b{self.nb}")

    def bufs(self, n):
        return [self.buf() for _ in range(n)]

    def dma_sem(self, name):
        key = "dma_" + name
        if key not in self.dma_cnt:
            self.dma_cnt[key] = 0
            self.semkeys.append(key)
        return key

    def _deps(self, eng, reads, writes):
        need = {}

        def add(tok):
            if tok is None:
                return
            k, v = tok
            if k == eng and (eng == "pe" or not SAME_ENG_SYNC):
                return
            if need.get(k, 0) < v:
                need[k] = v
        for b in reads:
            add(b.w)
        for b in writes:
            add(b.w)
            for t in b.r:
                add(t)
        out = []
        wd = self.waited[eng]
        order = {"pe": 3, "dve": 1, "act": 1, "pool": 1}
        for k, v in sorted(need.items(), key=lambda kv: order.get(kv[0], 5) if WAIT_SORT else 0):
            if wd.get(k, 0) < v:
                wd[k] = v
                out.append((k, v))
        return out

    def _commit(self, tok, reads, writes):
        for b in reads:
            b.r.append(tok)
        for b in writes:
            b.w = tok
            b.r = []

    def op(self, eng, fn, reads=(), writes=()):
        waits = self._deps(eng, reads, writes)
        self.cnt[eng] += 1
        tok = (eng, self.cnt[eng])
        self.streams[eng].append((waits, fn, (eng, 1)))
        self._commit(tok, reads, writes)
        return tok

    def dma(self, queue, semkey, out_ap, in_ap, reads=(), writes=()):
        waits = self._deps(queue, reads, writes)
        self.dma_cnt[semkey] += 16
        tok = (semkey, self.dma_cnt[semkey])

        def fn(e, out_ap=out_ap, in_ap=in_ap):
            return e.dma_start(out=out_ap, in_=in_ap)
        self.streams[queue].append((waits, fn, (semkey, 16)))
        self._commit(tok, reads, writes)
        return tok

    def barrier(self):
        allt = [(k, v) for k, v in self.dma_cnt.items() if v > 0]
        allt += [(e, self.cnt[e]) for e in self.ENGS if self.cnt[e] > 0]
        for eng in self.ENGS:
            wd = self.waited[eng]
            waits = []
            for k, v in allt:
                if k == eng:
                    continue
                if wd.get(k, 0) < v:
                    wd[k] = v
                    waits.append((k, v))
            if waits:
                self.streams[eng].append((waits, None, None))

    @contextlib.contextmanager
    def stage(self):
        saved = self.stack
        self.stack = contextlib.ExitStack()
        try:
            yield
        finally:
            self.stack.close()
            self.stack = saved
            self.barrier()

    def emit(self):
        nc = self.nc
        self.barrier()
        sems = {}
        for k in self.semkeys:
            sems[k] = self.base.enter_context(nc.semaphore("s_" + k))

        def replay(name, e):
            for waits, fn, inc in self.streams[name]:
                for k, v in waits:
                    e.wait_ge(sems[k], v)
                if fn is not None:
                    ins = fn(e)
                    ins.then_inc(sems[inc[0]], inc[1])

        with nc.Block() as block:
            @block.tensor
            def _(e):
                replay("pe", e)

            @block.vector
            def _(e):
                replay("dve", e)

            @block.scalar
            def _(e):
                replay("act", e)

            @block.gpsimd
            def _(e):
                replay("pool", e)

            @block.sync
            def _(e):
                replay("sp", e)
        self.base.close()


class Kx:
    pass


def host_consts():
    s = np.arange(128)[:, None]
    l = np.arange(128)[None, :]
    ones = np.ones((128, 128), np.float32)
    ident = np.eye(128, dtype=np.float32)
    tri = (s <= l).astype(np.float32)
    maskneg = np.where(s <= l, 0.0, -1e30).astype(np.float32)
    sel = np.zeros((128, 8, 128), np.float32)
    for h in range(8):
        sel[h, h, :] = 1.0
    cf = np.concatenate([ones, ident, tri, maskneg, sel.reshape(128, 1024)], axis=1)
    own = (s <= l)
    prev_swa = (s > l)
    prev_dil = (s >= l)
    m_swa = np.stack([prev_swa, own], axis=1).astype(np.float32).reshape(128, 256)
    m_dil = np.stack([prev_dil, own], axis=1).astype(np.float32).reshape(128, 256)
    cb = np.concatenate([ones, ident, m_swa, m_dil], axis=1).astype(NPBF)
    return cf, cb


def load_consts(K):
    P, nc = K.P, K.nc
    cfd = nc.dram_tensor("cst_f", [128, 1536], F32, kind="ExternalInput").ap()
    cbd = nc.dram_tensor("cst_b", [128, 768], BF16, kind="ExternalInput").ap()
    cf = P.sbuf("cf", [128, 1536], F32)
    cb = P.sbuf("cb", [128, 768], BF16)
    K.bc = P.buf("consts")
    s = P.dma_sem("cst")
    P.dma("sp", s, cf[:], cfd, writes=[K.bc])
    t = P.dma("sp", s, cb[:], cbd, writes=[K.bc])
    K.ones32 = cf[:, 0:128]
    K.ident32 = cf[:, 128:256]
    K.tri = cf[:, 256:384]
    K.maskneg = cf[:, 384:512]
    K.sel = cf[:, 512:1536].rearrange("p (h m) -> p h m", h=8)
    K.onesb = cb[:, 0:128]
    K.identb = cb[:, 128:256]
    K.m_swa = cb[:, 256:512].rearrange("p (s q) -> p s q", s=2)
    K.m_dil = cb[:, 512:768].rearrange("p (s q) -> p s q", s=2)
    P.barrier()


def dram(K, name, shape, dt, kind="Internal"):
    if DEBUG_OUT and kind == "Internal":
        kind = "ExternalOutput"
    return K.nc.dram_tensor(name, list(shape), dt, kind=kind).ap()


def st_rmsnorm(K, src, dst, gain_d, KC, NT, D, out_dt=BF16, TT=256):
    P = K.P
    TT = min(TT, NT)
    with P.stage():
        g = P.sbuf("rg", [128, KC], F32)
        bg = P.buf()
        P.dma("sp", P.dma_sem("rg"), g[:], gain_d, writes=[bg])
        NBF = 2
        KG = (KC + 7) // 8
        xt = [P.sbuf("rx", [128, KC, TT], F32) for _ in range(NBF)]
        bx = [P.bufs(KG) for _ in range(NBF)]
        sx = [P.dma_sem(f"rx{i}") for i in range(NBF)]
        ho = [P.sbuf("rh", [128, KC, TT], out_dt) for _ in range(NBF)]
        bh = [P.bufs(KG) for _ in range(NBF)]
        sh = [P.dma_sem(f"rh{i}") for i in range(NBF)]
        sq = [P.sbuf("rs", [128, TT], F32) for _ in range(4)]
        bsq = P.bufs(4)
        ps = [P.psum("rp", [128, 512])[:, 0:TT] for _ in range(2)]
        bps = P.bufs(2)
        rs = [P.sbuf("rr", [128, TT], F32) for _ in range(2)]
        brs = P.bufs(2)
        dd = P.buf()
        qi = 0
        for ti, t0 in enumerate(range(0, NT, TT)):
            s = ti % NBF
            pi = ti % 2
            for kg in range(KG):
                k0, k1 = kg * 8, min(KC, kg * 8 + 8)
                P.dma("sp", sx[s], xt[s][:, k0:k1, :], src[k0:k1, :, t0:t0 + TT].rearrange("k p t -> p k t"),
                      writes=[bx[s][kg]])
            for kc in range(KC):
                q = qi % 4
                qi += 1
                P.op("act", lambda e, q=q, s=s, kc=kc: e.activation(out=sq[q][:], in_=xt[s][:, kc, :], func=AF.Square),
                     reads=[bx[s][kc // 8]], writes=[bsq[q]])
                P.op("pe", lambda e, q=q, pi=pi, kc=kc: e.matmul(ps[pi][:], K.ones32, sq[q][:], start=(kc == 0), stop=(kc == KC - 1)),
                     reads=[bsq[q], K.bc], writes=[bps[pi]])
            P.op("dve", lambda e, pi=pi: e.tensor_scalar(out=rs[pi][:], in0=ps[pi][:], scalar1=1.0 / D, scalar2=EPS,
                                                         op0=ALU.mult, op1=ALU.add), reads=[bps[pi]], writes=[brs[pi]])
            P.op("act", lambda e, pi=pi: e.activation(out=rs[pi][:], in_=rs[pi][:], func=AF.Sqrt), reads=[brs[pi]], writes=[brs[pi]])
            P.op("dve", lambda e, pi=pi: e.reciprocal(out=rs[pi][:], in_=rs[pi][:]), reads=[brs[pi]], writes=[brs[pi]])
            for kc in range(KC):
                P.op("dve", lambda e, s=s, kc=kc, pi=pi: e.scalar_tensor_tensor(
                    out=ho[s][:, kc, :], in0=xt[s][:, kc, :], scalar=g[:, kc:kc + 1], in1=rs[pi][:],
                    op0=ALU.mult, op1=ALU.mult), reads=[bx[s][kc // 8], brs[pi], bg], writes=[bh[s][kc // 8]])
            for kg in range(KG):
                k0, k1 = kg * 8, min(KC, kg * 8 + 8)
                P.dma("sp", sh[s], dst[k0:k1, :, t0:t0 + TT].rearrange("k p t -> p k t"), ho[s][:, k0:k1, :],
                      reads=[bh[s][kg]], writes=[dd])


def st_linear(K, act, KCa, W, units, NT, T, epi_factory):
    P = K.P
    T = min(T, NT)
    NH = T // 512 if T >= 512 else 1
    TW = min(T, 512)
    with P.stage():
        at = P.sbuf("la", [128, KCa, T], BF16)
        KGa = (KCa + 7) // 8
        bat = P.bufs(KGa)
        sat = P.dma_sem("la")
        NS, NR = 4, 8
        stg = [P.sbuf("ls", [128, 8, 256], F32) for _ in range(NS)]
        bst = P.bufs(NS)
        sst = [P.dma_sem(f"ls{i}") for i in range(NS)]
        ring = [P.sbuf("lr", [128, 8, 256], BF16) for _ in range(NR)]
        brg = P.bufs(NR)
        nsets = 8 // (2 * NH)
        pss = [[[P.psum("lp", [128, 512]) for _ in range(NH)] for _ in range(2)] for _ in range(nsets)]
        bpss = [[[P.buf() for _ in range(NH)] for _ in range(2)] for _ in range(nsets)]
        epi = epi_factory(P)
        Wv = W.rearrange("(kc p) n -> p kc n", p=128)
        loads = []
        for ti, t0 in enumerate(range(0, NT, T)):
            for u, un in enumerate(units):
                ng = (un["nk"] + 7) // 8
                for gk in range(ng):
                    loads.append((ti, t0, u, gk, ng))
        cast_rot = ["dve", "act", "pool", "dve", "act"]
        state = {"next": 0}

        def emit_load(i):
            ti, t0, u, gk, ng = loads[i]
            un = units[u]
            st, r = i % NS, i % NR
            kk0 = un["k0"] + gk * 8
            nk = min(8, un["nk"] - gk * 8)
            ncol = un["nc"]
            P.dma("sp", sst[st], stg[st][:, 0:nk, 0:ncol], Wv[:, kk0:kk0 + nk, un["c0"]:un["c0"] + ncol], writes=[bst[st]])
            ce = cast_rot[i % len(cast_rot)]
            if ce == "act":
                P.op("act", lambda e, st=st, r=r, nk=nk, ncol=ncol: e.activation(func=AF.Copy, out=ring[r][:, 0:nk, 0:ncol], in_=stg[st][:, 0:nk, 0:ncol]),
                     reads=[bst[st]], writes=[brg[r]])
            else:
                P.op(ce, lambda e, st=st, r=r, nk=nk, ncol=ncol: e.tensor_copy(out=ring[r][:, 0:nk, 0:ncol], in_=stg[st][:, 0:nk, 0:ncol]),
                     reads=[bst[st]], writes=[brg[r]])

        LA = 5
        uc = 0
        for i, (ti, t0, u, gk, ng) in enumerate(loads):
            un = units[u]
            if u == 0 and gk == 0:
                for kg in range(KGa):
                    k0, k1 = kg * 8, min(KCa, kg * 8 + 8)
                    P.dma("sp", sat, at[:, k0:k1, :], act[k0:k1, :, t0:t0 + T].rearrange("k p t -> p k t"), writes=[bat[kg]])
            while state["next"] <= min(i + LA, len(loads) - 1):
                emit_load(state["next"])
                state["next"] += 1
            if gk == 0:
                sset = uc % nsets
                uc += 1
            r = i % NR
            kk0 = un["k0"] + gk * 8
            nk = min(8, un["nk"] - gk * 8)
            nbk = (un["nc"] + 127) // 128
            ps = pss[sset]
            bps = bpss[sset]

            def mm(e, r=r, kk0=kk0, nk=nk, nbk=nbk, ps=ps, gk=gk, ng=ng, un=un):
                ins = None
                for nb in range(nbk):
                    ncol = min(128, un["nc"] - nb * 128)
                    for kc in range(nk):
                        for th in range(NH):
                            ins = e.matmul(ps[nb][th][0:ncol, 0:TW], ring[r][:, kc, nb * 128:nb * 128 + ncol],
                                           at[:, kk0 + kc, th * TW:(th + 1) * TW],
                                           start=(gk == 0 and kc == 0), stop=(gk == ng - 1 and kc == nk - 1))
                return ins
            kgs = sorted(set((kk0 + j) // 8 for j in range(nk)))
            P.op("pe", mm, reads=[brg[r]] + [bat[k] for k in kgs],
                 writes=[bps[nb][th] for nb in range(nbk) for th in range(NH)])
            if gk == ng - 1:
                epi(u, un, t0, ps, bps)


class Stager:
    def __init__(self, P, name, dt, n=4, width=512):
        self.P = P
        self.t = [P.sbuf(name, [128, width], dt) for _ in range(n)]
        self.b = P.bufs(n)
        self.s = [P.dma_sem(f"{name}{i}") for i in range(n)]
        self.i = 0
        self.n = n
        self.dd = P.buf()

    def next(self):
        j = self.i % self.n
        self.i += 1
        return j


def epi_store_factory(K, route, TW=512):
    def factory(P):
        sb = Stager(P, "eb", BF16)
        sf = Stager(P, "ef", F32)
        cnt = {"i": 0}

        def epi(u, un, t0, ps, bps):
            nbk = (un["nc"] + 127) // 128
            for nb in range(nbk):
                dst, ch, nrows, dk, func = route(un, nb)
                S_ = sb if dk == "b" else sf
                for th in range(len(ps[nb])):
                    j = S_.next()
                    tw = min(TW, dst.shape[2] - t0)
                    cnt["i"] += 1
                    if func is None and cnt["i"] % 2 == 0:
                        P.op("dve", lambda e, j=j, S_=S_, nb=nb, th=th, nrows=nrows, tw=tw: e.tensor_copy(
                            out=S_.t[j][0:nrows, 0:tw], in_=ps[nb][th][0:nrows, 0:tw]), reads=[bps[nb][th]], writes=[S_.b[j]])
                    else:
                        f = AF.Copy if func is None else func
                        P.op("act", lambda e, j=j, S_=S_, nb=nb, th=th, nrows=nrows, tw=tw, f=f: e.activation(
                            out=S_.t[j][0:nrows, 0:tw], in_=ps[nb][th][0:nrows, 0:tw], func=f), reads=[bps[nb][th]], writes=[S_.b[j]])
                    P.dma("sp", S_.s[j], dst[ch, 0:nrows, t0 + th * tw:t0 + (th + 1) * tw], S_.t[j][0:nrows, 0:tw],
                          reads=[S_.b[j]], writes=[S_.dd])
        return epi
    return factory


def simple_units(N, KC, k0=0):
    return [dict(c0=c, nc=min(256, N - c), k0=k0, nk=KC) for c in range(0, N, 256)]


def st_attention(K, QKV, Y, S, sink_d):
    P = K.P
    ST = min(2048, S)
    scale = 128 ** -0.5
    with P.stage():
        sk = P.sbuf("sk", [128, 4], F32)
        bsk = P.buf()
        P.dma("sp", P.dma_sem("sk"), sk[:], sink_d, writes=[bsk])
        P.op("act", lambda e: e.activation(out=sk[:], in_=sk[:], func=AF.Exp), reads=[bsk], writes=[bsk])
        W = 2 * ST if ST == 2048 else ST + 2048
        NJ = 2
        kbuf = [P.sbuf("ak", [128, W], BF16) for _ in range(NJ)]
        vbuf = [P.sbuf("av", [128, W], BF16) for _ in range(NJ)]
        qbuf = [P.sbuf("aq", [128, 4, ST], BF16) for _ in range(NJ)]
        bk, bv, bq = P.bufs(NJ), P.bufs(NJ), P.bufs(NJ)
        sk_, sv_, sq_ = [P.dma_sem(f"ak{i}") for i in range(NJ)], [P.dma_sem(f"av{i}") for i in range(NJ)], [P.dma_sem(f"aq{i}") for i in range(NJ)]
        vtok = [P.sbuf("avt", [128, W], BF16) for _ in range(NJ)]
        bvt = P.bufs(NJ)
        pst = [P.psum("apt", [128, 512]) for _ in range(2)]
        bpst = P.bufs(2)
        scb = [[P.psum("asc", [128, 512]) for _ in range(2)] for _ in range(2)]
        bsc = [P.bufs(2) for _ in range(2)]
        pn = P.psum("apn", [128, 512])
        pd = P.psum("apd", [128, 512])
        bpn, bpd = P.buf(), P.buf()
        et = [P.sbuf("aet", [128, 2, 4, 128], BF16) for _ in range(2)]
        bet = P.bufs(2)
        dsb = [P.sbuf("ads", [128, 4, 128], F32) for _ in range(2)]
        bds = P.bufs(2)
        ya = [P.sbuf("aya", [128, 4, ST], BF16) for _ in range(1)]
        bya = P.buf()
        sya = P.dma_sem("aya")
        accN = [P.sbuf("aN", [128, ST], F32) for _ in range(2)]
        accD = [P.sbuf("aD", [128, ST], F32) for _ in range(2)]
        bacc = P.bufs(2)
        yb = [P.sbuf("ayb", [128, ST], BF16) for _ in range(2)]
        byb = P.bufs(2)
        syb = [P.dma_sem(f"ayb{i}") for i in range(2)]
        dd_ = P.buf()
        cnt = {"job": 0, "blk": 0, "tr": 0}

        def job(I, dd, qch, kch, vch, mask, mode, acc_i=None, first=False):
            js = cnt["job"] % NJ
            cnt["job"] += 1
            PREV = 128 * dd
            has_prev = I > 0
            Mb = ST // PREV
            nq = len(qch)
            lo = I * ST
            if has_prev:
                P.dma("sp", sk_[js], kbuf[js][:, 0:PREV + ST], QKV[kch, :, lo - PREV:lo + ST], writes=[bk[js]])
                P.dma("sp", sv_[js], vbuf[js][:, 0:PREV + ST], QKV[vch, :, lo - PREV:lo + ST], writes=[bv[js]])
            else:
                P.dma("sp", sk_[js], kbuf[js][:, PREV:PREV + ST], QKV[kch, :, lo:lo + ST], writes=[bk[js]])
                P.dma("sp", sv_[js], vbuf[js][:, PREV:PREV + ST], QKV[vch, :, lo:lo + ST], writes=[bv[js]])
            for h in range(nq):
                P.dma("sp", sq_[js], qbuf[js][:, h, :], QKV[qch[h], :, lo:lo + ST], writes=[bq[js]])
            kv = kbuf[js][:, 0:PREV + ST].rearrange("p (m i r) -> p r m i", i=128, r=dd)
            vv = vbuf[js][:, 0:PREV + ST].rearrange("p (m i r) -> p r m i", i=128, r=dd)
            vt = vtok[js][:, 0:PREV + ST].rearrange("p (r m i) -> p r m i", i=128, r=dd)
            qv = qbuf[js][:, :, :].rearrange("p h (m i r) -> p h r m i", i=128, r=dd)
            lvl = getattr(K, "att_lvl", 9)
            blocks = [(r, m) for r in range(dd) for m in range(0 if has_prev else 1, Mb + 1)]
            if lvl < 2:
                blocks = []
            groups = []
            for (r, m) in blocks:
                if groups and groups[-1][0] == r and groups[-1][1] + len(groups[-1][2]) == m and len(groups[-1][2]) < 4:
                    groups[-1][2].append(m)
                else:
                    groups.append([r, m, [m]])
            for (r, m0, ms) in groups:
                tb = cnt["tr"] % 2
                cnt["tr"] += 1
                n = len(ms)

                def trs(e, tb=tb, r=r, ms=ms):
                    ins = None
                    for j, m in enumerate(ms):
                        ins = e.matmul(pst[tb][:, j * 128:(j + 1) * 128], vv[:, r, m, :], K.identb, start=True, stop=True)
                    return ins
                P.op("pe", trs, reads=[bv[js], K.bc], writes=[bpst[tb]])
                if tb == 0:
                    P.op("act", lambda e, tb=tb, r=r, m0=m0, n=n: e.activation(out=vt[:, r, m0:m0 + n, :], in_=pst[tb][:, 0:n * 128].rearrange("p (m i) -> p m i", i=128), func=AF.Copy),
                         reads=[bpst[tb]], writes=[bvt[js]])
                else:
                    P.op("dve", lambda e, tb=tb, r=r, m0=m0, n=n: e.tensor_copy(out=vt[:, r, m0:m0 + n, :], in_=pst[tb][:, 0:n * 128].rearrange("p (m i) -> p m i", i=128)),
                         reads=[bpst[tb]], writes=[bvt[js]])
            for r in range(dd if lvl >= 3 else 0):
                for m in range(1, Mb + 1):
                    bi = cnt["blk"] % 2
                    cnt["blk"] += 1
                    two = has_prev or m > 1
                    slots = [(0, m - 1), (1, m)] if two else [(1, m)]
                    for (sl, kbm) in slots:
                        P.op("pe", lambda e, bi=bi, sl=sl, kbm=kbm, r=r, m=m: e.matmul(
                            scb[bi][sl][:, 0:nq * 128].rearrange("p (h q) -> p h q", h=nq), kv[:, r, kbm, :], qv[:, 0:nq, r, m - 1, :],
                            start=True, stop=True), reads=[bk[js], bq[js]], writes=[bsc[bi][sl]])
                        if lvl < 4:
                            continue
                        P.op("act", lambda e, bi=bi, sl=sl: e.activation(
                            out=et[bi][:, sl, 0:nq, :], in_=scb[bi][sl][:, 0:nq * 128].rearrange("p (h q) -> p h q", h=nq),
                            func=AF.Exp, scale=scale), reads=[bsc[bi][sl]], writes=[bet[bi]])
                        for h in range(nq if lvl >= 5 else 0):
                            P.op("pool", lambda e, bi=bi, sl=sl, h=h: e.tensor_tensor(
                                out=et[bi][:, sl, h, :], in0=et[bi][:, sl, h, :], in1=mask[:, sl, :], op=ALU.mult),
                                reads=[bet[bi], K.bc], writes=[bet[bi]])

                    def pv(e, bi=bi, slots=slots, r=r):
                        ins = None
                        for j, (sl, kbm) in enumerate(slots):
                            e.matmul(pn[:, 0:nq * 128].rearrange("p (h q) -> p h q", h=nq), vt[:, r, kbm, :], et[bi][:, sl, 0:nq, :],
                                     start=(j == 0), stop=(j == len(slots) - 1))
                        for j, (sl, kbm) in enumerate(slots):
                            ins = e.matmul(pd[:, 0:nq * 128].rearrange("p (h q) -> p h q", h=nq), K.onesb, et[bi][:, sl, 0:nq, :],
                                           start=(j == 0), stop=(j == len(slots) - 1))
                        return ins
                    if lvl < 6:
                        continue
                    P.op("pe", pv, reads=[bet[bi], bvt[js], K.bc], writes=[bpn, bpd])
                    if lvl < 7:
                        continue
                    if mode == "swa":
                        for h in range(nq):
                            P.op("dve", lambda e, bi=bi, h=h: e.tensor_scalar(
                                out=dsb[bi][:, h, :], in0=pd[:, h * 128:(h + 1) * 128], scalar1=sk[:, h:h + 1], scalar2=None,
                                op0=ALU.add), reads=[bpd, bsk], writes=[bds[bi]])
                        P.op("dve", lambda e, bi=bi: e.reciprocal(out=dsb[bi][:, 0:nq, :], in_=dsb[bi][:, 0:nq, :]),
                             reads=[bds[bi]], writes=[bds[bi]])
                        yv = ya[0][:, :, :].rearrange("p h (m i r) -> p h r m i", i=128, r=dd)
                        P.op("dve", lambda e, bi=bi, r=r, m=m, yv=yv: e.tensor_tensor(
                            out=yv[:, :, r, m - 1, :], in0=pn[:, 0:nq * 128].rearrange("p (h q) -> p h q", h=nq),
                            in1=dsb[bi][:, 0:nq, :], op=ALU.mult), reads=[bpn, bds[bi]], writes=[bya])
                    else:
                        aN = accN[acc_i][:, :].rearrange("p (m i r) -> p r m i", i=128, r=dd)[:, r, m - 1, :]
                        aD = accD[acc_i][:, :].rearrange("p (m i r) -> p r m i", i=128, r=dd)[:, r, m - 1, :]
                        if first:
                            P.op("act", lambda e, aN=aN: e.activation(func=AF.Copy, out=aN, in_=pn[:, 0:128]), reads=[bpn], writes=[bacc[acc_i]])
                            P.op("dve", lambda e, aD=aD: e.tensor_copy(out=aD, in_=pd[:, 0:128]), reads=[bpd], writes=[bacc[acc_i]])
                        else:
                            P.op("dve", lambda e, aN=aN: e.tensor_tensor(out=aN, in0=pn[:, 0:128], in1=aN, op=ALU.add),
                                 reads=[bpn, bacc[acc_i]], writes=[bacc[acc_i]])
                            P.op("dve", lambda e, aD=aD: e.tensor_tensor(out=aD, in0=pd[:, 0:128], in1=aD, op=ALU.add),
                                 reads=[bpd, bacc[acc_i]], writes=[bacc[acc_i]])

        for I in range(S // ST):
            lo = I * ST
            jobs_on = getattr(K, "att_jobs", None)
            if jobs_on is None or "swa" in jobs_on:
                job(I, 1, [0, 1, 2, 3], 4, 5, K.m_swa, "swa")
                for h in range(4):
                    P.dma("sp", sya, Y[h, :, lo:lo + ST], ya[0][:, h, :], reads=[bya], writes=[dd_])
            dils = [(gi, dil) for gi, dil in enumerate((1, 4, 16)) if jobs_on is None or f"d{dil}" in jobs_on]
            for hs in range(3):
                if not dils:
                    break
                ai = hs % 2
                for gi, dil in dils:
                    base = 6 + 9 * gi
                    job(I, dil, [base + hs], base + 3 + hs, base + 6 + hs, K.m_dil, "dil", acc_i=ai, first=(gi == dils[0][0]))
                P.op("dve", lambda e, ai=ai: e.reciprocal(out=accD[ai][:], in_=accD[ai][:]), reads=[bacc[ai]], writes=[bacc[ai]])
                P.op("dve", lambda e, ai=ai: e.tensor_tensor(out=yb[ai][:], in0=accN[ai][:], in1=accD[ai][:], op=ALU.mult),
                     reads=[bacc[ai]], writes=[byb[ai]])
                P.dma("sp", syb[ai], Y[4 + hs, :, lo:lo + ST], yb[ai][:], reads=[byb[ai]], writes=[dd_])


def st_ssd(K, SSDP, DTr, Y, S, prm):
    P, nc = K.P, K.nc
    XC = dram(K, "ssd_xc", [4, 128, S], F32)
    BCb = dram(K, "ssd_bc", [4, 128, S], BF16)
    DTs = dram(K, "ssd_dt", [8, S], F32)
    DAs = dram(K, "ssd_da", [8, S], F32)
    TS = 512
    with P.stage():
        cw = P.sbuf("cw", [128, 8, 4], F32)
        cbias = P.sbuf("cbs", [128, 8], F32)
        dtb = P.sbuf("dtb", [8, 1], F32)
        an = P.sbuf("an", [8, 1], F32)
        bp_ = P.buf()
        sp_ = P.dma_sem("sprm")
        P.dma("sp", sp_, cw[:], prm["conv_w"], writes=[bp_])
        P.dma("sp", sp_, cbias[:], prm["conv_b"], writes=[bp_])
        P.dma("sp", sp_, dtb[:], prm["dt_bias"], writes=[bp_])
        P.dma("sp", sp_, an[:], prm["a_log"], writes=[bp_])
        P.op("act", lambda e: e.activation(out=an[:], in_=an[:], func=AF.Exp), reads=[bp_], writes=[bp_])
        P.op("dve", lambda e: e.tensor_scalar(out=an[:], in0=an[:], scalar1=-1.0, scalar2=None, op0=ALU.mult), reads=[bp_], writes=[bp_])
        NB_ = 3
        raw = [P.sbuf("craw", [128, TS + 3], F32) for _ in range(NB_)]
        braw = P.bufs(NB_)
        sraw = [P.dma_sem(f"craw{i}") for i in range(NB_)]
        acc = [P.sbuf("cacc", [128, TS], F32) for _ in range(NB_)]
        bacc = P.bufs(NB_)
        of = [P.sbuf("cof", [128, TS], F32) for _ in range(NB_)]
        ob = [P.sbuf("cob", [128, TS], BF16) for _ in range(NB_)]
        bof, bob = P.bufs(NB_), P.bufs(NB_)
        sof = [P.dma_sem(f"cof{i}") for i in range(NB_)]
        sob = [P.dma_sem(f"cob{i}") for i in range(NB_)]
        dtt = [P.sbuf("cdt", [8, TS], F32) for _ in range(2)]
        dat = [P.sbuf("cda", [8, TS], F32) for _ in range(2)]
        bdt, bda = P.bufs(2), P.bufs(2)
        sdt = [P.dma_sem(f"cdt{i}") for i in range(2)]
        sda = [P.dma_sem(f"cda{i}") for i in range(2)]
        dd_ = P.buf()
        it = 0
        for ti, t0 in enumerate(range(0, S, TS)):
            for j in range(8):
                s = it % NB_
                it += 1
                if t0 == 0:
                    P.op("pool", lambda e, s=s: e.memset(raw[s][:, 0:3], 0.0), writes=[braw[s]])
                    P.dma("sp", sraw[s], raw[s][:, 3:3 + TS], SSDP[4 + j, :, 0:TS], writes=[braw[s]])
                else:
                    P.dma("sp", sraw[s], raw[s][:, :], SSDP[4 + j, :, t0 - 3:t0 + TS], writes=[braw[s]])
                P.op("act", lambda e, s=s, j=j: e.activation(out=acc[s][:], in_=raw[s][:, 3:3 + TS], func=AF.Identity,
                                                              scale=cw[:, j, 3:4], bias=cbias[:, j:j + 1]),
                     reads=[braw[s], bp_], writes=[bacc[s]])
                for k in range(3):
                    P.op("dve", lambda e, s=s, j=j, k=k: e.scalar_tensor_tensor(
                        out=acc[s][:], in0=raw[s][:, k:k + TS], scalar=cw[:, j, k:k + 1], in1=acc[s][:],
                        op0=ALU.mult, op1=ALU.add), reads=[braw[s], bacc[s], bp_], writes=[bacc[s]])
                if j < 4:
                    P.op("act", lambda e, s=s: e.activation(out=of[s][:], in_=acc[s][:], func=AF.Silu), reads=[bacc[s]], writes=[bof[s]])
                    P.dma("sp", sof[s], XC[j, :, t0:t0 + TS], of[s][:], reads=[bof[s]], writes=[dd_])
                else:
                    P.op("act", lambda e, s=s: e.activation(out=ob[s][:], in_=acc[s][:], func=AF.Silu), reads=[bacc[s]], writes=[bob[s]])
                    P.dma("sp", sob[s], BCb[j - 4, :, t0:t0 + TS], ob[s][:], reads=[bob[s]], writes=[dd_])
            d = ti % 2
            P.dma("sp", sdt[d], dtt[d][:], DTr[:, t0:t0 + TS], reads=[], writes=[bdt[d]])
            P.op("act", lambda e, d=d: e.activation(out=dtt[d][:], in_=dtt[d][:], func=AF.Exp, bias=dtb[:, 0:1]), reads=[bdt[d], bp_], writes=[bdt[d]])
            P.op("act", lambda e, d=d: e.activation(out=dtt[d][:], in_=dtt[d][:], func=AF.Ln, bias=1.0), reads=[bdt[d]], writes=[bdt[d]])
            P.op("dve", lambda e, d=d: e.tensor_scalar(out=dat[d][:], in0=dtt[d][:], scalar1=an[:, 0:1], scalar2=None, op0=ALU.mult),
                 reads=[bdt[d], bp_], writes=[bda[d]])
            P.dma("sp", sdt[d], DTs[:, t0:t0 + TS], dtt[d][:], reads=[bdt[d]], writes=[dd_])
            P.dma("sp", sda[d], DAs[:, t0:t0 + TS], dat[d][:], reads=[bda[d]], writes=[dd_])

    if getattr(K, 'ssd_lvl', 9) < 2:
        return
    with P.stage():
        dsk = P.sbuf("dsk", [128, 4], F32)
        nw = P.sbuf("nw", [128, 4], F32)
        bp_ = P.buf()
        sp_ = P.dma_sem("sprm2")
        P.dma("sp", sp_, dsk[:], prm["d_skip"], writes=[bp_])
        P.dma("sp", sp_, nw[:], prm["ssm_norm"], writes=[bp_])
        NL = 2
        xc = [P.sbuf("sxc", [128, 4, TS], F32) for _ in range(NL)]
        zt = [P.sbuf("sz", [128, 4, TS], F32) for _ in range(NL)]
        bcb = [P.sbuf("sbc", [128, 4, TS], BF16) for _ in range(NL)]
        dtda = [P.sbuf("sdtda", [16, TS], F32) for _ in range(NL)]
        bxc, bz, bbc, bdd = P.bufs(NL), P.bufs(NL), P.bufs(NL), P.bufs(NL)
        sxc = [P.dma_sem(f"sxc{i}") for i in range(NL)]
        sz = [P.dma_sem(f"sz{i}") for i in range(NL)]
        sbc = [P.dma_sem(f"sbc{i}") for i in range(NL)]
        sdd = [P.dma_sem(f"sdd{i}") for i in range(NL)]
        yt = [P.sbuf("syt", [128, 4, TS], BF16) for _ in range(NL)]
        byt = P.bufs(NL)
        syt = [P.dma_sem(f"syt{i}") for i in range(NL)]
        H = P.sbuf("sH", [128, 8, 64], F32)
        Hb = P.sbuf("sHb", [128, 8, 128], BF16)
        bH, bHb = P.buf(), P.buf()
        P.op("pool", lambda e: e.memset(H[:], 0.0), writes=[bH])
        P.op("pool", lambda e: e.memset(Hb[:], 0.0), writes=[bHb])
        Xp = [P.sbuf("sXp", [128, 8, 128], BF16) for _ in range(2)]
        Xpp = [P.sbuf("sXpp", [128, 8, 64], BF16) for _ in range(2)]
        bXp, bXpp = P.bufs(2), P.bufs(2)
        for i in range(2):
            P.op("pool", lambda e, i=i: e.memset(Xp[i][:], 0.0), writes=[bXp[i]])
        p_small = P.psum("ssm", [128, 512])
        b_small = P.buf()
        p_xT = P.psum("sxT", [128, 512])
        b_xT1 = P.buf()
        b_xT = [b_xT1] * 4
        p_bt = P.psum("sbt", [128, 512])
        b_bt1 = P.buf()
        b_bt = [b_bt1] * 2
        p_gs = P.psum("sgs", [128, 512])
        p_gt = p_gs[:, 0:256]
        p_ss = p_gs[:, 256:512]
        b_gs1 = P.buf()
        b_gt = [b_gs1] * 2
        p_ar = [P.psum("sar", [128, 512]) for _ in range(2)]
        b_ar2 = P.bufs(2)
        b_ar = [b_ar2[h // 4] for h in range(8)]
        p_y = P.psum("sy", [128, 512])
        b_y1 = P.buf()
        b_y = [b_y1] * 4
        p_S = P.psum("sS", [128, 512])
        b_S = P.buf()
        b_ss = [b_gs1] * 2
        dtT = P.sbuf("sdtT", [128, 16], F32)
        acT = P.sbuf("sacT", [128, 8], F32)
        nacT = P.sbuf("snacT", [128, 8], F32)
        dsT = P.sbuf("sdsT", [128, 8], F32)
        w2 = P.sbuf("sw2", [128, 8], F32)
        cdb = P.sbuf("scdb", [128, 8], F32)
        acF = P.sbuf("sacF", [8, 128], F32)
        alT = P.sbuf("salT", [128, 8], F32)
        b_alT = P.buf()
        arsb = [P.sbuf("sarsb", [128, 512], F32) for _ in range(2)]
        b_arsb = P.bufs(2)
        b_dtT, b_acT, b_nacT, b_dsT, b_w2, b_cdb, b_acF = P.bufs(7)
        Btok = P.sbuf("sBtok", [128, 2, 128], BF16)
        b_Btok = P.bufs(2)
        tmp = [P.sbuf("stmp", [128, 128], F32) for _ in range(2)]
        Ld = [P.sbuf("sLd", [128, 128], F32) for _ in range(2)]
        Er = [P.sbuf("sEr", [128, 128], F32) for _ in range(2)]
        b_tmp, b_Ld, b_Er = P.bufs(2), P.bufs(2), P.bufs(2)
        Mh = P.sbuf("sMh", [128, 8, 128], BF16)
        Cp = P.sbuf("sCp", [128, 8, 128], BF16)
        b_Mh, b_Cp = P.bufs(8), P.bufs(8)
        t1 = P.sbuf("st1", [128, 4, 128], F32)
        szs = P.sbuf("sszs", [128, 4, 128], F32)
        yg = P.sbuf("syg", [128, 4, 128], F32)
        sqy = P.sbuf("ssqy", [128, 4, 128], F32)
        b_t1, b_szs, b_yg, b_sqy = P.bufs(4), P.bufs(4), P.bufs(4), P.bufs(4)
        rstd = P.sbuf("srstd", [128, 2, 128], F32)
        b_rstd = P.bufs(2)
        dd_ = P.buf()
        hcnt = 0
        for ti, t0 in enumerate(range(0, S, TS)):
            L = ti % NL
            for c in range(4):
                P.dma("sp", sxc[L], xc[L][:, c, :], XC[c, :, t0:t0 + TS], writes=[bxc[L]])
                P.dma("sp", sz[L], zt[L][:, c, :], SSDP[c, :, t0:t0 + TS], writes=[bz[L]])
                P.dma("sp", sbc[L], bcb[L][:, c, :], BCb[c, :, t0:t0 + TS], writes=[bbc[L]])
            P.dma("sp", sdd[L], dtda[L][0:8, :], DTs[:, t0:t0 + TS], writes=[bdd[L]])
            P.dma("sp", sdd[L], dtda[L][8:16, :], DAs[:, t0:t0 + TS], writes=[bdd[L]])
            for zi in range(TS // 128):
                sl = slice(zi * 128, (zi + 1) * 128)
                par = (ti * 4 + zi) % 2
                P.op("pe", lambda e, L=L, sl=sl: e.matmul(p_small[:, 256:272], dtda[L][:, sl], K.ident32[0:16, 0:16], start=True, stop=True),
                     reads=[bdd[L], K.bc], writes=[b_small])
                P.op("dve", lambda e: e.tensor_copy(out=dtT[:], in_=p_small[:, 256:272]), reads=[b_small], writes=[b_dtT])

                def cum(e):
                    e.matmul(p_small[:, 0:8], K.tri, dtT[:, 8:16], start=True, stop=True)
                    e.matmul(p_small[:, 8:16], K.ones32, dtT[:, 8:16], start=True, stop=True)
                    return e.matmul(p_small[0:8, 128:256], dtT[:, 8:16], K.tri, start=True, stop=True)
                P.op("pe", cum, reads=[b_dtT, K.bc], writes=[b_small])
                P.op("dve", lambda e: e.tensor_copy(out=acT[:], in_=p_small[:, 0:8]), reads=[b_small], writes=[b_acT])
                P.op("dve", lambda e: e.tensor_scalar(out=nacT[:], in0=p_small[:, 0:8], scalar1=-1.0, scalar2=None, op0=ALU.mult),
                     reads=[b_small], writes=[b_nacT])
                P.op("dve", lambda e: e.tensor_copy(out=alT[:], in_=p_small[:, 8:16]), reads=[b_small], writes=[b_alT])
                P.op("dve", lambda e: e.tensor_copy(out=acF[:], in_=p_small[0:8, 128:256]), reads=[b_small], writes=[b_acF])
                P.op("dve", lambda e: e.tensor_tensor(out=dsT[:], in0=alT[:], in1=acT[:], op=ALU.subtract),
                     reads=[b_alT, b_acT], writes=[b_dsT])
                P.op("act", lambda e: e.activation(out=dsT[:], in_=dsT[:], func=AF.Exp), reads=[b_dsT], writes=[b_dsT])
                P.op("act", lambda e: e.activation(out=cdb[:], in_=alT[:], func=AF.Exp), reads=[b_alT], writes=[b_cdb])
                P.op("dve", lambda e: e.tensor_tensor(out=w2[:], in0=dtT[:, 0:8], in1=dsT[:], op=ALU.mult),
                     reads=[b_dtT, b_dsT], writes=[b_w2])
                if getattr(K, 'ssd_lvl', 9) < 3:
                    continue
                def xtr(e, L=L, sl=sl):
                    ins = None
                    for c in range(4):
                        ins = e.matmul(p_xT[:, c * 128:(c + 1) * 128], xc[L][:, c, sl], K.ident32, start=True, stop=True)
                    return ins
                P.op("pe", xtr, reads=[bxc[L], K.bc], writes=[b_xT1])
                for c in range(4):
                    for h2 in range(2):
                        h = 2 * c + h2
                        P.op("dve", lambda e, c=c, h=h, h2=h2, par=par: e.tensor_scalar(
                            out=Xp[par][:, h, h2 * 64:(h2 + 1) * 64], in0=p_xT[:, c * 128 + h2 * 64:c * 128 + (h2 + 1) * 64],
                            scalar1=dtT[:, h:h + 1], scalar2=None, op0=ALU.mult), reads=[b_xT[c], b_dtT], writes=[bXp[par]])
                        P.op("dve", lambda e, c=c, h=h, h2=h2, par=par: e.tensor_scalar(
                            out=Xpp[par][:, h, :], in0=p_xT[:, c * 128 + h2 * 64:c * 128 + (h2 + 1) * 64],
                            scalar1=w2[:, h:h + 1], scalar2=None, op0=ALU.mult), reads=[b_xT[c], b_w2], writes=[bXpp[par]])
                if getattr(K, 'ssd_lvl', 9) < 4:
                    continue
                def btr(e, L=L, sl=sl):
                    e.matmul(p_bt[:, 0:128], bcb[L][:, 0, sl], K.identb, start=True, stop=True)
                    return e.matmul(p_bt[:, 128:256], bcb[L][:, 1, sl], K.identb, start=True, stop=True)
                P.op("pe", btr, reads=[bbc[L], K.bc], writes=[b_bt1])
                P.op("act", lambda e: e.activation(out=Btok[:, :, :], in_=p_bt[:, 0:256].rearrange("p (g n) -> p g n", g=2), func=AF.Copy),
                     reads=[b_bt1], writes=[b_Btok[0], b_Btok[1]])

                def gtm(e, L=L, sl=sl):
                    e.matmul(p_gt[:, 0:128], bcb[L][:, 0, sl], bcb[L][:, 2, sl], start=True, stop=True)
                    return e.matmul(p_gt[:, 128:256], bcb[L][:, 1, sl], bcb[L][:, 3, sl], start=True, stop=True)
                P.op("pe", gtm, reads=[bbc[L]], writes=[b_gs1])
                if getattr(K, 'ssd_lvl', 9) < 5:
                    continue
                for h in range(8):
                    gl = h // 4
                    pa = arsb[h // 4][:, (h % 4) * 128:(h % 4 + 1) * 128]
                    k2 = hcnt % 2
                    hcnt += 1
                    if h % 4 == 0:
                        def arm(e, h0=h):
                            ins = None
                            for hh in range(h0, h0 + 4):
                                ins = e.matmul(p_ar[hh // 4][:, (hh % 4) * 128:(hh % 4 + 1) * 128], K.sel[0:8, hh, :], acF[:], start=True, stop=True)
                            return ins
                        P.op("pe", arm, reads=[b_acF, K.bc], writes=[b_ar2[h // 4]])
                        P.op("dve", lambda e, bk=h // 4: e.tensor_copy(out=arsb[bk][:], in_=p_ar[bk][:]), reads=[b_ar2[h // 4]], writes=[b_arsb[h // 4]])
                    P.op("pool", lambda e, pa=pa, k2=k2: e.tensor_tensor(out=tmp[k2][:], in0=pa, in1=K.maskneg, op=ALU.add),
                         reads=[b_arsb[h // 4], K.bc], writes=[b_tmp[k2]])
                    P.op("act", lambda e, h=h, k2=k2: e.activation(out=Ld[k2][:], in_=tmp[k2][:], func=AF.Exp, bias=nacT[:, h:h + 1]),
                         reads=[b_tmp[k2], b_nacT], writes=[b_Ld[k2]])
                    P.op("act", lambda e, pa=pa, k2=k2: e.activation(out=Er[k2][:], in_=pa, func=AF.Exp),
                         reads=[b_arsb[h // 4]], writes=[b_Er[k2]])
                    P.op("dve", lambda e, h=h, gl=gl, k2=k2: e.tensor_tensor(out=Mh[:, h, :], in0=p_gt[:, gl * 128:(gl + 1) * 128], in1=Ld[k2][:],
                                                                            op=ALU.mult), reads=[b_gt[gl], b_Ld[k2]], writes=[b_Mh[h]])
                    P.op("pool", lambda e, h=h, gl=gl, k2=k2, L=L, sl=sl: e.tensor_tensor(out=Cp[:, h, :], in0=bcb[L][:, 2 + gl, sl], in1=Er[k2][:],
                                                                                         op=ALU.mult), reads=[bbc[L], b_Er[k2]], writes=[b_Cp[h]])
                if getattr(K, 'ssd_lvl', 9) < 6:
                    continue
                def ymm(e, par=par):
                    ins = None
                    for c in range(4):
                        for h2 in range(2):
                            h = 2 * c + h2
                            e.matmul(p_y[:, c * 128:(c + 1) * 128], Xp[par][:, h, :], Mh[:, h, :], start=(h2 == 0), stop=False)
                            ins = e.matmul(p_y[:, c * 128:(c + 1) * 128], Hb[:, h, :], Cp[:, h, :], start=False, stop=(h2 == 1))
                    return ins
                P.op("pe", ymm, reads=[bXp[par], bHb] + b_Mh + b_Cp, writes=[b_y1])
                if getattr(K, 'ssd_lvl', 9) < 7:
                    continue
                def smm(e, par=par):
                    ins = None
                    for h in range(8):
                        ins = e.matmul(p_S[:, h * 64:(h + 1) * 64], Btok[:, h // 4, :], Xpp[par][:, h, :], start=True, stop=True)
                    return ins
                P.op("pe", smm, reads=[b_Btok[0], b_Btok[1], bXpp[par]], writes=[b_S])
                P.op("pool", lambda e: e.tensor_tensor(out=H[:], in0=H[:], in1=cdb[:].unsqueeze(2).broadcast_to([128, 8, 64]), op=ALU.mult),
                     reads=[bH, b_cdb], writes=[bH])
                P.op("dve", lambda e: e.tensor_tensor(out=H[:], in0=p_S[:].rearrange("p (h q) -> p h q", h=8), in1=H[:], op=ALU.add),
                     reads=[bH, b_S], writes=[bH])
                Hv = H[:].rearrange("p (c two) q -> p c two q", two=2)
                Hbv = Hb[:].rearrange("p (c two) q -> p c two q", two=2)
                P.op("act", lambda e, Hv=Hv, Hbv=Hbv: e.activation(func=AF.Copy, out=Hbv[:, :, 0, 0:64], in_=Hv[:, :, 0, :]), reads=[bH], writes=[bHb])
                P.op("act", lambda e, Hv=Hv, Hbv=Hbv: e.activation(func=AF.Copy, out=Hbv[:, :, 1, 64:128], in_=Hv[:, :, 1, :]), reads=[bH], writes=[bHb])
                if getattr(K, 'ssd_lvl', 9) < 8:
                    continue
                for c in range(4):
                    P.op("dve", lambda e, c=c, L=L, sl=sl: e.scalar_tensor_tensor(
                        out=t1[:, c, :], in0=xc[L][:, c, sl], scalar=dsk[:, c:c + 1], in1=p_y[:, c * 128:(c + 1) * 128],
                        op0=ALU.mult, op1=ALU.add), reads=[bxc[L], b_y[c], bp_], writes=[b_t1[c]])
                    P.op("act", lambda e, c=c, L=L, sl=sl: e.activation(out=szs[:, c, :], in_=zt[L][:, c, sl], func=AF.Silu),
                         reads=[bz[L]], writes=[b_szs[c]])
                    P.op("pool", lambda e, c=c: e.tensor_tensor(out=yg[:, c, :], in0=t1[:, c, :], in1=szs[:, c, :], op=ALU.mult),
                         reads=[b_t1[c], b_szs[c]], writes=[b_yg[c]])
                    P.op("act", lambda e, c=c: e.activation(out=sqy[:, c, :], in_=yg[:, c, :], func=AF.Square),
                         reads=[b_yg[c]], writes=[b_sqy[c]])
                def ssm(e):
                    ins = None
                    for gl in range(2):
                        e.matmul(p_ss[:, gl * 128:(gl + 1) * 128], K.ones32, sqy[:, 2 * gl, :], start=True, stop=False)
                        ins = e.matmul(p_ss[:, gl * 128:(gl + 1) * 128], K.ones32, sqy[:, 2 * gl + 1, :], start=False, stop=True)
                    return ins
                P.op("pe", ssm, reads=b_sqy + [K.bc], writes=[b_gs1])
                for gl in range(2):
                    P.op("dve", lambda e, gl=gl: e.tensor_scalar(out=rstd[:, gl, :], in0=p_ss[:, gl * 128:(gl + 1) * 128], scalar1=1.0 / 256,
                                                                  scalar2=EPS, op0=ALU.mult, op1=ALU.add), reads=[b_ss[gl]], writes=[b_rstd[gl]])
                    P.op("act", lambda e, gl=gl: e.activation(out=rstd[:, gl, :], in_=rstd[:, gl, :], func=AF.Sqrt), reads=[b_rstd[gl]], writes=[b_rstd[gl]])
                    P.op("dve", lambda e, gl=gl: e.reciprocal(out=rstd[:, gl, :], in_=rstd[:, gl, :]), reads=[b_rstd[gl]], writes=[b_rstd[gl]])
                    for c in (2 * gl, 2 * gl + 1):
                        P.op("dve", lambda e, c=c, gl=gl, L=L, sl=sl: e.scalar_tensor_tensor(
                            out=yt[L][:, c, sl], in0=yg[:, c, :], scalar=nw[:, c:c + 1], in1=rstd[:, gl, :],
                            op0=ALU.mult, op1=ALU.mult), reads=[b_yg[c], b_rstd[gl], bp_], writes=[byt[L]])
            for c in range(4):
                if getattr(K, 'ssd_lvl', 9) < 8:
                    break
                P.dma("sp", syt[L], Y[7 + c, :, t0:t0 + TS], yt[L][:, c, :], reads=[byt[L]], writes=[dd_])


def build_A(cfg, upto=4):
    nc = bass.Bass("TRN2", target_bir_lowering=False)
    K = Kx()
    K.nc, K.P = nc, Prog(nc)
    D, S, KC = cfg.D, cfg.S, cfg.KC
    X = nc.dram_tensor("x", [KC, 128, S], F32, kind="ExternalInput").ap()
    WA = nc.dram_tensor("wa", [D, cfg.NA], F32, kind="ExternalInput").ap()
    gmix = nc.dram_tensor("gmix", [128, KC], F32, kind="ExternalInput").ap()
    sink = nc.dram_tensor("sink", [128, 4], F32, kind="ExternalInput").ap()
    prm = dict(
        conv_w=nc.dram_tensor("conv_w", [128, 8, 4], F32, kind="ExternalInput").ap(),
        conv_b=nc.dram_tensor("conv_b", [128, 8], F32, kind="ExternalInput").ap(),
        dt_bias=nc.dram_tensor("dt_bias", [8, 1], F32, kind="ExternalInput").ap(),
        a_log=nc.dram_tensor("a_log", [8, 1], F32, kind="ExternalInput").ap(),
        d_skip=nc.dram_tensor("d_skip", [128, 4], F32, kind="ExternalInput").ap(),
        ssm_norm=nc.dram_tensor("ssm_norm", [128, 4], F32, kind="ExternalInput").ap(),
    )
    Y = nc.dram_tensor("y", [11, 128, S], BF16, kind="ExternalOutput").ap()
    load_consts(K)
    Hd = dram(K, "a_h", [KC, 128, S], BF16)
    QKV = dram(K, "a_qkv", [33, 128, S], BF16)
    SSDP = dram(K, "a_ssdp", [12, 128, S], F32)
    DTr = dram(K, "a_dt", [1, 8, S], F32)
    st_rmsnorm(K, X, Hd, gmix, KC, S, D)

    def route(un, nb):
        ch = (un["c0"] + nb * 128) // 128
        if ch < 33:
            return QKV, ch, 128, "b", None
        if ch < 45:
            return SSDP, ch - 33, 128, "f", None
        return DTr, 0, 8, "f", None
    if upto >= 2:
        st_linear(K, Hd, KC, WA, simple_units(cfg.NA, KC), S, 1024, epi_store_factory(K, route))
    if upto >= 3:
        st_attention(K, QKV, Y, S, sink)
    if upto >= 4:
        st_ssd(K, SSDP, DTr[0], Y, S, prm)
    K.P.emit()
    return nc


def pack_A_inputs(cfg, l, g, xT_b, inp, cf, cb):
    D = cfg.D
    w_in = inp["w_in"][l]
    cols = []
    A_Q, A_KV = 2048, 512
    cols += list(range(g * 512, g * 512 + 512))
    cols += list(range(A_Q + g * 128, A_Q + g * 128 + 128))
    cols += list(range(A_Q + A_KV + g * 128, A_Q + A_KV + g * 128 + 128))
    B0 = 3072
    for gi in range(3):
        for part in range(3):
            o = B0 + gi * 3 * 1536 + part * 1536 + g * 384
            cols += list(range(o, o + 384))
    C0 = 3072 + 13824
    cols += list(range(C0 + g * 512, C0 + g * 512 + 512))
    XB = C0 + 2048
    cols += list(range(XB + g * 512, XB + g * 512 + 512))
    cols += list(range(XB + 2048 + g * 256, XB + 2048 + g * 256 + 256))
    cols += list(range(XB + 2048 + 1024 + g * 256, XB + 2048 + 1024 + g * 256 + 256))
    DT0 = C0 + 2048 + 4096
    cols += list(range(DT0 + g * 8, DT0 + g * 8 + 8))
    cols = np.asarray(cols)
    assert cols.size == cfg.NA
    wa = np.ascontiguousarray(w_in[:, cols])
    conv_ch = np.concatenate([np.arange(g * 512, g * 512 + 512), 2048 + np.arange(g * 256, g * 256 + 256),
                              2048 + 1024 + np.arange(g * 256, g * 256 + 256)])
    cw = inp["conv_w"][l][:, conv_ch]
    cw = np.ascontiguousarray(cw.T.reshape(8, 128, 4).transpose(1, 0, 2))
    cbias = np.ascontiguousarray(inp["conv_b"][l][conv_ch].reshape(8, 128).T)
    heads = np.arange(g * 8, g * 8 + 8)
    dsk = np.repeat(inp["d_skip"][l][heads], 64).reshape(4, 128).T
    return {
        "x": xT_b, "wa": wa,
        "gmix": np.ascontiguousarray(inp["norm_mix"][l].reshape(cfg.KC, 128).T),
        "sink": np.ascontiguousarray(np.broadcast_to(inp["attn_sink"][l][g * 4:g * 4 + 4][None, :], (128, 4))),
        "conv_w": cw, "conv_b": cbias,
        "dt_bias": np.ascontiguousarray(inp["dt_bias"][l][heads].reshape(8, 1)),
        "a_log": np.ascontiguousarray(inp["a_log"][l][heads].reshape(8, 1)),
        "d_skip": np.ascontiguousarray(dsk),
        "ssm_norm": np.ascontiguousarray(inp["ssm_norm"][l][g * 512:g * 512 + 512].reshape(4, 128).T),
        "cst_f": cf, "cst_b": cb,
    }


def st_xattn(K, Qd, KTd, VFd, Od, NT):
    P = K.P
    scale = 128 ** -0.5
    with P.stage():
        kt = P.sbuf("xk", [128, 4, 256], BF16)
        vf = P.sbuf("xv", [128, 4, 256], BF16)
        bkv = P.buf()
        s_ = P.dma_sem("xkv")
        for h in range(4):
            P.dma("sp", s_, kt[:, h, :], KTd[h, :, :], writes=[bkv])
            P.dma("sp", s_, vf[:, h, :], VFd[h, :, :], writes=[bkv])
        vtok = P.sbuf("xvt", [128, 2, 4, 128], BF16)
        bvt = P.buf()
        pt = [P.psum("xpt", [128, 512]) for _ in range(2)]
        bpt = P.bufs(2)
        for h in range(4):
            tb = h % 2

            def xtr(e, h=h, tb=tb):
                e.matmul(pt[tb][:, 0:128], vf[:, h, 0:128], K.identb, start=True, stop=True)
                return e.matmul(pt[tb][:, 128:256], vf[:, h, 128:256], K.identb, start=True, stop=True)
            P.op("pe", xtr, reads=[bkv, K.bc], writes=[bpt[tb]])
            P.op("act", lambda e, h=h, tb=tb: e.activation(out=vtok[:, :, h, :], in_=pt[tb][:, 0:256].rearrange("p (m d) -> p m d", m=2), func=AF.Copy),
                 reads=[bpt[tb]], writes=[bvt])
        TQ = 512
        q = [P.sbuf("xq", [128, 4, TQ], BF16) for _ in range(2)]
        bq = P.bufs(2)
        sq = [P.dma_sem(f"xq{i}") for i in range(2)]
        sc = [[P.psum("xsc", [128, 512]) for _ in range(2)] for _ in range(2)]
        bsc = [P.bufs(2) for _ in range(2)]
        pn, pd = P.psum("xpn", [128, 512]), P.psum("xpd", [128, 512])
        bpn, bpd = P.buf(), P.buf()
        et = [P.sbuf("xet", [128, 2, TQ], BF16) for _ in range(2)]
        bet = P.bufs(2)
        rec = [P.sbuf("xrec", [128, TQ], F32) for _ in range(2)]
        brec = P.bufs(2)
        ot = [P.sbuf("xo", [128, TQ], BF16) for _ in range(2)]
        bot = P.bufs(2)
        so = [P.dma_sem(f"xo{i}") for i in range(2)]
        dd_ = P.buf()
        it = 0
        for ti, t0 in enumerate(range(0, NT, TQ)):
            s = ti % 2
            for h in range(4):
                P.dma("sp", sq[s], q[s][:, h, :], Qd[h, :, t0:t0 + TQ], writes=[bq[s]])
            for h in range(4):
                b = it % 2
                it += 1
                for mc in range(2):
                    P.op("pe", lambda e, b=b, mc=mc, h=h, s=s: e.matmul(sc[b][mc][:], kt[:, h, mc * 128:(mc + 1) * 128], q[s][:, h, :], start=True, stop=True),
                         reads=[bkv, bq[s]], writes=[bsc[b][mc]])
                    P.op("act", lambda e, b=b, mc=mc: e.activation(out=et[b][:, mc, :], in_=sc[b][mc][:], func=AF.Exp, scale=scale),
                         reads=[bsc[b][mc]], writes=[bet[b]])

                def pv(e, b=b, h=h):
                    e.matmul(pn[:], vtok[:, 0, h, :], et[b][:, 0, :], start=True, stop=False)
                    e.matmul(pn[:], vtok[:, 1, h, :], et[b][:, 1, :], start=False, stop=True)
                    e.matmul(pd[:], K.onesb, et[b][:, 0, :], start=True, stop=False)
                    return e.matmul(pd[:], K.onesb, et[b][:, 1, :], start=False, stop=True)
                P.op("pe", pv, reads=[bet[b], bvt, K.bc], writes=[bpn, bpd])
                P.op("dve", lambda e, b=b: e.reciprocal(out=rec[b][:], in_=pd[:]), reads=[bpd], writes=[brec[b]])
                P.op("dve", lambda e, b=b: e.tensor_tensor(out=ot[b][:], in0=pn[:], in1=rec[b][:], op=ALU.mult),
                     reads=[bpn, brec[b]], writes=[bot[b]])
                P.dma("sp", so[b], Od[h, :, t0:t0 + TQ], ot[b][:], reads=[bot[b]], writes=[dd_])


def epi_residual_factory(K, Xin, Xout):
    def factory(P):
        xi = [P.sbuf("eri", [128, 512], F32) for _ in range(4)]
        bxi = P.bufs(4)
        sxi = [P.dma_sem(f"eri{i}") for i in range(4)]
        so = Stager(P, "ero", F32)
        c = {"i": 0}

        def epi(u, un, t0, ps, bps):
            nbk = (un["nc"] + 127) // 128
            for nb in range(nbk):
                ch = (un["c0"] + nb * 128) // 128
                for th in range(len(ps[nb])):
                    j = c["i"] % 4
                    c["i"] += 1
                    tw = min(512, Xin.shape[2] - t0)
                    P.dma("sp", sxi[j], xi[j][:, 0:tw], Xin[ch, :, t0 + th * tw:t0 + (th + 1) * tw], writes=[bxi[j]])
                    k = so.next()
                    P.op("dve", lambda e, j=j, k=k, nb=nb, th=th, tw=tw: e.tensor_tensor(out=so.t[k][:, 0:tw], in0=ps[nb][th][:, 0:tw], in1=xi[j][:, 0:tw], op=ALU.add),
                         reads=[bps[nb][th], bxi[j]], writes=[so.b[k]])
                    P.dma("sp", so.s[k], Xout[ch, :, t0 + th * tw:t0 + (th + 1) * tw], so.t[k][:, 0:tw], reads=[so.b[k]], writes=[so.dd])
        return epi
    return factory


def epi_swiglu_factory(K, ACTd):
    def factory(P):
        sg = [P.sbuf("esg", [128, 512], F32) for _ in range(4)]
        bsg = P.bufs(4)
        so = Stager(P, "eso", BF16)
        c = {"i": 0}

        def epi(u, un, t0, ps, bps):
            jb = un["c0"] // 256
            for th in range(len(ps[0])):
                j = c["i"] % 4
                c["i"] += 1
                tw = min(512, ACTd.shape[2] - t0)
                P.op("act", lambda e, j=j, th=th, tw=tw: e.activation(out=sg[j][:, 0:tw], in_=ps[0][th][:, 0:tw], func=AF.Silu),
                     reads=[bps[0][th]], writes=[bsg[j]])
                k = so.next()
                P.op("dve", lambda e, j=j, k=k, th=th, tw=tw: e.tensor_tensor(out=so.t[k][:, 0:tw], in0=ps[1][th][:, 0:tw], in1=sg[j][:, 0:tw], op=ALU.mult),
                     reads=[bps[1][th], bsg[j]], writes=[so.b[k]])
                P.dma("sp", so.s[k], ACTd[jb, :, t0 + th * tw:t0 + (th + 1) * tw], so.t[k][:, 0:tw], reads=[so.b[k]], writes=[so.dd])
        return epi
    return factory


def epi_gated_factory(K, Gd, Md, KC):
    def factory(P):
        gt = [P.sbuf("egg", [128, 512], F32) for _ in range(4)]
        bgt = P.bufs(4)
        sgt = [P.dma_sem(f"egg{i}") for i in range(4)]
        acc = [P.sbuf("ega", [128, 2, 512], F32) for _ in range(2)]
        bacc = [P.bufs(2) for _ in range(2)]
        so = Stager(P, "ego", BF16)
        c = {"i": 0, "u": 0}

        def epi(u, un, t0, ps, bps):
            br = un["br"]
            if br == 0:
                c["u"] += 1
            a = c["u"] % 2
            nbk = (un["nc"] + 127) // 128
            for nb in range(nbk):
                ch = (un["c0"] + nb * 128) // 128
                j = c["i"] % 4
                c["i"] += 1
                tw = min(512, Gd.shape[2] - t0)
                P.dma("sp", sgt[j], gt[j][:, 0:tw], Gd[br * KC + ch, :, t0:t0 + tw], writes=[bgt[j]])
                if br == 0:
                    P.op("dve", lambda e, a=a, nb=nb, j=j, tw=tw: e.tensor_tensor(out=acc[a][:, nb, 0:tw], in0=ps[nb][0][:, 0:tw], in1=gt[j][:, 0:tw], op=ALU.mult),
                         reads=[bps[nb][0], bgt[j]], writes=[bacc[a][nb]])
                else:
                    P.op("dve", lambda e, nb=nb, j=j, tw=tw: e.tensor_tensor(out=gt[j][:, 0:tw], in0=ps[nb][0][:, 0:tw], in1=gt[j][:, 0:tw], op=ALU.mult),
                         reads=[bps[nb][0], bgt[j]], writes=[bgt[j]])
                    if br == 1:
                        P.op("pool", lambda e, a=a, nb=nb, j=j, tw=tw: e.tensor_tensor(out=acc[a][:, nb, 0:tw], in0=acc[a][:, nb, 0:tw], in1=gt[j][:, 0:tw], op=ALU.add),
                             reads=[bacc[a][nb], bgt[j]], writes=[bacc[a][nb]])
                    else:
                        k = so.next()
                        P.op("pool", lambda e, a=a, nb=nb, j=j, k=k, tw=tw: e.tensor_tensor(out=so.t[k][:, 0:tw], in0=acc[a][:, nb, 0:tw], in1=gt[j][:, 0:tw], op=ALU.add),
                             reads=[bacc[a][nb], bgt[j]], writes=[so.b[k]])
                        P.dma("sp", so.s[k], Md[ch, :, t0:t0 + tw], so.t[k][:, 0:tw], reads=[so.b[k]], writes=[so.dd])
        return epi
    return factory


def build_B(cfg, last):
    nc = bass.Bass("TRN2", target_bir_lowering=False)
    K = Kx()
    K.nc, K.P = nc, Prog(nc)
    D, KC, NT, F, FC = cfg.D, cfg.KC, cfg.NT, cfg.F, cfg.FC

    def inp(name, shape, dt=F32):
        return nc.dram_tensor(name, list(shape), dt, kind="ExternalInput").ap()
    X = inp("x", [KC, 128, NT])
    Yd = inp("y", [44, 128, NT], BF16)
    MEMd = inp("mem", [KC, 128, 256])
    WG = inp("wg", [D, 3 * D])
    WBR = inp("wbr", [5632, D])
    WO = inp("wo", [D, D])
    WXQ, WXK, WXV = inp("wxq", [D, 512]), inp("wxk", [D, 512]), inp("wxv", [D, 512])
    WXO = inp("wxo", [512, D])
    WGU = inp("wgu", [D, 2 * F])
    WD = inp("wd", [F, D])
    gmix, gx, gmem, gffn = inp("gmix", [128, KC]), inp("gx", [128, KC]), inp("gmem", [128, KC]), inp("gffn", [128, KC])
    gfin = inp("gfin", [128, KC])
    OUT = nc.dram_tensor("out", [KC, 128, NT], F32, kind="ExternalOutput").ap()
    load_consts(K)
    Hd = dram(K, "b_h", [KC, 128, NT], BF16)
    Gd = dram(K, "b_g", [3 * KC, 128, NT], F32)
    Md = dram(K, "b_m", [KC, 128, NT], BF16)
    X1 = dram(K, "b_x1", [KC, 128, NT], F32)
    HX = dram(K, "b_hx", [KC, 128, NT], BF16)
    Qd = dram(K, "b_q", [4, 128, NT], BF16)
    HM = dram(K, "b_hm", [KC, 128, 256], BF16)
    KTd = dram(K, "b_kt", [4, 128, 256], BF16)
    VFd = dram(K, "b_vf", [4, 128, 256], BF16)
    Od = dram(K, "b_o", [4, 128, NT], BF16)
    X2 = dram(K, "b_x2", [KC, 128, NT], F32)
    HF = dram(K, "b_hf", [KC, 128, NT], BF16)
    ACTd = dram(K, "b_act", [FC, 128, NT], BF16)
    X3 = OUT if not last else dram(K, "b_x3", [KC, 128, NT], F32)

    def store_to(dst, dk, func=None):
        def route(un, nb):
            return dst, (un["c0"] + nb * 128) // 128, 128, dk, func
        return epi_store_factory(K, route)
    T1 = 1024 if KC * 1024 * 2 <= 65536 else 512
    st_rmsnorm(K, X, Hd, gmix, KC, NT, D)
    st_linear(K, Hd, KC, WG, simple_units(3 * D, KC), NT, T1, store_to(Gd, "f", AF.Sigmoid))
    units = []
    kofs = [0, 16, 28]
    knum = [16, 12, 16]
    for c0 in range(0, D, 256):
        for br in range(3):
            units.append(dict(c0=c0, nc=min(256, D - c0), k0=kofs[br], nk=knum[br], br=br))
    st_linear(K, Yd, 44, WBR, units, NT, 512, epi_gated_factory(K, Gd, Md, KC))
    st_linear(K, Md, KC, WO, simple_units(D, KC), NT, T1, epi_residual_factory(K, X, X1))
    st_rmsnorm(K, X1, HX, gx, KC, NT, D)
    st_linear(K, HX, KC, WXQ, simple_units(512, KC), NT, T1, store_to(Qd, "b"))
    st_rmsnorm(K, MEMd, HM, gmem, KC, 256, D)
    st_linear(K, HM, KC, WXK, simple_units(512, KC), 256, 256, store_to(KTd, "b"))
    st_linear(K, HM, KC, WXV, simple_units(512, KC), 256, 256, store_to(VFd, "b"))
    st_xattn(K, Qd, KTd, VFd, Od, NT)
    st_linear(K, Od, 4, WXO, simple_units(D, 4), NT, 1024, epi_residual_factory(K, X1, X2))
    st_rmsnorm(K, X2, HF, gffn, KC, NT, D)
    st_linear(K, HF, KC, WGU, simple_units(2 * F, KC), NT, T1, epi_swiglu_factory(K, ACTd))
    st_linear(K, ACTd, FC, WD, simple_units(D, FC), NT, 512, epi_residual_factory(K, X2, X3))
    if last:
        st_rmsnorm(K, X3, OUT, gfin, KC, NT, D, out_dt=F32)
    K.P.emit()
    return nc


def pack_B_weights(cfg, l, inp, cf, cb):
    D, F = cfg.D, cfg.F
    G0 = 3072 + 13824 + 6176
    wg = np.ascontiguousarray(inp["w_in"][l][:, G0:G0 + 3 * D])
    fg = inp["ffn_gate"][l].reshape(D, F // 128, 128)
    fu = inp["ffn_up"][l].reshape(D, F // 128, 128)
    wgu = np.ascontiguousarray(np.stack([fg, fu], axis=2).reshape(D, 2 * F))

    def gl(v):
        return np.ascontiguousarray(v.reshape(cfg.KC, 128).T)
    return {
        "wg": wg, "wbr": inp["w_branch"][l], "wo": inp["w_out"][l],
        "wxq": inp["xattn_q"][l], "wxk": inp["xattn_k"][l], "wxv": inp["xattn_v"][l], "wxo": inp["xattn_o"][l],
        "wgu": wgu, "wd": inp["ffn_down"][l],
        "gmix": gl(inp["norm_mix"][l]), "gx": gl(inp["norm_x"][l]), "gmem": gl(inp["norm_mem"][l]),
        "gffn": gl(inp["norm_ffn"][l]), "gfin": gl(inp["norm_final"]),
        "cst_f": cf, "cst_b": cb,
    }


_PROGS = {}
DEBUG_STORE = {}


def get_prog(key, builder):
    if key not in _PROGS:
        _PROGS[key] = builder()
    return _PROGS[key]


def run_model(cfg, inp):
    cf, cb = host_consts()
    B, S, D, KC, NT = cfg.B, cfg.S, cfg.D, cfg.KC, cfg.NT
    assert B * 4 == NCORES
    xT = [np.ascontiguousarray(np.asarray(inp["x"][b], np.float32).T).reshape(KC, 128, S) for b in range(B)]
    memT = [np.ascontiguousarray(np.asarray(inp["mem"][b], np.float32).T).reshape(KC, 128, 256) for b in range(B)]
    inp = {k: np.asarray(v, np.float32) for k, v in inp.items()}
    for l in range(2):
        ncA = get_prog(("A", cfg.D, cfg.S), lambda: build_A(cfg))
        in_maps = [pack_A_inputs(cfg, l, c % 4, xT[c // 4], inp, cf, cb) for c in range(NCORES)]
        resA = run_bass_kernel_spmd(ncA, in_maps, core_ids=list(range(NCORES))).results
        del in_maps
        Yb = []
        for b in range(B):
            ya = np.concatenate([resA[b * 4 + g]["y"][0:4] for g in range(4)], axis=0)
            yb = np.concatenate([resA[b * 4 + g]["y"][4:7] for g in range(4)], axis=0)
            yc = np.concatenate([resA[b * 4 + g]["y"][7:11] for g in range(4)], axis=0)
            Yb.append(np.concatenate([ya, yb, yc], axis=0))
        DEBUG_STORE[f'Y{l}'] = Yb
        last = (l == 1)
        ncB = get_prog(("B", cfg.D, cfg.S, last), lambda: build_B(cfg, last))
        wts = pack_B_weights(cfg, l, inp, cf, cb)
        in_maps = []
        for c in range(NCORES):
            b, qd = c // 4, c % 4
            m = dict(wts)
            m["x"] = np.ascontiguousarray(xT[b][:, :, qd * NT:(qd + 1) * NT])
            m["y"] = np.ascontiguousarray(Yb[b][:, :, qd * NT:(qd + 1) * NT])
            m["mem"] = memT[b]
            in_maps.append(m)
        resB = run_bass_kernel_spmd(ncB, in_maps, core_ids=list(range(NCORES))).results
        del in_maps, wts
        xT = [np.concatenate([resB[b * 4 + qd]["out"] for qd in range(4)], axis=2) for b in range(B)]
        DEBUG_STORE[f'X{l}'] = xT
    out = np.stack([xT[b].reshape(D, S).T for b in range(B)], axis=0)
    return np.ascontiguousarray(out.astype(np.float32))


def kernel(**inputs):
    cfg = Cfg()
    return run_model(cfg, inputs)
```

```python
import contextlib
import numpy as np
import ml_dtypes
import concourse.bass as bass
import concourse.mybir as mybir
from concourse.bass_utils import run_bass_kernel_spmd

F32 = mybir.dt.float32
BF16 = mybir.dt.bfloat16
AF = mybir.ActivationFunctionType
ALU = mybir.AluOpType
NPBF = ml_dtypes.bfloat16

EPS = 1e-5
NCORES = 8
SAME_ENG_SYNC = True
WAIT_SORT = True
DEBUG_OUT = False


class Cfg:
    def __init__(self, D=4096, S=8192, B=2, F=None):
        self.D, self.S, self.B = D, S, B
        self.F = F if F is not None else -(-(8 * D) // (3 * 256)) * 256
        self.KC = D // 128
        self.FC = self.F // 128
        self.NT = S // 4
        self.MEM = 256
        self.NA = 33 * 128 + 12 * 128 + 8


class Buf:
    __slots__ = ("name", "w", "r")

    def __init__(self, name=""):
        self.name, self.w, self.r = name, None, []


class Prog:
    ENGS = ("pe", "dve", "act", "pool", "sp")

    def __init__(self, nc):
        self.nc = nc
        self.streams = {e: [] for e in self.ENGS}
        self.cnt = {e: 0 for e in self.ENGS}
        self.waited = {e: {} for e in self.ENGS}
        self.dma_cnt = {}
        self.semkeys = list(self.ENGS)
        self.base = contextlib.ExitStack()
        self.stack = self.base
        self.nb = 0
        self.uid = 0

    def sbuf(self, name, shape, dtype):
        self.uid += 1
        return self.stack.enter_context(self.nc.sbuf_tensor(f"{name}_{self.uid}", list(shape), dtype))

    def psum(self, name, shape, dtype=F32):
        self.uid += 1
        return self.stack.enter_context(self.nc.psum_tensor(f"{name}_{self.uid}", list(shape), dtype))

    def buf(self, name=""):
        self.nb += 1
        return Buf(name or f"b{self.nb}")

    def bufs(self, n):
        return [self.buf() for _ in range(n)]

    def dma_sem(self, name):
        key = "dma_" + name
        if key not in self.dma_cnt:
            self.dma_cnt[key] = 0
            self.semkeys.append(key)
        return key

    def _deps(self, eng, reads, writes):
        need = {}

        def add(tok):
            if tok is None:
                return
            k, v = tok
            if k == eng and (eng == "pe" or not SAME_ENG_SYNC):
                return
            if need.get(k, 0) < v:
                need[k] = v
        for b in reads:
            add(b.w)
        for b in writes:
            add(b.w)
            for t in b.r:
                add(t)
        out = []
        wd = self.waited[eng]
        order = {"pe": 3, "dve": 1, "act": 1, "pool": 1}
        for k, v in sorted(need.items(), key=lambda kv: order.get(kv[0], 5) if WAIT_SORT else 0):
            if wd.get(k, 0) < v:
                wd[k] = v
                out.append((k, v))
        return out

    def _commit(self, tok, reads, writes):
        for b in reads:
            b.r.append(tok)
        for b in writes:
            b.w = tok
            b.r = []

    def op(self, eng, fn, reads=(), writes=()):
        waits = self._deps(eng, reads, writes)
        self.cnt[eng] += 1
        tok = (eng, self.cnt[eng])
        self.streams[eng].append((waits, fn, (eng, 1)))
        self._commit(tok, reads, writes)
        return tok

    def dma(self, queue, semkey, out_ap, in_ap, reads=(), writes=()):
        waits = self._deps(queue, reads, writes)
        self.dma_cnt[semkey] += 16
        tok = (semkey, self.dma_cnt[semkey])

        def fn(e, out_ap=out_ap, in_ap=in_ap):
            return e.dma_start(out=out_ap, in_=in_ap)
        self.streams[queue].append((waits, fn, (semkey, 16)))
        self._commit(tok, reads, writes)
        return tok

    def barrier(self):
        allt = [(k, v) for k, v in self.dma_cnt.items() if v > 0]
        allt += [(e, self.cnt[e]) for e in self.ENGS if self.cnt[e] > 0]
        for eng in self.ENGS:
            wd = self.waited[eng]
            waits = []
            for k, v in allt:
                if k == eng:
                    continue
                if wd.get(k, 0) < v:
                    wd[k] = v
                    waits.append((k, v))
            if waits:
                self.streams[eng].append((waits, None, None))

    @contextlib.contextmanager
    def stage(self):
        saved = self.stack
        self.stack = contextlib.ExitStack()
        try:
            yield
        finally:
            self.stack.close()
            self.stack = saved
            self.barrier()

    def emit(self):
        nc = self.nc
        self.barrier()
        sems = {}
        for k in self.semkeys:
            sems[k] = self.base.enter_context(nc.semaphore("s_" + k))

        def replay(name, e):
            for waits, fn, inc in self.streams[name]:
                for k, v in waits:
                    e.wait_ge(sems[k], v)
                if fn is not None:
                    ins = fn(e)
                    ins.then_inc(sems[inc[0]], inc[1])

        with nc.Block() as block:
            @block.tensor
            def _(e):
                replay("pe", e)

            @block.vector
            def _(e):
                replay("dve", e)

            @block.scalar
            def _(e):
                replay("act", e)

            @block.gpsimd
            def _(e):
                replay("pool", e)

            @block.sync
            def _(e):
                replay("sp", e)
        self.base.close()


class Kx:
    pass


def host_consts():
    s = np.arange(128)[:, None]
    l = np.arange(128)[None, :]
    ones = np.ones((128, 128), np.float32)
    ident = np.eye(128, dtype=np.float32)
    tri = (s <= l).astype(np.float32)
    maskneg = np.where(s <= l, 0.0, -1e30).astype(np.float32)
    sel = np.zeros((128, 8, 128), np.float32)
    for h in range(8):
        sel[h, h, :] = 1.0
    cf = np.concatenate([ones, ident, tri, maskneg, sel.reshape(128, 1024)], axis=1)
    own = (s <= l)
    prev_swa = (s > l)
    prev_dil = (s >= l)
    m_swa = np.stack([prev_swa, own], axis=1).astype(np.float32).reshape(128, 256)
    m_dil = np.stack([prev_dil, own], axis=1).astype(np.float32).reshape(128, 256)
    cb = np.concatenate([ones, ident, m_swa, m_dil], axis=1).astype(NPBF)
    return cf, cb


def load_consts(K):
    P, nc = K.P, K.nc
    cfd = nc.dram_tensor("cst_f", [128, 1536], F32, kind="ExternalInput").ap()
    cbd = nc.dram_tensor("cst_b", [128, 768], BF16, kind="ExternalInput").ap()
    cf = P.sbuf("cf", [128, 1536], F32)
    cb = P.sbuf("cb", [128, 768], BF16)
    K.bc = P.buf("consts")
    s = P.dma_sem("cst")
    P.dma("sp", s, cf[:], cfd, writes=[K.bc])
    t = P.dma("sp", s, cb[:], cbd, writes=[K.bc])
    K.ones32 = cf[:, 0:128]
    K.ident32 = cf[:, 128:256]
    K.tri = cf[:, 256:384]
    K.maskneg = cf[:, 384:512]
    K.sel = cf[:, 512:1536].rearrange("p (h m) -> p h m", h=8)
    K.onesb = cb[:, 0:128]
    K.identb = cb[:, 128:256]
    K.m_swa = cb[:, 256:512].rearrange("p (s q) -> p s q", s=2)
    K.m_dil = cb[:, 512:768].rearrange("p (s q) -> p s q", s=2)
    P.barrier()


def dram(K, name, shape, dt, kind="Internal"):
    if DEBUG_OUT and kind == "Internal":
        kind = "ExternalOutput"
    return K.nc.dram_tensor(name, list(shape), dt, kind=kind).ap()


def st_rmsnorm(K, src, dst, gain_d, KC, NT, D, out_dt=BF16, TT=256):
    P = K.P
    TT = min(TT, NT)
    with P.stage():
        g = P.sbuf("rg", [128, KC], F32)
        bg = P.buf()
        P.dma("sp", P.dma_sem("rg"), g[:], gain_d, writes=[bg])
        NBF = 2
        KG = (KC + 7) // 8
        xt = [P.sbuf("rx", [128, KC, TT], F32) for _ in range(NBF)]
        bx = [P.bufs(KG) for _ in range(NBF)]
        sx = [P.dma_sem(f"rx{i}") for i in range(NBF)]
        ho = [P.sbuf("rh", [128, KC, TT], out_dt) for _ in range(NBF)]
        bh = [P.bufs(KG) for _ in range(NBF)]
        sh = [P.dma_sem(f"rh{i}") for i in range(NBF)]
        sq = [P.sbuf("rs", [128, TT], F32) for _ in range(4)]
        bsq = P.bufs(4)
        ps = [P.psum("rp", [128, 512])[:, 0:TT] for _ in range(2)]
        bps = P.bufs(2)
        rs = [P.sbuf("rr", [128, TT], F32) for _ in range(2)]
        brs = P.bufs(2)
        dd = P.buf()
        qi = 0
        for ti, t0 in enumerate(range(0, NT, TT)):
            s = ti % NBF
            pi = ti % 2
            for kg in range(KG):
                k0, k1 = kg * 8, min(KC, kg * 8 + 8)
                P.dma("sp", sx[s], xt[s][:, k0:k1, :], src[k0:k1, :, t0:t0 + TT].rearrange("k p t -> p k t"),
                      writes=[bx[s][kg]])
            for kc in range(KC):
                q = qi % 4
                qi += 1
                P.op("act", lambda e, q=q, s=s, kc=kc: e.activation(out=sq[q][:], in_=xt[s][:, kc, :], func=AF.Square),
                     reads=[bx[s][kc // 8]], writes=[bsq[q]])
                P.op("pe", lambda e, q=q, pi=pi, kc=kc: e.matmul(ps[pi][:], K.ones32, sq[q][:], start=(kc == 0), stop=(kc == KC - 1)),
                     reads=[bsq[q], K.bc], writes=[bps[pi]])
            P.op("dve", lambda e, pi=pi: e.tensor_scalar(out=rs[pi][:], in0=ps[pi][:], scalar1=1.0 / D, scalar2=EPS,
                                                         op0=ALU.mult, op1=ALU.add), reads=[bps[pi]], writes=[brs[pi]])
            P.op("act", lambda e, pi=pi: e.activation(out=rs[pi][:], in_=rs[pi][:], func=AF.Sqrt), reads=[brs[pi]], writes=[brs[pi]])
            P.op("dve", lambda e, pi=pi: e.reciprocal(out=rs[pi][:], in_=rs[pi][:]), reads=[brs[pi]], writes=[brs[pi]])
            for kc in range(KC):
                P.op("dve", lambda e, s=s, kc=kc, pi=pi: e.scalar_tensor_tensor(
                    out=ho[s][:, kc, :], in0=xt[s][:, kc, :], scalar=g[:, kc:kc + 1], in1=rs[pi][:],
                    op0=ALU.mult, op1=ALU.mult), reads=[bx[s][kc // 8], brs[pi], bg], writes=[bh[s][kc // 8]])
            for kg in range(KG):
                k0, k1 = kg * 8, min(KC, kg * 8 + 8)
                P.dma("sp", sh[s], dst[k0:k1, :, t0:t0 + TT].rearrange("k p t -> p k t"), ho[s][:, k0:k1, :],
                      reads=[bh[s][kg]], writes=[dd])


def st_linear(K, act, KCa, W, units, NT, T, epi_factory, kbase=0):
    P = K.P
    T = min(T, NT)
    NH = T // 512 if T >= 512 else 1
    TW = min(T, 512)
    with P.stage():
        at = P.sbuf("la", [128, KCa, T], BF16)
        KGa = (KCa + 7) // 8
        bat = P.bufs(KGa)
        sat = P.dma_sem("la")
        NS, NR = 4, 8
        stg = [P.sbuf("ls", [128, 8, 256], F32) for _ in range(NS)]
        bst = P.bufs(NS)
        sst = [P.dma_sem(f"ls{i}") for i in range(NS)]
        ring = [P.sbuf("lr", [128, 8, 256], BF16) for _ in range(NR)]
        brg = P.bufs(NR)
        nsets = 8 // (2 * NH)
        pss = [[[P.psum("lp", [128, 512]) for _ in range(NH)] for _ in range(2)] for _ in range(nsets)]
        bpss = [[[P.buf() for _ in range(NH)] for _ in range(2)] for _ in range(nsets)]
        epi = epi_factory(P)
        Wv = W.rearrange("(kc p) n -> p kc n", p=128)
        loads = []
        for ti, t0 in enumerate(range(0, NT, T)):
            for u, un in enumerate(units):
                ng = (un["nk"] + 7) // 8
                for gk in range(ng):
                    loads.append((ti, t0, u, gk, ng))
        cast_rot = ["dve", "act", "pool", "dve", "act"]
        state = {"next": 0}

        def emit_load(i):
            ti, t0, u, gk, ng = loads[i]
            un = units[u]
            st, r = i % NS, i % NR
            kk0 = un["k0"] + gk * 8
            nk = min(8, un["nk"] - gk * 8)
            ncol = un["nc"]
            P.dma("sp", sst[st], stg[st][:, 0:nk, 0:ncol], Wv[:, kk0:kk0 + nk, un["c0"]:un["c0"] + ncol], writes=[bst[st]])
            ce = cast_rot[i % len(cast_rot)]
            if ce == "act":
                P.op("act", lambda e, st=st, r=r, nk=nk, ncol=ncol: e.activation(func=AF.Copy, out=ring[r][:, 0:nk, 0:ncol], in_=stg[st][:, 0:nk, 0:ncol]),
                     reads=[bst[st]], writes=[brg[r]])
            else:
                P.op(ce, lambda e, st=st, r=r, nk=nk, ncol=ncol: e.tensor_copy(out=ring[r][:, 0:nk, 0:ncol], in_=stg[st][:, 0:nk, 0:ncol]),
                     reads=[bst[st]], writes=[brg[r]])

        LA = 5
        uc = 0
        for i, (ti, t0, u, gk, ng) in enumerate(loads):
            un = units[u]
            if u == 0 and gk == 0:
                for kg in range(KGa):
                    k0, k1 = kg * 8, min(KCa, kg * 8 + 8)
                    P.dma("sp", sat, at[:, k0:k1, :], act[kbase + k0:kbase + k1, :, t0:t0 + T].rearrange("k p t -> p k t"), writes=[bat[kg]])
            while state["next"] <= min(i + LA, len(loads) - 1):
                emit_load(state["next"])
                state["next"] += 1
            if gk == 0:
                sset = uc % nsets
                uc += 1
            r = i % NR
            kk0 = un["k0"] + gk * 8
            nk = min(8, un["nk"] - gk * 8)
            nbk = (un["nc"] + 127) // 128
            ps = pss[sset]
            bps = bpss[sset]

            def mm(e, r=r, kk0=kk0, nk=nk, nbk=nbk, ps=ps, gk=gk, ng=ng, un=un):
                ins = None
                for nb in range(nbk):
                    ncol = min(128, un["nc"] - nb * 128)
                    for kc in range(nk):
                        for th in range(NH):
                            ins = e.matmul(ps[nb][th][0:ncol, 0:TW], ring[r][:, kc, nb * 128:nb * 128 + ncol],
                                           at[:, kk0 + kc - kbase, th * TW:(th + 1) * TW],
                                           start=(gk == 0 and kc == 0), stop=(gk == ng - 1 and kc == nk - 1))
                return ins
            kgs = sorted(set((kk0 + j - kbase) // 8 for j in range(nk)))
            P.op("pe", mm, reads=[brg[r]] + [bat[k] for k in kgs],
                 writes=[bps[nb][th] for nb in range(nbk) for th in range(NH)])
            if gk == ng - 1:
                epi(u, un, t0, ps, bps)


class Stager:
    def __init__(self, P, name, dt, n=4, width=512):
        self.P = P
        self.t = [P.sbuf(name, [128, width], dt) for _ in range(n)]
        self.b = P.bufs(n)
        self.s = [P.dma_sem(f"{name}{i}") for i in range(n)]
        self.i = 0
        self.n = n
        self.dd = P.buf()

    def next(self):
        j = self.i % self.n
        self.i += 1
        return j


def epi_store_factory(K, route, TW=512):
    def factory(P):
        sb = Stager(P, "eb", BF16)
        sf = Stager(P, "ef", F32)
        cnt = {"i": 0}

        def epi(u, un, t0, ps, bps):
            nbk = (un["nc"] + 127) // 128
            for nb in range(nbk):
                dst, ch, nrows, dk, func = route(un, nb)
                S_ = sb if dk == "b" else sf
                for th in range(len(ps[nb])):
                    j = S_.next()
                    tw = min(TW, dst.shape[2] - t0)
                    cnt["i"] += 1
                    if func is None and cnt["i"] % 2 == 0:
                        P.op("dve", lambda e, j=j, S_=S_, nb=nb, th=th, nrows=nrows, tw=tw: e.tensor_copy(
                            out=S_.t[j][0:nrows, 0:tw], in_=ps[nb][th][0:nrows, 0:tw]), reads=[bps[nb][th]], writes=[S_.b[j]])
                    else:
                        f = AF.Copy if func is None else func
                        P.op("act", lambda e, j=j, S_=S_, nb=nb, th=th, nrows=nrows, tw=tw, f=f: e.activation(
                            out=S_.t[j][0:nrows, 0:tw], in_=ps[nb][th][0:nrows, 0:tw], func=f), reads=[bps[nb][th]], writes=[S_.b[j]])
                    P.dma("sp", S_.s[j], dst[ch, 0:nrows, t0 + th * tw:t0 + (th + 1) * tw], S_.t[j][0:nrows, 0:tw],
                          reads=[S_.b[j]], writes=[S_.dd])
        return epi
    return factory


def simple_units(N, KC, k0=0):
    return [dict(c0=c, nc=min(256, N - c), k0=k0, nk=KC) for c in range(0, N, 256)]


def st_attention(K, QKV, Y, S, sink_d):
    P = K.P
    ST = min(2048, S)
    scale = 128 ** -0.5
    with P.stage():
        sk = P.sbuf("sk", [128, 4], F32)
        bsk = P.buf()
        P.dma("sp", P.dma_sem("sk"), sk[:], sink_d, writes=[bsk])
        P.op("act", lambda e: e.activation(out=sk[:], in_=sk[:], func=AF.Exp), reads=[bsk], writes=[bsk])
        W = 2 * ST if ST == 2048 else ST + 2048
        NJ = 2
        kbuf = [P.sbuf("ak", [128, W], BF16) for _ in range(NJ)]
        vbuf = [P.sbuf("av", [128, W], BF16) for _ in range(NJ)]
        qbuf = [P.sbuf("aq", [128, 4, ST], BF16) for _ in range(NJ)]
        bk, bv, bq = P.bufs(NJ), P.bufs(NJ), P.bufs(NJ)
        sk_, sv_, sq_ = [P.dma_sem(f"ak{i}") for i in range(NJ)], [P.dma_sem(f"av{i}") for i in range(NJ)], [P.dma_sem(f"aq{i}") for i in range(NJ)]
        vtok = [P.sbuf("avt", [128, W], BF16) for _ in range(NJ)]
        bvt = P.bufs(NJ)
        pst = [P.psum("apt", [128, 512]) for _ in range(2)]
        bpst = P.bufs(2)
        scb = [[P.psum("asc", [128, 512]) for _ in range(2)] for _ in range(2)]
        bsc = [P.bufs(2) for _ in range(2)]
        pn = P.psum("apn", [128, 512])
        pd = P.psum("apd", [128, 512])
        bpn, bpd = P.buf(), P.buf()
        et = [P.sbuf("aet", [128, 2, 4, 128], BF16) for _ in range(2)]
        bet = P.bufs(2)
        dsb = [P.sbuf("ads", [128, 4, 128], F32) for _ in range(2)]
        bds = P.bufs(2)
        ya = [P.sbuf("aya", [128, 4, ST], BF16) for _ in range(1)]
        bya = P.buf()
        sya = P.dma_sem("aya")
        accN = [P.sbuf("aN", [128, ST], F32) for _ in range(2)]
        accD = [P.sbuf("aD", [128, ST], F32) for _ in range(2)]
        bacc = P.bufs(2)
        yb = [P.sbuf("ayb", [128, ST], BF16) for _ in range(2)]
        byb = P.bufs(2)
        syb = [P.dma_sem(f"ayb{i}") for i in range(2)]
        dd_ = P.buf()
        cnt = {"job": 0, "blk": 0, "tr": 0}

        def job(I, dd, qch, kch, vch, mask, mode, acc_i=None, first=False):
            js = cnt["job"] % NJ
            cnt["job"] += 1
            PREV = 128 * dd
            has_prev = I > 0
            Mb = ST // PREV
            nq = len(qch)
            lo = I * ST
            if has_prev:
                P.dma("sp", sk_[js], kbuf[js][:, 0:PREV + ST], QKV[kch, :, lo - PREV:lo + ST], writes=[bk[js]])
                P.dma("sp", sv_[js], vbuf[js][:, 0:PREV + ST], QKV[vch, :, lo - PREV:lo + ST], writes=[bv[js]])
            else:
                P.dma("sp", sk_[js], kbuf[js][:, PREV:PREV + ST], QKV[kch, :, lo:lo + ST], writes=[bk[js]])
                P.dma("sp", sv_[js], vbuf[js][:, PREV:PREV + ST], QKV[vch, :, lo:lo + ST], writes=[bv[js]])
            for h in range(nq):
                P.dma("sp", sq_[js], qbuf[js][:, h, :], QKV[qch[h], :, lo:lo + ST], writes=[bq[js]])
            kv = kbuf[js][:, 0:PREV + ST].rearrange("p (m i r) -> p r m i", i=128, r=dd)
            vv = vbuf[js][:, 0:PREV + ST].rearrange("p (m i r) -> p r m i", i=128, r=dd)
            vt = vtok[js][:, 0:PREV + ST].rearrange("p (r m i) -> p r m i", i=128, r=dd)
            qv = qbuf[js][:, :, :].rearrange("p h (m i r) -> p h r m i", i=128, r=dd)
            lvl = getattr(K, "att_lvl", 9)
            blocks = [(r, m) for r in range(dd) for m in range(0 if has_prev else 1, Mb + 1)]
            if lvl < 2:
                blocks = []
            groups = []
            for (r, m) in blocks:
                if groups and groups[-1][0] == r and groups[-1][1] + len(groups[-1][2]) == m and len(groups[-1][2]) < 4:
                    groups[-1][2].append(m)
                else:
                    groups.append([r, m, [m]])
            for (r, m0, ms) in groups:
                tb = cnt["tr"] % 2
                cnt["tr"] += 1
                n = len(ms)

                def trs(e, tb=tb, r=r, ms=ms):
                    ins = None
                    for j, m in enumerate(ms):
                        ins = e.matmul(pst[tb][:, j * 128:(j + 1) * 128], vv[:, r, m, :], K.identb, start=True, stop=True)
                    return ins
                P.op("pe", trs, reads=[bv[js], K.bc], writes=[bpst[tb]])
                if tb == 0:
                    P.op("act", lambda e, tb=tb, r=r, m0=m0, n=n: e.activation(out=vt[:, r, m0:m0 + n, :], in_=pst[tb][:, 0:n * 128].rearrange("p (m i) -> p m i", i=128), func=AF.Copy),
                         reads=[bpst[tb]], writes=[bvt[js]])
                else:
                    P.op("dve", lambda e, tb=tb, r=r, m0=m0, n=n: e.tensor_copy(out=vt[:, r, m0:m0 + n, :], in_=pst[tb][:, 0:n * 128].rearrange("p (m i) -> p m i", i=128)),
                         reads=[bpst[tb]], writes=[bvt[js]])
            for r in range(dd if lvl >= 3 else 0):
                for m in range(1, Mb + 1):
                    bi = cnt["blk"] % 2
                    cnt["blk"] += 1
                    two = has_prev or m > 1
                    slots = [(0, m - 1), (1, m)] if two else [(1, m)]
                    for (sl, kbm) in slots:
                        P.op("pe", lambda e, bi=bi, sl=sl, kbm=kbm, r=r, m=m: e.matmul(
                            scb[bi][sl][:, 0:nq * 128].rearrange("p (h q) -> p h q", h=nq), kv[:, r, kbm, :], qv[:, 0:nq, r, m - 1, :],
                            start=True, stop=True), reads=[bk[js], bq[js]], writes=[bsc[bi][sl]])
                        if lvl < 4:
                            continue
                        P.op("act", lambda e, bi=bi, sl=sl: e.activation(
                            out=et[bi][:, sl, 0:nq, :], in_=scb[bi][sl][:, 0:nq * 128].rearrange("p (h q) -> p h q", h=nq),
                            func=AF.Exp, scale=scale), reads=[bsc[bi][sl]], writes=[bet[bi]])
                        if lvl >= 5:
                            P.op("dve", lambda e, bi=bi, sl=sl: e.tensor_tensor(
                                out=et[bi][:, sl, 0:nq, :], in0=et[bi][:, sl, 0:nq, :],
                                in1=mask[:, sl, :].unsqueeze(1).broadcast_to([128, nq, 128]), op=ALU.mult),
                                reads=[bet[bi], K.bc], writes=[bet[bi]])

                    def pv(e, bi=bi, slots=slots, r=r):
                        ins = None
                        for j, (sl, kbm) in enumerate(slots):
                            e.matmul(pn[:, 0:nq * 128].rearrange("p (h q) -> p h q", h=nq), vt[:, r, kbm, :], et[bi][:, sl, 0:nq, :],
                                     start=(j == 0), stop=(j == len(slots) - 1))
                        for j, (sl, kbm) in enumerate(slots):
                            ins = e.matmul(pd[:, 0:nq * 128].rearrange("p (h q) -> p h q", h=nq), K.onesb, et[bi][:, sl, 0:nq, :],
                                           start=(j == 0), stop=(j == len(slots) - 1))
                        return ins
                    if lvl < 6:
                        continue
                    P.op("pe", pv, reads=[bet[bi], bvt[js], K.bc], writes=[bpn, bpd])
                    if lvl < 7:
                        continue
                    if mode == "swa":
                        P.op("dve", lambda e, bi=bi: e.tensor_tensor(
                            out=dsb[bi][:, 0:nq, :], in0=pd[:, 0:nq * 128].rearrange("p (h q) -> p h q", h=nq),
                            in1=sk[:, 0:nq].unsqueeze(2).broadcast_to([128, nq, 128]), op=ALU.add), reads=[bpd, bsk], writes=[bds[bi]])
                        P.op("dve", lambda e, bi=bi: e.reciprocal(out=dsb[bi][:, 0:nq, :], in_=dsb[bi][:, 0:nq, :]),
                             reads=[bds[bi]], writes=[bds[bi]])
                        yv = ya[0][:, :, :].rearrange("p h (m i r) -> p h r m i", i=128, r=dd)
                        P.op("dve", lambda e, bi=bi, r=r, m=m, yv=yv: e.tensor_tensor(
                            out=yv[:, :, r, m - 1, :], in0=pn[:, 0:nq * 128].rearrange("p (h q) -> p h q", h=nq),
                            in1=dsb[bi][:, 0:nq, :], op=ALU.mult), reads=[bpn, bds[bi]], writes=[bya])
                    else:
                        aN = accN[acc_i][:, :].rearrange("p (m i r) -> p r m i", i=128, r=dd)[:, r, m - 1, :]
                        aD = accD[acc_i][:, :].rearrange("p (m i r) -> p r m i", i=128, r=dd)[:, r, m - 1, :]
                        if first:
                            P.op("act", lambda e, aN=aN: e.activation(func=AF.Copy, out=aN, in_=pn[:, 0:128]), reads=[bpn], writes=[bacc[acc_i]])
                            P.op("dve", lambda e, aD=aD: e.tensor_copy(out=aD, in_=pd[:, 0:128]), reads=[bpd], writes=[bacc[acc_i]])
                        else:
                            P.op("dve", lambda e, aN=aN: e.tensor_tensor(out=aN, in0=pn[:, 0:128], in1=aN, op=ALU.add),
                                 reads=[bpn, bacc[acc_i]], writes=[bacc[acc_i]])
                            P.op("dve", lambda e, aD=aD: e.tensor_tensor(out=aD, in0=pd[:, 0:128], in1=aD, op=ALU.add),
                                 reads=[bpd, bacc[acc_i]], writes=[bacc[acc_i]])

        for I in range(S // ST):
            lo = I * ST
            jobs_on = getattr(K, "att_jobs", None)
            if jobs_on is None or "swa" in jobs_on:
                job(I, 1, [0, 1, 2, 3], 4, 5, K.m_swa, "swa")
                for h in range(4):
                    P.dma("sp", sya, Y[h, :, lo:lo + ST], ya[0][:, h, :], reads=[bya], writes=[dd_])
            dils = [(gi, dil) for gi, dil in enumerate((1, 4, 16)) if jobs_on is None or f"d{dil}" in jobs_on]
            for hs in range(3):
                if not dils:
                    break
                ai = hs % 2
                for gi, dil in dils:
                    base = 6 + 9 * gi
                    job(I, dil, [base + hs], base + 3 + hs, base + 6 + hs, K.m_dil, "dil", acc_i=ai, first=(gi == dils[0][0]))
                P.op("dve", lambda e, ai=ai: e.reciprocal(out=accD[ai][:], in_=accD[ai][:]), reads=[bacc[ai]], writes=[bacc[ai]])
                P.op("dve", lambda e, ai=ai: e.tensor_tensor(out=yb[ai][:], in0=accN[ai][:], in1=accD[ai][:], op=ALU.mult),
                     reads=[bacc[ai]], writes=[byb[ai]])
                P.dma("sp", syb[ai], Y[4 + hs, :, lo:lo + ST], yb[ai][:], reads=[byb[ai]], writes=[dd_])


def st_ssd(K, SSDP, DTr, Y, S, prm):
    P, nc = K.P, K.nc
    XC = dram(K, "ssd_xc", [4, 128, S], F32)
    BCb = dram(K, "ssd_bc", [4, 128, S], BF16)
    DTs = dram(K, "ssd_dt", [8, S], F32)
    DAs = dram(K, "ssd_da", [8, S], F32)
    TS = 512
    with P.stage():
        cw = P.sbuf("cw", [128, 8, 4], F32)
        cbias = P.sbuf("cbs", [128, 8], F32)
        dtb = P.sbuf("dtb", [8, 1], F32)
        an = P.sbuf("an", [8, 1], F32)
        bp_ = P.buf()
        sp_ = P.dma_sem("sprm")
        P.dma("sp", sp_, cw[:], prm["conv_w"], writes=[bp_])
        P.dma("sp", sp_, cbias[:], prm["conv_b"], writes=[bp_])
        P.dma("sp", sp_, dtb[:], prm["dt_bias"], writes=[bp_])
        P.dma("sp", sp_, an[:], prm["a_log"], writes=[bp_])
        P.op("act", lambda e: e.activation(out=an[:], in_=an[:], func=AF.Exp), reads=[bp_], writes=[bp_])
        P.op("dve", lambda e: e.tensor_scalar(out=an[:], in0=an[:], scalar1=-1.0, scalar2=None, op0=ALU.mult), reads=[bp_], writes=[bp_])
        NB_ = 3
        raw = [P.sbuf("craw", [128, TS + 3], F32) for _ in range(NB_)]
        braw = P.bufs(NB_)
        sraw = [P.dma_sem(f"craw{i}") for i in range(NB_)]
        acc = [P.sbuf("cacc", [128, TS], F32) for _ in range(NB_)]
        bacc = P.bufs(NB_)
        of = [P.sbuf("cof", [128, TS], F32) for _ in range(NB_)]
        ob = [P.sbuf("cob", [128, TS], BF16) for _ in range(NB_)]
        bof, bob = P.bufs(NB_), P.bufs(NB_)
        sof = [P.dma_sem(f"cof{i}") for i in range(NB_)]
        sob = [P.dma_sem(f"cob{i}") for i in range(NB_)]
        dtt = [P.sbuf("cdt", [8, TS], F32) for _ in range(2)]
        dat = [P.sbuf("cda", [8, TS], F32) for _ in range(2)]
        bdt, bda = P.bufs(2), P.bufs(2)
        sdt = [P.dma_sem(f"cdt{i}") for i in range(2)]
        sda = [P.dma_sem(f"cda{i}") for i in range(2)]
        dd_ = P.buf()
        it = 0
        for ti, t0 in enumerate(range(0, S, TS)):
            for j in range(8):
                s = it % NB_
                it += 1
                if t0 == 0:
                    P.op("pool", lambda e, s=s: e.memset(raw[s][:, 0:3], 0.0), writes=[braw[s]])
                    P.dma("sp", sraw[s], raw[s][:, 3:3 + TS], SSDP[4 + j, :, 0:TS], writes=[braw[s]])
                else:
                    P.dma("sp", sraw[s], raw[s][:, :], SSDP[4 + j, :, t0 - 3:t0 + TS], writes=[braw[s]])
                P.op("act", lambda e, s=s, j=j: e.activation(out=acc[s][:], in_=raw[s][:, 3:3 + TS], func=AF.Identity,
                                                              scale=cw[:, j, 3:4], bias=cbias[:, j:j + 1]),
                     reads=[braw[s], bp_], writes=[bacc[s]])
                for k in range(3):
                    P.op("dve", lambda e, s=s, j=j, k=k: e.scalar_tensor_tensor(
                        out=acc[s][:], in0=raw[s][:, k:k + TS], scalar=cw[:, j, k:k + 1], in1=acc[s][:],
                        op0=ALU.mult, op1=ALU.add), reads=[braw[s], bacc[s], bp_], writes=[bacc[s]])
                if j < 4:
                    P.op("act", lambda e, s=s: e.activation(out=of[s][:], in_=acc[s][:], func=AF.Silu), reads=[bacc[s]], writes=[bof[s]])
                    P.dma("sp", sof[s], XC[j, :, t0:t0 + TS], of[s][:], reads=[bof[s]], writes=[dd_])
                else:
                    P.op("act", lambda e, s=s: e.activation(out=ob[s][:], in_=acc[s][:], func=AF.Silu), reads=[bacc[s]], writes=[bob[s]])
                    P.dma("sp", sob[s], BCb[j - 4, :, t0:t0 + TS], ob[s][:], reads=[bob[s]], writes=[dd_])
            d = ti % 2
            P.dma("sp", sdt[d], dtt[d][:], DTr[:, t0:t0 + TS], reads=[], writes=[bdt[d]])
            P.op("act", lambda e, d=d: e.activation(out=dtt[d][:], in_=dtt[d][:], func=AF.Exp, bias=dtb[:, 0:1]), reads=[bdt[d], bp_], writes=[bdt[d]])
            P.op("act", lambda e, d=d: e.activation(out=dtt[d][:], in_=dtt[d][:], func=AF.Ln, bias=1.0), reads=[bdt[d]], writes=[bdt[d]])
            P.op("dve", lambda e, d=d: e.tensor_scalar(out=dat[d][:], in0=dtt[d][:], scalar1=an[:, 0:1], scalar2=None, op0=ALU.mult),
                 reads=[bdt[d], bp_], writes=[bda[d]])
            P.dma("sp", sdt[d], DTs[:, t0:t0 + TS], dtt[d][:], reads=[bdt[d]], writes=[dd_])
            P.dma("sp", sda[d], DAs[:, t0:t0 + TS], dat[d][:], reads=[bda[d]], writes=[dd_])

    if getattr(K, 'ssd_lvl', 9) < 2:
        return
    with P.stage():
        dsk = P.sbuf("dsk", [128, 4], F32)
        nw = P.sbuf("nw", [128, 4], F32)
        bp_ = P.buf()
        sp_ = P.dma_sem("sprm2")
        P.dma("sp", sp_, dsk[:], prm["d_skip"], writes=[bp_])
        P.dma("sp", sp_, nw[:], prm["ssm_norm"], writes=[bp_])
        NL = 2
        xc = [P.sbuf("sxc", [128, 4, TS], F32) for _ in range(NL)]
        zt = [P.sbuf("sz", [128, 4, TS], F32) for _ in range(NL)]
        bcb = [P.sbuf("sbc", [128, 4, TS], BF16) for _ in range(NL)]
        dtda = [P.sbuf("sdtda", [16, TS], F32) for _ in range(NL)]
        bxc, bz, bbc, bdd = P.bufs(NL), P.bufs(NL), P.bufs(NL), P.bufs(NL)
        sxc = [P.dma_sem(f"sxc{i}") for i in range(NL)]
        sz = [P.dma_sem(f"sz{i}") for i in range(NL)]
        sbc = [P.dma_sem(f"sbc{i}") for i in range(NL)]
        sdd = [P.dma_sem(f"sdd{i}") for i in range(NL)]
        yt = [P.sbuf("syt", [128, 4, TS], BF16) for _ in range(NL)]
        byt = P.bufs(NL)
        syt = [P.dma_sem(f"syt{i}") for i in range(NL)]
        H = P.sbuf("sH", [128, 8, 64], F32)
        Hb = P.sbuf("sHb", [128, 8, 128], BF16)
        bH, bHb = P.buf(), P.buf()
        P.op("pool", lambda e: e.memset(H[:], 0.0), writes=[bH])
        P.op("pool", lambda e: e.memset(Hb[:], 0.0), writes=[bHb])
        Xp = [P.sbuf("sXp", [128, 8, 128], BF16) for _ in range(2)]
        Xpp = [P.sbuf("sXpp", [128, 8, 64], BF16) for _ in range(2)]
        bXp, bXpp = P.bufs(2), P.bufs(2)
        for i in range(2):
            P.op("pool", lambda e, i=i: e.memset(Xp[i][:], 0.0), writes=[bXp[i]])
        p_small = P.psum("ssm", [128, 512])
        b_small = P.buf()
        p_xT = P.psum("sxT", [128, 512])
        b_xT1 = P.buf()
        b_xT = [b_xT1] * 4
        p_bt = P.psum("sbt", [128, 512])
        b_bt1 = P.buf()
        b_bt = [b_bt1] * 2
        p_gs = P.psum("sgs", [128, 512])
        p_gt = p_gs[:, 0:256]
        p_ss = p_gs[:, 256:512]
        b_gs1 = P.buf()
        b_gt = [b_gs1] * 2
        p_ar = [P.psum("sar", [128, 512]) for _ in range(2)]
        b_ar2 = P.bufs(2)
        b_ar = [b_ar2[h // 4] for h in range(8)]
        p_y = P.psum("sy", [128, 512])
        b_y1 = P.buf()
        b_y = [b_y1] * 4
        p_S = P.psum("sS", [128, 512])
        b_S = P.buf()
        b_ss = [b_gs1] * 2
        dtT = P.sbuf("sdtT", [128, 16], F32)
        acT = P.sbuf("sacT", [128, 8], F32)
        nacT = P.sbuf("snacT", [128, 8], F32)
        dsT = P.sbuf("sdsT", [128, 8], F32)
        w2 = P.sbuf("sw2", [128, 8], F32)
        cdb = P.sbuf("scdb", [128, 8], F32)
        acF = P.sbuf("sacF", [8, 128], F32)
        alT = P.sbuf("salT", [128, 8], F32)
        b_alT = P.buf()
        arsb = [P.sbuf("sarsb", [128, 512], F32) for _ in range(2)]
        b_arsb = P.bufs(2)
        b_dtT, b_acT, b_nacT, b_dsT, b_w2, b_cdb, b_acF = P.bufs(7)
        Btok = P.sbuf("sBtok", [128, 2, 128], BF16)
        b_Btok = P.bufs(2)
        tmp4 = [P.sbuf("stmp", [128, 512], F32) for _ in range(2)]
        Ld4 = [P.sbuf("sLd", [128, 512], F32) for _ in range(2)]
        Er4 = [P.sbuf("sEr", [128, 512], F32) for _ in range(2)]
        b_tmp4, b_Ld4, b_Er4 = P.bufs(2), P.bufs(2), P.bufs(2)
        Mh = P.sbuf("sMh", [128, 8, 128], BF16)
        Cp = P.sbuf("sCp", [128, 8, 128], BF16)
        b_Mh, b_Cp = P.bufs(8), P.bufs(8)
        t1 = P.sbuf("st1", [128, 4, 128], F32)
        szs = P.sbuf("sszs", [128, 4, 128], F32)
        yg = P.sbuf("syg", [128, 4, 128], F32)
        sqy = P.sbuf("ssqy", [128, 4, 128], F32)
        b_t1, b_szs, b_yg, b_sqy = P.bufs(4), P.bufs(4), P.bufs(4), P.bufs(4)
        rstd = P.sbuf("srstd", [128, 2, 128], F32)
        b_rstd = P.bufs(2)
        dd_ = P.buf()
        hcnt = 0
        for ti, t0 in enumerate(range(0, S, TS)):
            L = ti % NL
            for c in range(4):
                P.dma("sp", sxc[L], xc[L][:, c, :], XC[c, :, t0:t0 + TS], writes=[bxc[L]])
                P.dma("sp", sz[L], zt[L][:, c, :], SSDP[c, :, t0:t0 + TS], writes=[bz[L]])
                P.dma("sp", sbc[L], bcb[L][:, c, :], BCb[c, :, t0:t0 + TS], writes=[bbc[L]])
            P.dma("sp", sdd[L], dtda[L][0:8, :], DTs[:, t0:t0 + TS], writes=[bdd[L]])
            P.dma("sp", sdd[L], dtda[L][8:16, :], DAs[:, t0:t0 + TS], writes=[bdd[L]])
            for zi in range(TS // 128):
                sl = slice(zi * 128, (zi + 1) * 128)
                par = (ti * 4 + zi) % 2
                P.op("pe", lambda e, L=L, sl=sl: e.matmul(p_small[:, 256:272], dtda[L][:, sl], K.ident32[0:16, 0:16], start=True, stop=True),
                     reads=[bdd[L], K.bc], writes=[b_small])
                P.op("dve", lambda e: e.tensor_copy(out=dtT[:], in_=p_small[:, 256:272]), reads=[b_small], writes=[b_dtT])

                def cum(e):
                    e.matmul(p_small[:, 0:8], K.tri, dtT[:, 8:16], start=True, stop=True)
                    e.matmul(p_small[:, 8:16], K.ones32, dtT[:, 8:16], start=True, stop=True)
                    return e.matmul(p_small[0:8, 128:256], dtT[:, 8:16], K.tri, start=True, stop=True)
                P.op("pe", cum, reads=[b_dtT, K.bc], writes=[b_small])
                P.op("dve", lambda e: e.tensor_copy(out=acT[:], in_=p_small[:, 0:8]), reads=[b_small], writes=[b_acT])
                P.op("dve", lambda e: e.tensor_scalar(out=nacT[:], in0=p_small[:, 0:8], scalar1=-1.0, scalar2=None, op0=ALU.mult),
                     reads=[b_small], writes=[b_nacT])
                P.op("dve", lambda e: e.tensor_copy(out=alT[:], in_=p_small[:, 8:16]), reads=[b_small], writes=[b_alT])
                P.op("dve", lambda e: e.tensor_copy(out=acF[:], in_=p_small[0:8, 128:256]), reads=[b_small], writes=[b_acF])
                P.op("dve", lambda e: e.tensor_tensor(out=dsT[:], in0=alT[:], in1=acT[:], op=ALU.subtract),
                     reads=[b_alT, b_acT], writes=[b_dsT])
                P.op("act", lambda e: e.activation(out=dsT[:], in_=dsT[:], func=AF.Exp), reads=[b_dsT], writes=[b_dsT])
                P.op("act", lambda e: e.activation(out=cdb[:], in_=alT[:], func=AF.Exp), reads=[b_alT], writes=[b_cdb])
                P.op("dve", lambda e: e.tensor_tensor(out=w2[:], in0=dtT[:, 0:8], in1=dsT[:], op=ALU.mult),
                     reads=[b_dtT, b_dsT], writes=[b_w2])
                if getattr(K, 'ssd_lvl', 9) < 3:
                    continue
                def xtr(e, L=L, sl=sl):
                    ins = None
                    for c in range(4):
                        ins = e.matmul(p_xT[:, c * 128:(c + 1) * 128], xc[L][:, c, sl], K.ident32, start=True, stop=True)
                    return ins
                P.op("pe", xtr, reads=[bxc[L], K.bc], writes=[b_xT1])
                xT4 = p_xT[:, 0:512].rearrange("p (c two q) -> p c two q", two=2, q=64)
                Xpv = Xp[par][:].rearrange("p (c two) q -> p c two q", two=2)
                dt4 = dtT[:, 0:8].rearrange("p (c two) -> p c two", two=2)
                for h2 in range(2):
                    P.op("dve", lambda e, h2=h2, xT4=xT4, Xpv=Xpv, dt4=dt4: e.tensor_tensor(
                        out=Xpv[:, :, h2, h2 * 64:(h2 + 1) * 64], in0=xT4[:, :, h2, :],
                        in1=dt4[:, :, h2].unsqueeze(2).broadcast_to([128, 4, 64]), op=ALU.mult),
                        reads=[b_xT1, b_dtT], writes=[bXp[par]])
                P.op("dve", lambda e, par=par: e.tensor_tensor(
                    out=Xpp[par][:], in0=p_xT[:, 0:512].rearrange("p (h q) -> p h q", q=64),
                    in1=w2[:, 0:8].unsqueeze(2).broadcast_to([128, 8, 64]), op=ALU.mult),
                    reads=[b_xT1, b_w2], writes=[bXpp[par]])
                if getattr(K, 'ssd_lvl', 9) < 4:
                    continue
                def btr(e, L=L, sl=sl):
                    e.matmul(p_bt[:, 0:128], bcb[L][:, 0, sl], K.identb, start=True, stop=True)
                    return e.matmul(p_bt[:, 128:256], bcb[L][:, 1, sl], K.identb, start=True, stop=True)
                P.op("pe", btr, reads=[bbc[L], K.bc], writes=[b_bt1])
                P.op("act", lambda e: e.activation(out=Btok[:, :, :], in_=p_bt[:, 0:256].rearrange("p (g n) -> p g n", g=2), func=AF.Copy),
                     reads=[b_bt1], writes=[b_Btok[0], b_Btok[1]])

                def gtm(e, L=L, sl=sl):
                    e.matmul(p_gt[:, 0:128], bcb[L][:, 0, sl], bcb[L][:, 2, sl], start=True, stop=True)
                    return e.matmul(p_gt[:, 128:256], bcb[L][:, 1, sl], bcb[L][:, 3, sl], start=True, stop=True)
                P.op("pe", gtm, reads=[bbc[L]], writes=[b_gs1])
                if getattr(K, 'ssd_lvl', 9) < 5:
                    continue
                for bk in range(2):
                    gl = bk

                    def arm(e, h0=4 * bk):
                        ins = None
                        for hh in range(h0, h0 + 4):
                            ins = e.matmul(p_ar[hh // 4][:, (hh % 4) * 128:(hh % 4 + 1) * 128], K.sel[0:8, hh, :], acF[:], start=True, stop=True)
                        return ins
                    P.op("pe", arm, reads=[b_acF, K.bc], writes=[b_ar2[bk]])
                    P.op("dve", lambda e, bk=bk: e.tensor_copy(out=arsb[bk][:], in_=p_ar[bk][:]), reads=[b_ar2[bk]], writes=[b_arsb[bk]])
                    a4 = arsb[bk][:].rearrange("p (h l) -> p h l", h=4)
                    t4 = tmp4[bk][:].rearrange("p (h l) -> p h l", h=4)
                    l4 = Ld4[bk][:].rearrange("p (h l) -> p h l", h=4)
                    e4 = Er4[bk][:].rearrange("p (h l) -> p h l", h=4)
                    P.op("pool", lambda e, a4=a4, t4=t4: e.tensor_tensor(out=t4, in0=a4, in1=K.maskneg.unsqueeze(1).broadcast_to([128, 4, 128]), op=ALU.add),
                         reads=[b_arsb[bk], K.bc], writes=[b_tmp4[bk]])
                    P.op("pool", lambda e, t4=t4, bk=bk: e.tensor_tensor(out=t4, in0=t4, in1=nacT[:, 4 * bk:4 * bk + 4].unsqueeze(2).broadcast_to([128, 4, 128]), op=ALU.add),
                         reads=[b_tmp4[bk], b_nacT], writes=[b_tmp4[bk]])
                    P.op("act", lambda e, bk=bk: e.activation(out=Ld4[bk][:], in_=tmp4[bk][:], func=AF.Exp), reads=[b_tmp4[bk]], writes=[b_Ld4[bk]])
                    P.op("act", lambda e, bk=bk: e.activation(out=Er4[bk][:], in_=arsb[bk][:], func=AF.Exp), reads=[b_arsb[bk]], writes=[b_Er4[bk]])
                    P.op("dve", lambda e, bk=bk, gl=gl, l4=l4: e.tensor_tensor(
                        out=Mh[:, 4 * bk:4 * bk + 4, :], in0=p_gt[:, gl * 128:(gl + 1) * 128].unsqueeze(1).broadcast_to([128, 4, 128]), in1=l4, op=ALU.mult),
                        reads=[b_gs1, b_Ld4[bk]], writes=b_Mh[4 * bk:4 * bk + 4])
                    P.op("pool", lambda e, bk=bk, gl=gl, e4=e4, L=L, sl=sl: e.tensor_tensor(
                        out=Cp[:, 4 * bk:4 * bk + 4, :], in0=bcb[L][:, 2 + gl, sl].unsqueeze(1).broadcast_to([128, 4, 128]), in1=e4, op=ALU.mult),
                        reads=[bbc[L], b_Er4[bk]], writes=b_Cp[4 * bk:4 * bk + 4])
                if getattr(K, 'ssd_lvl', 9) < 6:
                    continue
                def ymm(e, par=par):
                    ins = None
                    for c in range(4):
                        for h2 in range(2):
                            h = 2 * c + h2
                            e.matmul(p_y[:, c * 128:(c + 1) * 128], Xp[par][:, h, :], Mh[:, h, :], start=(h2 == 0), stop=False)
                            ins = e.matmul(p_y[:, c * 128:(c + 1) * 128], Hb[:, h, :], Cp[:, h, :], start=False, stop=(h2 == 1))
                    return ins
                P.op("pe", ymm, reads=[bXp[par], bHb] + b_Mh + b_Cp, writes=[b_y1])
                if getattr(K, 'ssd_lvl', 9) < 7:
                    continue
                def smm(e, par=par):
                    ins = None
                    for h in range(8):
                        ins = e.matmul(p_S[:, h * 64:(h + 1) * 64], Btok[:, h // 4, :], Xpp[par][:, h, :], start=True, stop=True)
                    return ins
                P.op("pe", smm, reads=[b_Btok[0], b_Btok[1], bXpp[par]], writes=[b_S])
                P.op("pool", lambda e: e.tensor_tensor(out=H[:], in0=H[:], in1=cdb[:].unsqueeze(2).broadcast_to([128, 8, 64]), op=ALU.mult),
                     reads=[bH, b_cdb], writes=[bH])
                P.op("dve", lambda e: e.tensor_tensor(out=H[:], in0=p_S[:].rearrange("p (h q) -> p h q", h=8), in1=H[:], op=ALU.add),
                     reads=[bH, b_S], writes=[bH])
                Hv = H[:].rearrange("p (c two) q -> p c two q", two=2)
                Hbv = Hb[:].rearrange("p (c two) q -> p c two q", two=2)
                P.op("act", lambda e, Hv=Hv, Hbv=Hbv: e.activation(func=AF.Copy, out=Hbv[:, :, 0, 0:64], in_=Hv[:, :, 0, :]), reads=[bH], writes=[bHb])
                P.op("act", lambda e, Hv=Hv, Hbv=Hbv: e.activation(func=AF.Copy, out=Hbv[:, :, 1, 64:128], in_=Hv[:, :, 1, :]), reads=[bH], writes=[bHb])
                if getattr(K, 'ssd_lvl', 9) < 8:
                    continue
                P.op("pool", lambda e, L=L, sl=sl: e.tensor_tensor(out=t1[:], in0=xc[L][:, :, sl], in1=dsk[:, 0:4].unsqueeze(2).broadcast_to([128, 4, 128]), op=ALU.mult),
                     reads=[bxc[L], bp_], writes=[b_t1[0]])
                P.op("dve", lambda e: e.tensor_tensor(out=t1[:], in0=p_y[:].rearrange("p (c l) -> p c l", c=4), in1=t1[:], op=ALU.add),
                     reads=[b_t1[0], b_y1], writes=[b_t1[0]])
                P.op("act", lambda e, L=L, sl=sl: e.activation(out=szs[:], in_=zt[L][:, :, sl], func=AF.Silu), reads=[bz[L]], writes=[b_szs[0]])
                P.op("pool", lambda e: e.tensor_tensor(out=yg[:], in0=t1[:], in1=szs[:], op=ALU.mult), reads=[b_t1[0], b_szs[0]], writes=[b_yg[0]])
                P.op("act", lambda e: e.activation(out=sqy[:], in_=yg[:], func=AF.Square), reads=[b_yg[0]], writes=[b_sqy[0]])

                def ssm(e):
                    ins = None
                    for gl in range(2):
                        e.matmul(p_ss[:, gl * 128:(gl + 1) * 128], K.ones32, sqy[:, 2 * gl, :], start=True, stop=False)
                        ins = e.matmul(p_ss[:, gl * 128:(gl + 1) * 128], K.ones32, sqy[:, 2 * gl + 1, :], start=False, stop=True)
                    return ins
                P.op("pe", ssm, reads=[b_sqy[0], K.bc], writes=[b_gs1])
                P.op("dve", lambda e: e.tensor_scalar(out=rstd[:], in0=p_ss.rearrange("p (g l) -> p g l", g=2), scalar1=1.0 / 256, scalar2=EPS,
                                                      op0=ALU.mult, op1=ALU.add), reads=[b_gs1], writes=[b_rstd[0]])
                P.op("act", lambda e: e.activation(out=rstd[:], in_=rstd[:], func=AF.Sqrt), reads=[b_rstd[0]], writes=[b_rstd[0]])
                P.op("dve", lambda e: e.reciprocal(out=rstd[:], in_=rstd[:]), reads=[b_rstd[0]], writes=[b_rstd[0]])
                P.op("pool", lambda e: e.tensor_tensor(out=yg[:], in0=yg[:], in1=nw[:, 0:4].unsqueeze(2).broadcast_to([128, 4, 128]), op=ALU.mult),
                     reads=[b_yg[0], bp_], writes=[b_yg[0]])
                P.op("dve", lambda e, L=L, sl=sl: e.tensor_tensor(
                    out=yt[L][:, :, sl].rearrange("p (g c) l -> p g c l", g=2), in0=yg[:].rearrange("p (g c) l -> p g c l", g=2),
                    in1=rstd[:].unsqueeze(2).broadcast_to([128, 2, 2, 128]), op=ALU.mult), reads=[b_yg[0], b_rstd[0]], writes=[byt[L]])
            for c in range(4):
                if getattr(K, 'ssd_lvl', 9) < 8:
                    break
                P.dma("sp", syt[L], Y[7 + c, :, t0:t0 + TS], yt[L][:, c, :], reads=[byt[L]], writes=[dd_])


def build_A(cfg, upto=4):
    nc = bass.Bass("TRN2", target_bir_lowering=False)
    K = Kx()
    K.nc, K.P = nc, Prog(nc)
    D, S, KC = cfg.D, cfg.S, cfg.KC
    X = nc.dram_tensor("x", [KC, 128, S], F32, kind="ExternalInput").ap()
    WA = nc.dram_tensor("wa", [D, cfg.NA], F32, kind="ExternalInput").ap()
    gmix = nc.dram_tensor("gmix", [128, KC], F32, kind="ExternalInput").ap()
    sink = nc.dram_tensor("sink", [128, 4], F32, kind="ExternalInput").ap()
    prm = dict(
        conv_w=nc.dram_tensor("conv_w", [128, 8, 4], F32, kind="ExternalInput").ap(),
        conv_b=nc.dram_tensor("conv_b", [128, 8], F32, kind="ExternalInput").ap(),
        dt_bias=nc.dram_tensor("dt_bias", [8, 1], F32, kind="ExternalInput").ap(),
        a_log=nc.dram_tensor("a_log", [8, 1], F32, kind="ExternalInput").ap(),
        d_skip=nc.dram_tensor("d_skip", [128, 4], F32, kind="ExternalInput").ap(),
        ssm_norm=nc.dram_tensor("ssm_norm", [128, 4], F32, kind="ExternalInput").ap(),
    )
    Y = nc.dram_tensor("y", [11, 128, S], BF16, kind="ExternalOutput").ap()
    load_consts(K)
    Hd = dram(K, "a_h", [KC, 128, S], BF16)
    QKV = dram(K, "a_qkv", [33, 128, S], BF16)
    SSDP = dram(K, "a_ssdp", [12, 128, S], F32)
    DTr = dram(K, "a_dt", [1, 8, S], F32)
    st_rmsnorm(K, X, Hd, gmix, KC, S, D)

    def route(un, nb):
        ch = (un["c0"] + nb * 128) // 128
        if ch < 33:
            return QKV, ch, 128, "b", None
        if ch < 45:
            return SSDP, ch - 33, 128, "f", None
        return DTr, 0, 8, "f", None
    if upto >= 2:
        st_linear(K, Hd, KC, WA, simple_units(cfg.NA, KC), S, 1024, epi_store_factory(K, route))
    if upto >= 3:
        st_attention(K, QKV, Y, S, sink)
    if upto >= 4:
        st_ssd(K, SSDP, DTr[0], Y, S, prm)
    K.P.emit()
    return nc


def pack_A_inputs(cfg, l, g, xT_b, inp, cf, cb):
    D = cfg.D
    w_in = inp["w_in"][l]
    cols = []
    A_Q, A_KV = 2048, 512
    cols += list(range(g * 512, g * 512 + 512))
    cols += list(range(A_Q + g * 128, A_Q + g * 128 + 128))
    cols += list(range(A_Q + A_KV + g * 128, A_Q + A_KV + g * 128 + 128))
    B0 = 3072
    for gi in range(3):
        for part in range(3):
            o = B0 + gi * 3 * 1536 + part * 1536 + g * 384
            cols += list(range(o, o + 384))
    C0 = 3072 + 13824
    cols += list(range(C0 + g * 512, C0 + g * 512 + 512))
    XB = C0 + 2048
    cols += list(range(XB + g * 512, XB + g * 512 + 512))
    cols += list(range(XB + 2048 + g * 256, XB + 2048 + g * 256 + 256))
    cols += list(range(XB + 2048 + 1024 + g * 256, XB + 2048 + 1024 + g * 256 + 256))
    DT0 = C0 + 2048 + 4096
    cols += list(range(DT0 + g * 8, DT0 + g * 8 + 8))
    cols = np.asarray(cols)
    assert cols.size == cfg.NA
    wa = np.ascontiguousarray(w_in[:, cols])
    conv_ch = np.concatenate([np.arange(g * 512, g * 512 + 512), 2048 + np.arange(g * 256, g * 256 + 256),
                              2048 + 1024 + np.arange(g * 256, g * 256 + 256)])
    cw = inp["conv_w"][l][:, conv_ch]
    cw = np.ascontiguousarray(cw.T.reshape(8, 128, 4).transpose(1, 0, 2))
    cbias = np.ascontiguousarray(inp["conv_b"][l][conv_ch].reshape(8, 128).T)
    heads = np.arange(g * 8, g * 8 + 8)
    dsk = np.repeat(inp["d_skip"][l][heads], 64).reshape(4, 128).T
    return {
        "x": xT_b, "wa": wa,
        "gmix": np.ascontiguousarray(inp["norm_mix"][l].reshape(cfg.KC, 128).T),
        "sink": np.ascontiguousarray(np.broadcast_to(inp["attn_sink"][l][g * 4:g * 4 + 4][None, :], (128, 4))),
        "conv_w": cw, "conv_b": cbias,
        "dt_bias": np.ascontiguousarray(inp["dt_bias"][l][heads].reshape(8, 1)),
        "a_log": np.ascontiguousarray(inp["a_log"][l][heads].reshape(8, 1)),
        "d_skip": np.ascontiguousarray(dsk),
        "ssm_norm": np.ascontiguousarray(inp["ssm_norm"][l][g * 512:g * 512 + 512].reshape(4, 128).T),
        "cst_f": cf, "cst_b": cb,
    }


def st_xattn(K, Qd, KTd, VFd, Od, NT):
    P = K.P
    scale = 128 ** -0.5
    with P.stage():
        kt = P.sbuf("xk", [128, 4, 256], BF16)
        vf = P.sbuf("xv", [128, 4, 256], BF16)
        bkv = P.buf()
        s_ = P.dma_sem("xkv")
        for h in range(4):
            P.dma("sp", s_, kt[:, h, :], KTd[h, :, :], writes=[bkv])
            P.dma("sp", s_, vf[:, h, :], VFd[h, :, :], writes=[bkv])
        vtok = P.sbuf("xvt", [128, 2, 4, 128], BF16)
        bvt = P.buf()
        pt = [P.psum("xpt", [128, 512]) for _ in range(2)]
        bpt = P.bufs(2)
        for h in range(4):
            tb = h % 2

            def xtr(e, h=h, tb=tb):
                e.matmul(pt[tb][:, 0:128], vf[:, h, 0:128], K.identb, start=True, stop=True)
                return e.matmul(pt[tb][:, 128:256], vf[:, h, 128:256], K.identb, start=True, stop=True)
            P.op("pe", xtr, reads=[bkv, K.bc], writes=[bpt[tb]])
            P.op("act", lambda e, h=h, tb=tb: e.activation(out=vtok[:, :, h, :], in_=pt[tb][:, 0:256].rearrange("p (m d) -> p m d", m=2), func=AF.Copy),
                 reads=[bpt[tb]], writes=[bvt])
        TQ = 512
        q = [P.sbuf("xq", [128, 4, TQ], BF16) for _ in range(2)]
        bq = P.bufs(2)
        sq = [P.dma_sem(f"xq{i}") for i in range(2)]
        sc = [[P.psum("xsc", [128, 512]) for _ in range(2)] for _ in range(2)]
        bsc = [P.bufs(2) for _ in range(2)]
        pn, pd = P.psum("xpn", [128, 512]), P.psum("xpd", [128, 512])
        bpn, bpd = P.buf(), P.buf()
        et = [P.sbuf("xet", [128, 2, TQ], BF16) for _ in range(2)]
        bet = P.bufs(2)
        rec = [P.sbuf("xrec", [128, TQ], F32) for _ in range(2)]
        brec = P.bufs(2)
        ot = [P.sbuf("xo", [128, TQ], BF16) for _ in range(2)]
        bot = P.bufs(2)
        so = [P.dma_sem(f"xo{i}") for i in range(2)]
        dd_ = P.buf()
        it = 0
        for ti, t0 in enumerate(range(0, NT, TQ)):
            s = ti % 2
            for h in range(4):
                P.dma("sp", sq[s], q[s][:, h, :], Qd[h, :, t0:t0 + TQ], writes=[bq[s]])
            for h in range(4):
                b = it % 2
                it += 1
                for mc in range(2):
                    P.op("pe", lambda e, b=b, mc=mc, h=h, s=s: e.matmul(sc[b][mc][:], kt[:, h, mc * 128:(mc + 1) * 128], q[s][:, h, :], start=True, stop=True),
                         reads=[bkv, bq[s]], writes=[bsc[b][mc]])
                    P.op("act", lambda e, b=b, mc=mc: e.activation(out=et[b][:, mc, :], in_=sc[b][mc][:], func=AF.Exp, scale=scale),
                         reads=[bsc[b][mc]], writes=[bet[b]])

                def pv(e, b=b, h=h):
                    e.matmul(pn[:], vtok[:, 0, h, :], et[b][:, 0, :], start=True, stop=False)
                    e.matmul(pn[:], vtok[:, 1, h, :], et[b][:, 1, :], start=False, stop=True)
                    e.matmul(pd[:], K.onesb, et[b][:, 0, :], start=True, stop=False)
                    return e.matmul(pd[:], K.onesb, et[b][:, 1, :], start=False, stop=True)
                P.op("pe", pv, reads=[bet[b], bvt, K.bc], writes=[bpn, bpd])
                P.op("dve", lambda e, b=b: e.reciprocal(out=rec[b][:], in_=pd[:]), reads=[bpd], writes=[brec[b]])
                P.op("dve", lambda e, b=b: e.tensor_tensor(out=ot[b][:], in0=pn[:], in1=rec[b][:], op=ALU.mult),
                     reads=[bpn, brec[b]], writes=[bot[b]])
                P.dma("sp", so[b], Od[h, :, t0:t0 + TQ], ot[b][:], reads=[bot[b]], writes=[dd_])


def epi_residual_factory(K, Xin, Xout):
    def factory(P):
        xi = [P.sbuf("eri", [128, 512], F32) for _ in range(4)]
        bxi = P.bufs(4)
        sxi = [P.dma_sem(f"eri{i}") for i in range(4)]
        so = Stager(P, "ero", F32)
        c = {"i": 0}

        def epi(u, un, t0, ps, bps):
            nbk = (un["nc"] + 127) // 128
            for nb in range(nbk):
                ch = (un["c0"] + nb * 128) // 128
                for th in range(len(ps[nb])):
                    j = c["i"] % 4
                    c["i"] += 1
                    tw = min(512, Xin.shape[2] - t0)
                    P.dma("sp", sxi[j], xi[j][:, 0:tw], Xin[ch, :, t0 + th * tw:t0 + (th + 1) * tw], writes=[bxi[j]])
                    k = so.next()
                    P.op("dve", lambda e, j=j, k=k, nb=nb, th=th, tw=tw: e.tensor_tensor(out=so.t[k][:, 0:tw], in0=ps[nb][th][:, 0:tw], in1=xi[j][:, 0:tw], op=ALU.add),
                         reads=[bps[nb][th], bxi[j]], writes=[so.b[k]])
                    P.dma("sp", so.s[k], Xout[ch, :, t0 + th * tw:t0 + (th + 1) * tw], so.t[k][:, 0:tw], reads=[so.b[k]], writes=[so.dd])
        return epi
    return factory


def epi_swiglu_factory(K, ACTd):
    def factory(P):
        sg = [P.sbuf("esg", [128, 512], F32) for _ in range(4)]
        bsg = P.bufs(4)
        so = Stager(P, "eso", BF16)
        c = {"i": 0}

        def epi(u, un, t0, ps, bps):
            jb = un["c0"] // 256
            for th in range(len(ps[0])):
                j = c["i"] % 4
                c["i"] += 1
                tw = min(512, ACTd.shape[2] - t0)
                P.op("act", lambda e, j=j, th=th, tw=tw: e.activation(out=sg[j][:, 0:tw], in_=ps[0][th][:, 0:tw], func=AF.Silu),
                     reads=[bps[0][th]], writes=[bsg[j]])
                k = so.next()
                P.op("dve", lambda e, j=j, k=k, th=th, tw=tw: e.tensor_tensor(out=so.t[k][:, 0:tw], in0=ps[1][th][:, 0:tw], in1=sg[j][:, 0:tw], op=ALU.mult),
                     reads=[bps[1][th], bsg[j]], writes=[so.b[k]])
                P.dma("sp", so.s[k], ACTd[jb, :, t0 + th * tw:t0 + (th + 1) * tw], so.t[k][:, 0:tw], reads=[so.b[k]], writes=[so.dd])
        return epi
    return factory


def epi_gated_factory(K, Gd, Md, KC):
    def factory(P):
        gt = [P.sbuf("egg", [128, 512], F32) for _ in range(4)]
        bgt = P.bufs(4)
        sgt = [P.dma_sem(f"egg{i}") for i in range(4)]
        acc = [P.sbuf("ega", [128, 2, 2, 512], F32) for _ in range(2)]
        bacc = [[P.bufs(2) for _ in range(2)] for _ in range(2)]
        so = Stager(P, "ego", BF16)
        c = {"i": 0, "u": 0}

        def epi(u, un, t0, ps, bps):
            br = un["br"]
            if br == 0:
                c["u"] += 1
            a = c["u"] % 2
            nbk = (un["nc"] + 127) // 128
            for nb in range(nbk):
                ch = (un["c0"] + nb * 128) // 128
                for th in range(len(ps[nb])):
                    j = c["i"] % 4
                    c["i"] += 1
                    tw = min(512, Gd.shape[2] - t0)
                    tt = t0 + th * tw
                    P.dma("sp", sgt[j], gt[j][:, 0:tw], Gd[br * KC + ch, :, tt:tt + tw], writes=[bgt[j]])
                    if br == 0:
                        P.op("dve", lambda e, a=a, nb=nb, th=th, j=j, tw=tw: e.tensor_tensor(out=acc[a][:, nb, th, 0:tw], in0=ps[nb][th][:, 0:tw], in1=gt[j][:, 0:tw], op=ALU.mult),
                             reads=[bps[nb][th], bgt[j]], writes=[bacc[a][nb][th]])
                    else:
                        P.op("dve", lambda e, nb=nb, th=th, j=j, tw=tw: e.tensor_tensor(out=gt[j][:, 0:tw], in0=ps[nb][th][:, 0:tw], in1=gt[j][:, 0:tw], op=ALU.mult),
                             reads=[bps[nb][th], bgt[j]], writes=[bgt[j]])
                        if br == 1:
                            P.op("pool", lambda e, a=a, nb=nb, th=th, j=j, tw=tw: e.tensor_tensor(out=acc[a][:, nb, th, 0:tw], in0=acc[a][:, nb, th, 0:tw], in1=gt[j][:, 0:tw], op=ALU.add),
                                 reads=[bacc[a][nb][th], bgt[j]], writes=[bacc[a][nb][th]])
                        else:
                            k = so.next()
                            P.op("pool", lambda e, a=a, nb=nb, th=th, j=j, k=k, tw=tw: e.tensor_tensor(out=so.t[k][:, 0:tw], in0=acc[a][:, nb, th, 0:tw], in1=gt[j][:, 0:tw], op=ALU.add),
                                 reads=[bacc[a][nb][th], bgt[j]], writes=[so.b[k]])
                            P.dma("sp", so.s[k], Md[ch, :, tt:tt + tw], so.t[k][:, 0:tw], reads=[so.b[k]], writes=[so.dd])
        return epi
    return factory


def build_B(cfg, last):
    nc = bass.Bass("TRN2", target_bir_lowering=False)
    K = Kx()
    K.nc, K.P = nc, Prog(nc)
    D, KC, NT, F, FC = cfg.D, cfg.KC, cfg.NT, cfg.F, cfg.FC

    def inp(name, shape, dt=F32):
        return nc.dram_tensor(name, list(shape), dt, kind="ExternalInput").ap()
    X = inp("x", [KC, 128, NT])
    Yd = inp("y", [44, 128, NT], BF16)
    MEMd = inp("mem", [KC, 128, 256])
    WG = inp("wg", [D, 3 * D])
    WBR = inp("wbr", [5632, D])
    WO = inp("wo", [D, D])
    WXQ, WXK, WXV = inp("wxq", [D, 512]), inp("wxk", [D, 512]), inp("wxv", [D, 512])
    WXO = inp("wxo", [512, D])
    WGU = inp("wgu", [D, 2 * F])
    WD = inp("wd", [F, D])
    gmix, gx, gmem, gffn = inp("gmix", [128, KC]), inp("gx", [128, KC]), inp("gmem", [128, KC]), inp("gffn", [128, KC])
    gfin = inp("gfin", [128, KC])
    OUT = nc.dram_tensor("out", [KC, 128, NT], F32, kind="ExternalOutput").ap()
    load_consts(K)
    Hd = dram(K, "b_h", [KC, 128, NT], BF16)
    Gd = dram(K, "b_g", [3 * KC, 128, NT], F32)
    Md = dram(K, "b_m", [KC, 128, NT], BF16)
    X1 = dram(K, "b_x1", [KC, 128, NT], F32)
    HX = dram(K, "b_hx", [KC, 128, NT], BF16)
    Qd = dram(K, "b_q", [4, 128, NT], BF16)
    HM = dram(K, "b_hm", [KC, 128, 256], BF16)
    KTd = dram(K, "b_kt", [4, 128, 256], BF16)
    VFd = dram(K, "b_vf", [4, 128, 256], BF16)
    Od = dram(K, "b_o", [4, 128, NT], BF16)
    X2 = dram(K, "b_x2", [KC, 128, NT], F32)
    HF = dram(K, "b_hf", [KC, 128, NT], BF16)
    ACTd = dram(K, "b_act", [FC, 128, NT], BF16)
    X3 = OUT if not last else dram(K, "b_x3", [KC, 128, NT], F32)

    def store_to(dst, dk, func=None):
        def route(un, nb):
            return dst, (un["c0"] + nb * 128) // 128, 128, dk, func
        return epi_store_factory(K, route)
    T1 = 1024 if KC * 1024 * 2 <= 65536 else 512
    st_rmsnorm(K, X, Hd, gmix, KC, NT, D)
    st_linear(K, Hd, KC, WG, simple_units(3 * D, KC), NT, T1, store_to(Gd, "f", AF.Sigmoid))
    units = []
    kofs = [0, 16, 28]
    knum = [16, 12, 16]
    for c0 in range(0, D, 256):
        for br in range(3):
            units.append(dict(c0=c0, nc=min(256, D - c0), k0=kofs[br], nk=knum[br], br=br))
    st_linear(K, Yd, 44, WBR, units, NT, 1024, epi_gated_factory(K, Gd, Md, KC))
    st_linear(K, Md, KC, WO, simple_units(D, KC), NT, T1, epi_residual_factory(K, X, X1))
    st_rmsnorm(K, X1, HX, gx, KC, NT, D)
    st_linear(K, HX, KC, WXQ, simple_units(512, KC), NT, T1, store_to(Qd, "b"))
    st_rmsnorm(K, MEMd, HM, gmem, KC, 256, D)
    st_linear(K, HM, KC, WXK, simple_units(512, KC), 256, 256, store_to(KTd, "b"))
    st_linear(K, HM, KC, WXV, simple_units(512, KC), 256, 256, store_to(VFd, "b"))
    st_xattn(K, Qd, KTd, VFd, Od, NT)
    st_linear(K, Od, 4, WXO, simple_units(D, 4), NT, 1024, epi_residual_factory(K, X1, X2))
    st_rmsnorm(K, X2, HF, gffn, KC, NT, D)
    st_linear(K, HF, KC, WGU, simple_units(2 * F, KC), NT, T1, epi_swiglu_factory(K, ACTd))
    Xh = dram(K, "b_xh", [KC, 128, NT], F32)
    F1 = FC // 2
    st_linear(K, ACTd, F1, WD, simple_units(D, F1, 0), NT, 1024, epi_residual_factory(K, X2, Xh), kbase=0)
    st_linear(K, ACTd, FC - F1, WD, simple_units(D, FC - F1, F1), NT, 1024, epi_residual_factory(K, Xh, X3), kbase=F1)
    if last:
        st_rmsnorm(K, X3, OUT, gfin, KC, NT, D, out_dt=F32)
    K.P.emit()
    return nc


def pack_B_weights(cfg, l, inp, cf, cb):
    D, F = cfg.D, cfg.F
    G0 = 3072 + 13824 + 6176
    wg = np.ascontiguousarray(inp["w_in"][l][:, G0:G0 + 3 * D])
    fg = inp["ffn_gate"][l].reshape(D, F // 128, 128)
    fu = inp["ffn_up"][l].reshape(D, F // 128, 128)
    wgu = np.ascontiguousarray(np.stack([fg, fu], axis=2).reshape(D, 2 * F))

    def gl(v):
        return np.ascontiguousarray(v.reshape(cfg.KC, 128).T)
    return {
        "wg": wg, "wbr": inp["w_branch"][l], "wo": inp["w_out"][l],
        "wxq": inp["xattn_q"][l], "wxk": inp["xattn_k"][l], "wxv": inp["xattn_v"][l], "wxo": inp["xattn_o"][l],
        "wgu": wgu, "wd": inp["ffn_down"][l],
        "gmix": gl(inp["norm_mix"][l]), "gx": gl(inp["norm_x"][l]), "gmem": gl(inp["norm_mem"][l]),
        "gffn": gl(inp["norm_ffn"][l]), "gfin": gl(inp["norm_final"]),
        "cst_f": cf, "cst_b": cb,
    }


_PROGS = {}
DEBUG_STORE = {}


def get_prog(key, builder):
    if key not in _PROGS:
        _PROGS[key] = builder()
    return _PROGS[key]


def run_model(cfg, inp):
    cf, cb = host_consts()
    B, S, D, KC, NT = cfg.B, cfg.S, cfg.D, cfg.KC, cfg.NT
    assert B * 4 == NCORES
    xT = [np.ascontiguousarray(np.asarray(inp["x"][b], np.float32).T).reshape(KC, 128, S) for b in range(B)]
    memT = [np.ascontiguousarray(np.asarray(inp["mem"][b], np.float32).T).reshape(KC, 128, 256) for b in range(B)]
    inp = {k: np.asarray(v, np.float32) for k, v in inp.items()}
    for l in range(2):
        ncA = get_prog(("A", cfg.D, cfg.S), lambda: build_A(cfg))
        in_maps = [pack_A_inputs(cfg, l, c % 4, xT[c // 4], inp, cf, cb) for c in range(NCORES)]
        resA = run_bass_kernel_spmd(ncA, in_maps, core_ids=list(range(NCORES))).results
        del in_maps
        Yb = []
        for b in range(B):
            ya = np.concatenate([resA[b * 4 + g]["y"][0:4] for g in range(4)], axis=0)
            yb = np.concatenate([resA[b * 4 + g]["y"][4:7] for g in range(4)], axis=0)
            yc = np.concatenate([resA[b * 4 + g]["y"][7:11] for g in range(4)], axis=0)
            Yb.append(np.concatenate([ya, yb, yc], axis=0))
        DEBUG_STORE[f'Y{l}'] = Yb
        last = (l == 1)
        ncB = get_prog(("B", cfg.D, cfg.S, last), lambda: build_B(cfg, last))
        wts = pack_B_weights(cfg, l, inp, cf, cb)
        in_maps = []
        for c in range(NCORES):
            b, qd = c // 4, c % 4
            m = dict(wts)
            m["x"] = np.ascontiguousarray(xT[b][:, :, qd * NT:(qd + 1) * NT])
            m["y"] = np.ascontiguousarray(Yb[b][:, :, qd * NT:(qd + 1) * NT])
            m["mem"] = memT[b]
            in_maps.append(m)
        resB = run_bass_kernel_spmd(ncB, in_maps, core_ids=list(range(NCORES))).results
        del in_maps, wts
        xT = [np.concatenate([resB[b * 4 + qd]["out"] for qd in range(4)], axis=2) for b in range(B)]
        DEBUG_STORE[f'X{l}'] = xT
    out = np.stack([xT[b].reshape(D, S).T for b in range(B)], axis=0)
    return np.ascontiguousarray(out.astype(np.float32))


def kernel(**inputs):
    cfg = Cfg()
    return run_model(cfg, inputs)
```

```python
import contextlib
import numpy as np
import ml_dtypes
import concourse.bass as bass
import concourse.mybir as mybir
from concourse.bass_utils import run_bass_kernel_spmd

F32 = mybir.dt.float32
BF16 = mybir.dt.bfloat16
AF = mybir.ActivationFunctionType
ALU = mybir.AluOpType
NPBF = ml_dtypes.bfloat16

EPS = 1e-5
NCORES = 8
SAME_ENG_SYNC = True
WAIT_SORT = True
DEBUG_OUT = False


class Cfg:
    def __init__(self, D=4096, S=8192, B=2, F=None):
        self.D, self.S, self.B = D, S, B
        self.F = F if F is not None else -(-(8 * D) // (3 * 256)) * 256
        self.KC = D // 128
        self.FC = self.F // 128
        self.NT = S // 4
        self.MEM = 256
        self.NA = 33 * 128 + 12 * 128 + 8


class Buf:
    __slots__ = ("name", "w", "r")

    def __init__(self, name=""):
        self.name, self.w, self.r = name, None, []


class Prog:
    ENGS = ("pe", "dve", "act", "pool", "sp")

    def __init__(self, nc):
        self.nc = nc
        self.streams = {e: [] for e in self.ENGS}
        self.cnt = {e: 0 for e in self.ENGS}
        self.waited = {e: {} for e in self.ENGS}
        self.dma_cnt = {}
        self.semkeys = list(self.ENGS)
        self.base = contextlib.ExitStack()
        self.stack = self.base
        self.nb = 0
        self.uid = 0

    def sbuf(self, name, shape, dtype):
        self.uid += 1
        return self.stack.enter_context(self.nc.sbuf_tensor(f"{name}_{self.uid}", list(shape), dtype))

    def psum(self, name, shape, dtype=F32):
        self.uid += 1
        return self.stack.enter_context(self.nc.psum_tensor(f"{name}_{self.uid}", list(shape), dtype))

    def buf(self, name=""):
        self.nb += 1
        return Buf(name or f"b{self.nb}")

    def bufs(self, n):
        return [self.buf() for _ in range(n)]

    def dma_sem(self, name):
        key = "dma_" + name
        if key not in self.dma_cnt:
            self.dma_cnt[key] = 0
            self.semkeys.append(key)
        return key

    def _deps(self, eng, reads, writes):
        need = {}

        def add(tok):
            if tok is None:
                return
            k, v = tok
            if k == eng and (eng == "pe" or not SAME_ENG_SYNC):
                return
            if need.get(k, 0) < v:
                need[k] = v
        for b in reads:
            add(b.w)
        for b in writes:
            add(b.w)
            for t in b.r:
                add(t)
        out = []
        wd = self.waited[eng]
        order = {"pe": 3, "dve": 1, "act": 1, "pool": 1}
        for k, v in sorted(need.items(), key=lambda kv: order.get(kv[0], 5) if WAIT_SORT else 0):
            if wd.get(k, 0) < v:
                wd[k] = v
                out.append((k, v))
        return out

    def _commit(self, tok, reads, writes):
        for b in reads:
            b.r.append(tok)
        for b in writes:
            b.w = tok
            b.r = []

    def op(self, eng, fn, reads=(), writes=()):
        waits = self._deps(eng, reads, writes)
        self.cnt[eng] += 1
        tok = (eng, self.cnt[eng])
        self.streams[eng].append((waits, fn, (eng, 1)))
        self._commit(tok, reads, writes)
        return tok

    def dma(self, queue, semkey, out_ap, in_ap, reads=(), writes=()):
        waits = self._deps(queue, reads, writes)
        self.dma_cnt[semkey] += 16
        tok = (semkey, self.dma_cnt[semkey])

        def fn(e, out_ap=out_ap, in_ap=in_ap):
            return e.dma_start(out=out_ap, in_=in_ap)
        self.streams[queue].append((waits, fn, (semkey, 16)))
        self._commit(tok, reads, writes)
        return tok

    def barrier(self):
        allt = [(k, v) for k, v in self.dma_cnt.items() if v > 0]
        allt += [(e, self.cnt[e]) for e in self.ENGS if self.cnt[e] > 0]
        for eng in self.ENGS:
            wd = self.waited[eng]
            waits = []
            for k, v in allt:
                if k == eng:
                    continue
                if wd.get(k, 0) < v:
                    wd[k] = v
                    waits.append((k, v))
            if waits:
                self.streams[eng].append((waits, None, None))

    @contextlib.contextmanager
    def stage(self):
        saved = self.stack
        self.stack = contextlib.ExitStack()
        try:
            yield
        finally:
            self.stack.close()
            self.stack = saved
            self.barrier()

    def emit(self):
        nc = self.nc
        self.barrier()
        sems = {}
        for k in self.semkeys:
            sems[k] = self.base.enter_context(nc.semaphore("s_" + k))

        def replay(name, e):
            for waits, fn, inc in self.streams[name]:
                for k, v in waits:
                    e.wait_ge(sems[k], v)
                if fn is not None:
                    ins = fn(e)
                    ins.then_inc(sems[inc[0]], inc[1])

        with nc.Block() as block:
            @block.tensor
            def _(e):
                replay("pe", e)

            @block.vector
            def _(e):
                replay("dve", e)

            @block.scalar
            def _(e):
                replay("act", e)

            @block.gpsimd
            def _(e):
                replay("pool", e)

            @block.sync
            def _(e):
                replay("sp", e)
        self.base.close()


class Kx:
    pass


def host_consts():
    s = np.arange(128)[:, None]
    l = np.arange(128)[None, :]
    ones = np.ones((128, 128), np.float32)
    ident = np.eye(128, dtype=np.float32)
    tri = (s <= l).astype(np.float32)
    maskneg = np.where(s <= l, 0.0, -1e30).astype(np.float32)
    sel = np.zeros((128, 8, 128), np.float32)
    for h in range(8):
        sel[h, h, :] = 1.0
    cf = np.concatenate([ones, ident, tri, maskneg, sel.reshape(128, 1024)], axis=1)
    own = (s <= l)
    prev_swa = (s > l)
    prev_dil = (s >= l)
    m_swa = np.stack([prev_swa, own], axis=1).astype(np.float32).reshape(128, 256)
    m_dil = np.stack([prev_dil, own], axis=1).astype(np.float32).reshape(128, 256)
    cb = np.concatenate([ones, ident, m_swa, m_dil], axis=1).astype(NPBF)
    return cf, cb


def load_consts(K):
    P, nc = K.P, K.nc
    cfd = nc.dram_tensor("cst_f", [128, 1536], F32, kind="ExternalInput").ap()
    cbd = nc.dram_tensor("cst_b", [128, 768], BF16, kind="ExternalInput").ap()
    cf = P.sbuf("cf", [128, 1536], F32)
    cb = P.sbuf("cb", [128, 768], BF16)
    K.bc = P.buf("consts")
    s = P.dma_sem("cst")
    P.dma("sp", s, cf[:], cfd, writes=[K.bc])
    t = P.dma("sp", s, cb[:], cbd, writes=[K.bc])
    K.ones32 = cf[:, 0:128]
    K.ident32 = cf[:, 128:256]
    K.tri = cf[:, 256:384]
    K.maskneg = cf[:, 384:512]
    K.sel = cf[:, 512:1536].rearrange("p (h m) -> p h m", h=8)
    K.onesb = cb[:, 0:128]
    K.identb = cb[:, 128:256]
    K.m_swa = cb[:, 256:512].rearrange("p (s q) -> p s q", s=2)
    K.m_dil = cb[:, 512:768].rearrange("p (s q) -> p s q", s=2)
    P.barrier()


def dram(K, name, shape, dt, kind="Internal"):
    if DEBUG_OUT and kind == "Internal":
        kind = "ExternalOutput"
    return K.nc.dram_tensor(name, list(shape), dt, kind=kind).ap()


def st_rmsnorm(K, src, dst, gain_d, KC, NT, D, out_dt=BF16, TT=256):
    P = K.P
    TT = min(TT, NT)
    with P.stage():
        g = P.sbuf("rg", [128, KC], F32)
        bg = P.buf()
        P.dma("sp", P.dma_sem("rg"), g[:], gain_d, writes=[bg])
        NBF = 2
        KG = (KC + 7) // 8
        xt = [P.sbuf("rx", [128, KC, TT], F32) for _ in range(NBF)]
        bx = [P.bufs(KG) for _ in range(NBF)]
        sx = [P.dma_sem(f"rx{i}") for i in range(NBF)]
        ho = [P.sbuf("rh", [128, KC, TT], out_dt) for _ in range(NBF)]
        bh = [P.bufs(KC) for _ in range(NBF)]
        sh = [P.dma_sem(f"rh{i}") for i in range(NBF)]
        sq = [P.sbuf("rs", [128, TT], F32) for _ in range(4)]
        bsq = P.bufs(4)
        ps = [P.psum("rp", [128, 512])[:, 0:TT] for _ in range(2)]
        bps = P.bufs(2)
        rs = [P.sbuf("rr", [128, TT], F32) for _ in range(2)]
        brs = P.bufs(2)
        dd = P.buf()
        qi = 0
        for ti, t0 in enumerate(range(0, NT, TT)):
            s = ti % NBF
            pi = ti % 2
            for kg in range(KG):
                k0, k1 = kg * 8, min(KC, kg * 8 + 8)
                P.dma("sp", sx[s], xt[s][:, k0:k1, :], src[k0:k1, :, t0:t0 + TT].rearrange("k p t -> p k t"),
                      writes=[bx[s][kg]])
            for kc in range(KC):
                q = qi % 4
                qi += 1
                P.op("act", lambda e, q=q, s=s, kc=kc: e.activation(out=sq[q][:], in_=xt[s][:, kc, :], func=AF.Square),
                     reads=[bx[s][kc // 8]], writes=[bsq[q]])
                P.op("pe", lambda e, q=q, pi=pi, kc=kc: e.matmul(ps[pi][:], K.ones32, sq[q][:], start=(kc == 0), stop=(kc == KC - 1)),
                     reads=[bsq[q], K.bc], writes=[bps[pi]])
            P.op("dve", lambda e, pi=pi: e.tensor_scalar(out=rs[pi][:], in0=ps[pi][:], scalar1=1.0 / D, scalar2=EPS,
                                                         op0=ALU.mult, op1=ALU.add), reads=[bps[pi]], writes=[brs[pi]])
            P.op("act", lambda e, pi=pi: e.activation(out=rs[pi][:], in_=rs[pi][:], func=AF.Sqrt), reads=[brs[pi]], writes=[brs[pi]])
            P.op("dve", lambda e, pi=pi: e.reciprocal(out=rs[pi][:], in_=rs[pi][:]), reads=[brs[pi]], writes=[brs[pi]])
            for kc in range(KC):
                P.op("dve", lambda e, s=s, kc=kc, pi=pi: e.scalar_tensor_tensor(
                    out=ho[s][:, kc, :], in0=xt[s][:, kc, :], scalar=g[:, kc:kc + 1], in1=rs[pi][:],
                    op0=ALU.mult, op1=ALU.mult), reads=[bx[s][kc // 8], brs[pi], bg], writes=[bh[s][kc]])
            for kg in range(KG):
                k0, k1 = kg * 8, min(KC, kg * 8 + 8)
                P.dma("sp", sh[s], dst[k0:k1, :, t0:t0 + TT].rearrange("k p t -> p k t"), ho[s][:, k0:k1, :],
                      reads=bh[s][k0:k1], writes=[dd])


def st_linear(K, act, KCa, W, units, NT, T, epi_factory, kbase=0):
    P = K.P
    T = min(T, NT)
    NH = T // 512 if T >= 512 else 1
    TW = min(T, 512)
    with P.stage():
        at = P.sbuf("la", [128, KCa, T], BF16)
        KGa = (KCa + 7) // 8
        bat = P.bufs(KGa)
        sat = P.dma_sem("la")
        NS, NR = 4, 8
        stg = [P.sbuf("ls", [128, 8, 256], F32) for _ in range(NS)]
        bst = P.bufs(NS)
        sst = [P.dma_sem(f"ls{i}") for i in range(NS)]
        ring = [P.sbuf("lr", [128, 8, 256], BF16) for _ in range(NR)]
        brg = P.bufs(NR)
        nsets = 8 // (2 * NH)
        pss = [[[P.psum("lp", [128, 512]) for _ in range(NH)] for _ in range(2)] for _ in range(nsets)]
        bpss = [[[P.buf() for _ in range(NH)] for _ in range(2)] for _ in range(nsets)]
        epi = epi_factory(P)
        Wv = W.rearrange("(kc p) n -> p kc n", p=128)
        loads = []
        for ti, t0 in enumerate(range(0, NT, T)):
            for u, un in enumerate(units):
                ng = (un["nk"] + 7) // 8
                for gk in range(ng):
                    loads.append((ti, t0, u, gk, ng))
        cast_rot = ["dve", "act", "pool", "dve", "act"]
        state = {"next": 0}

        def emit_load(i):
            ti, t0, u, gk, ng = loads[i]
            un = units[u]
            st, r = i % NS, i % NR
            kk0 = un["k0"] + gk * 8
            nk = min(8, un["nk"] - gk * 8)
            ncol = un["nc"]
            P.dma("sp", sst[st], stg[st][:, 0:nk, 0:ncol], Wv[:, kk0:kk0 + nk, un["c0"]:un["c0"] + ncol], writes=[bst[st]])
            ce = cast_rot[i % len(cast_rot)]
            if ce == "act":
                P.op("act", lambda e, st=st, r=r, nk=nk, ncol=ncol: e.activation(func=AF.Copy, out=ring[r][:, 0:nk, 0:ncol], in_=stg[st][:, 0:nk, 0:ncol]),
                     reads=[bst[st]], writes=[brg[r]])
            else:
                P.op(ce, lambda e, st=st, r=r, nk=nk, ncol=ncol: e.tensor_copy(out=ring[r][:, 0:nk, 0:ncol], in_=stg[st][:, 0:nk, 0:ncol]),
                     reads=[bst[st]], writes=[brg[r]])

        LA = 5
        uc = 0
        for i, (ti, t0, u, gk, ng) in enumerate(loads):
            un = units[u]
            if u == 0 and gk == 0:
                for kg in range(KGa):
                    k0, k1 = kg * 8, min(KCa, kg * 8 + 8)
                    P.dma("sp", sat, at[:, k0:k1, :], act[kbase + k0:kbase + k1, :, t0:t0 + T].rearrange("k p t -> p k t"), writes=[bat[kg]])
            while state["next"] <= min(i + LA, len(loads) - 1):
                emit_load(state["next"])
                state["next"] += 1
            if gk == 0:
                sset = uc % nsets
                uc += 1
            r = i % NR
            kk0 = un["k0"] + gk * 8
            nk = min(8, un["nk"] - gk * 8)
            nbk = (un["nc"] + 127) // 128
            ps = pss[sset]
            bps = bpss[sset]

            def mm(e, r=r, kk0=kk0, nk=nk, nbk=nbk, ps=ps, gk=gk, ng=ng, un=un):
                ins = None
                for nb in range(nbk):
                    ncol = min(128, un["nc"] - nb * 128)
                    for kc in range(nk):
                        for th in range(NH):
                            ins = e.matmul(ps[nb][th][0:ncol, 0:TW], ring[r][:, kc, nb * 128:nb * 128 + ncol],
                                           at[:, kk0 + kc - kbase, th * TW:(th + 1) * TW],
                                           start=(gk == 0 and kc == 0), stop=(gk == ng - 1 and kc == nk - 1))
                return ins
            kgs = sorted(set((kk0 + j - kbase) // 8 for j in range(nk)))
            P.op("pe", mm, reads=[brg[r]] + [bat[k] for k in kgs],
                 writes=[bps[nb][th] for nb in range(nbk) for th in range(NH)])
            if gk == ng - 1:
                epi(u, un, t0, ps, bps)


class Stager:
    def __init__(self, P, name, dt, n=4, width=512):
        self.P = P
        self.t = [P.sbuf(name, [128, width], dt) for _ in range(n)]
        self.b = P.bufs(n)
        self.s = [P.dma_sem(f"{name}{i}") for i in range(n)]
        self.i = 0
        self.n = n
        self.dd = P.buf()

    def next(self):
        j = self.i % self.n
        self.i += 1
        return j


def epi_store_factory(K, route, TW=512):
    def factory(P):
        sb = Stager(P, "eb", BF16)
        sf = Stager(P, "ef", F32)
        cnt = {"i": 0}

        def epi(u, un, t0, ps, bps):
            nbk = (un["nc"] + 127) // 128
            for nb in range(nbk):
                dst, ch, nrows, dk, func = route(un, nb)
                S_ = sb if dk == "b" else sf
                for th in range(len(ps[nb])):
                    j = S_.next()
                    tw = min(TW, dst.shape[2] - t0)
                    cnt["i"] += 1
                    if func is None and cnt["i"] % 2 == 0:
                        P.op("dve", lambda e, j=j, S_=S_, nb=nb, th=th, nrows=nrows, tw=tw: e.tensor_copy(
                            out=S_.t[j][0:nrows, 0:tw], in_=ps[nb][th][0:nrows, 0:tw]), reads=[bps[nb][th]], writes=[S_.b[j]])
                    else:
                        f = AF.Copy if func is None else func
                        P.op("act", lambda e, j=j, S_=S_, nb=nb, th=th, nrows=nrows, tw=tw, f=f: e.activation(
                            out=S_.t[j][0:nrows, 0:tw], in_=ps[nb][th][0:nrows, 0:tw], func=f), reads=[bps[nb][th]], writes=[S_.b[j]])
                    P.dma("sp", S_.s[j], dst[ch, 0:nrows, t0 + th * tw:t0 + (th + 1) * tw], S_.t[j][0:nrows, 0:tw],
                          reads=[S_.b[j]], writes=[S_.dd])
        return epi
    return factory


def simple_units(N, KC, k0=0):
    return [dict(c0=c, nc=min(256, N - c), k0=k0, nk=KC) for c in range(0, N, 256)]


def st_attention(K, QKV, Y, S, sink_d):
    P = K.P
    ST = min(2048, S)
    scale = 128 ** -0.5
    with P.stage():
        sk = P.sbuf("sk", [128, 4], F32)
        bsk = P.buf()
        P.dma("sp", P.dma_sem("sk"), sk[:], sink_d, writes=[bsk])
        P.op("act", lambda e: e.activation(out=sk[:], in_=sk[:], func=AF.Exp), reads=[bsk], writes=[bsk])
        W = 2 * ST if ST == 2048 else ST + 2048
        NJ = 2
        kbuf = [P.sbuf("ak", [128, W], BF16) for _ in range(NJ)]
        vbuf = [P.sbuf("av", [128, W], BF16) for _ in range(NJ)]
        qbuf = [P.sbuf("aq", [128, 4, ST], BF16) for _ in range(NJ)]
        bk, bv, bq = P.bufs(NJ), P.bufs(NJ), P.bufs(NJ)
        sk_, sv_, sq_ = [P.dma_sem(f"ak{i}") for i in range(NJ)], [P.dma_sem(f"av{i}") for i in range(NJ)], [P.dma_sem(f"aq{i}") for i in range(NJ)]
        vtok = [P.sbuf("avt", [128, W], BF16) for _ in range(NJ)]
        bvt = P.bufs(NJ)
        pst = [P.psum("apt", [128, 512]) for _ in range(2)]
        bpst = P.bufs(2)
        scb = [[P.psum("asc", [128, 512]) for _ in range(2)] for _ in range(2)]
        bsc = [P.bufs(2) for _ in range(2)]
        pn = P.psum("apn", [128, 512])
        pd = P.psum("apd", [128, 512])
        bpn, bpd = P.buf(), P.buf()
        et = [P.sbuf("aet", [128, 2, 4, 128], BF16) for _ in range(2)]
        bet = P.bufs(2)
        dsb = [P.sbuf("ads", [128, 4, 128], F32) for _ in range(2)]
        bds = P.bufs(2)
        ya = [P.sbuf("aya", [128, 4, ST], BF16) for _ in range(1)]
        bya = P.buf()
        sya = P.dma_sem("aya")
        accN = [P.sbuf("aN", [128, ST], F32) for _ in range(2)]
        accD = [P.sbuf("aD", [128, ST], F32) for _ in range(2)]
        bacc = P.bufs(2)
        yb = [P.sbuf("ayb", [128, ST], BF16) for _ in range(2)]
        byb = P.bufs(2)
        syb = [P.dma_sem(f"ayb{i}") for i in range(2)]
        dd_ = P.buf()
        cnt = {"job": 0, "blk": 0, "tr": 0}

        def job(I, dd, qch, kch, vch, mask, mode, acc_i=None, first=False):
            js = cnt["job"] % NJ
            cnt["job"] += 1
            PREV = 128 * dd
            has_prev = I > 0
            Mb = ST // PREV
            nq = len(qch)
            lo = I * ST
            if has_prev:
                P.dma("sp", sk_[js], kbuf[js][:, 0:PREV + ST], QKV[kch, :, lo - PREV:lo + ST], writes=[bk[js]])
                P.dma("sp", sv_[js], vbuf[js][:, 0:PREV + ST], QKV[vch, :, lo - PREV:lo + ST], writes=[bv[js]])
            else:
                P.dma("sp", sk_[js], kbuf[js][:, PREV:PREV + ST], QKV[kch, :, lo:lo + ST], writes=[bk[js]])
                P.dma("sp", sv_[js], vbuf[js][:, PREV:PREV + ST], QKV[vch, :, lo:lo + ST], writes=[bv[js]])
            for h in range(nq):
                P.dma("sp", sq_[js], qbuf[js][:, h, :], QKV[qch[h], :, lo:lo + ST], writes=[bq[js]])
            kv = kbuf[js][:, 0:PREV + ST].rearrange("p (m i r) -> p r m i", i=128, r=dd)
            vv = vbuf[js][:, 0:PREV + ST].rearrange("p (m i r) -> p r m i", i=128, r=dd)
            vt = vtok[js][:, 0:PREV + ST].rearrange("p (r m i) -> p r m i", i=128, r=dd)
            qv = qbuf[js][:, :, :].rearrange("p h (m i r) -> p h r m i", i=128, r=dd)
            lvl = getattr(K, "att_lvl", 9)
            blocks = [(r, m) for r in range(dd) for m in range(0 if has_prev else 1, Mb + 1)]
            if lvl < 2:
                blocks = []
            groups = []
            for (r, m) in blocks:
                if groups and groups[-1][0] == r and groups[-1][1] + len(groups[-1][2]) == m and len(groups[-1][2]) < 4:
                    groups[-1][2].append(m)
                else:
                    groups.append([r, m, [m]])
            vtb = {}
            last_pv = [None]
            for (r, m0, ms) in groups:
                gb = P.buf()
                for m_ in ms:
                    vtb[(r, m_)] = gb
                tb = cnt["tr"] % 2
                cnt["tr"] += 1
                n = len(ms)

                def trs(e, tb=tb, r=r, ms=ms):
                    ins = None
                    for j, m in enumerate(ms):
                        ins = e.matmul(pst[tb][:, j * 128:(j + 1) * 128], vv[:, r, m, :], K.identb, start=True, stop=True)
                    return ins
                P.op("pe", trs, reads=[bv[js], K.bc], writes=[bpst[tb]])
                if tb == 0:
                    P.op("act", lambda e, tb=tb, r=r, m0=m0, n=n: e.activation(out=vt[:, r, m0:m0 + n, :], in_=pst[tb][:, 0:n * 128].rearrange("p (m i) -> p m i", i=128), func=AF.Copy),
                         reads=[bpst[tb], bvt[js]], writes=[gb])
                else:
                    P.op("dve", lambda e, tb=tb, r=r, m0=m0, n=n: e.tensor_copy(out=vt[:, r, m0:m0 + n, :], in_=pst[tb][:, 0:n * 128].rearrange("p (m i) -> p m i", i=128)),
                         reads=[bpst[tb], bvt[js]], writes=[gb])
            for r in range(dd if lvl >= 3 else 0):
                for m in range(1, Mb + 1):
                    bi = cnt["blk"] % 2
                    cnt["blk"] += 1
                    two = has_prev or m > 1
                    slots = [(0, m - 1), (1, m)] if two else [(1, m)]
                    for (sl, kbm) in slots:
                        P.op("pe", lambda e, bi=bi, sl=sl, kbm=kbm, r=r, m=m: e.matmul(
                            scb[bi][sl][:, 0:nq * 128].rearrange("p (h q) -> p h q", h=nq), kv[:, r, kbm, :], qv[:, 0:nq, r, m - 1, :],
                            start=True, stop=True), reads=[bk[js], bq[js]], writes=[bsc[bi][sl]])
                        if lvl < 4:
                            continue
                        P.op("act", lambda e, bi=bi, sl=sl: e.activation(
                            out=et[bi][:, sl, 0:nq, :], in_=scb[bi][sl][:, 0:nq * 128].rearrange("p (h q) -> p h q", h=nq),
                            func=AF.Exp, scale=scale), reads=[bsc[bi][sl]], writes=[bet[bi]])
                        if lvl >= 5:
                            P.op("dve", lambda e, bi=bi, sl=sl: e.tensor_tensor(
                                out=et[bi][:, sl, 0:nq, :], in0=et[bi][:, sl, 0:nq, :],
                                in1=mask[:, sl, :].unsqueeze(1).broadcast_to([128, nq, 128]), op=ALU.mult),
                                reads=[bet[bi], K.bc], writes=[bet[bi]])

                    def pv(e, bi=bi, slots=slots, r=r):
                        ins = None
                        for j, (sl, kbm) in enumerate(slots):
                            e.matmul(pn[:, 0:nq * 128].rearrange("p (h q) -> p h q", h=nq), vt[:, r, kbm, :], et[bi][:, sl, 0:nq, :],
                                     start=(j == 0), stop=(j == len(slots) - 1))
                        for j, (sl, kbm) in enumerate(slots):
                            ins = e.matmul(pd[:, 0:nq * 128].rearrange("p (h q) -> p h q", h=nq), K.onesb, et[bi][:, sl, 0:nq, :],
                                           start=(j == 0), stop=(j == len(slots) - 1))
                        return ins
                    if lvl < 6:
                        continue
                    last_pv[0] = P.op("pe", pv, reads=[bet[bi], K.bc] + [vtb[(r, kbm)] for (_, kbm) in slots], writes=[bpn, bpd])
                    if lvl < 7:
                        continue
                    if mode == "swa":
                        P.op("dve", lambda e, bi=bi: e.tensor_tensor(
                            out=dsb[bi][:, 0:nq, :], in0=pd[:, 0:nq * 128].rearrange("p (h q) -> p h q", h=nq),
                            in1=sk[:, 0:nq].unsqueeze(2).broadcast_to([128, nq, 128]), op=ALU.add), reads=[bpd, bsk], writes=[bds[bi]])
                        P.op("dve", lambda e, bi=bi: e.reciprocal(out=dsb[bi][:, 0:nq, :], in_=dsb[bi][:, 0:nq, :]),
                             reads=[bds[bi]], writes=[bds[bi]])
                        yv = ya[0][:, :, :].rearrange("p h (m i r) -> p h r m i", i=128, r=dd)
                        P.op("dve", lambda e, bi=bi, r=r, m=m, yv=yv: e.tensor_tensor(
                            out=yv[:, :, r, m - 1, :], in0=pn[:, 0:nq * 128].rearrange("p (h q) -> p h q", h=nq),
                            in1=dsb[bi][:, 0:nq, :], op=ALU.mult), reads=[bpn, bds[bi]], writes=[bya])
                    else:
                        aN = accN[acc_i][:, :].rearrange("p (m i r) -> p r m i", i=128, r=dd)[:, r, m - 1, :]
                        aD = accD[acc_i][:, :].rearrange("p (m i r) -> p r m i", i=128, r=dd)[:, r, m - 1, :]
                        if first:
                            P.op("dve", lambda e, aN=aN: e.tensor_copy(out=aN, in_=pn[:, 0:128]), reads=[bpn], writes=[bacc[acc_i]])
                            P.op("dve", lambda e, aD=aD: e.tensor_copy(out=aD, in_=pd[:, 0:128]), reads=[bpd], writes=[bacc[acc_i]])
                        else:
                            P.op("dve", lambda e, aN=aN: e.tensor_tensor(out=aN, in0=pn[:, 0:128], in1=aN, op=ALU.add),
                                 reads=[bpn, bacc[acc_i]], writes=[bacc[acc_i]])
                            P.op("dve", lambda e, aD=aD: e.tensor_tensor(out=aD, in0=pd[:, 0:128], in1=aD, op=ALU.add),
                                 reads=[bpd, bacc[acc_i]], writes=[bacc[acc_i]])
            if last_pv[0] is not None:
                bvt[js].w = last_pv[0]
                bvt[js].r = []

        for I in range(S // ST):
            lo = I * ST
            jobs_on = getattr(K, "att_jobs", None)
            if jobs_on is None or "swa" in jobs_on:
                job(I, 1, [0, 1, 2, 3], 4, 5, K.m_swa, "swa")
                for h in range(4):
                    P.dma("sp", sya, Y[h, :, lo:lo + ST], ya[0][:, h, :], reads=[bya], writes=[dd_])
            dils = [(gi, dil) for gi, dil in enumerate((1, 4, 16)) if jobs_on is None or f"d{dil}" in jobs_on]
            for hs in range(3):
                if not dils:
                    break
                ai = hs % 2
                for gi, dil in dils:
                    base = 6 + 9 * gi
                    job(I, dil, [base + hs], base + 3 + hs, base + 6 + hs, K.m_dil, "dil", acc_i=ai, first=(gi == dils[0][0]))
                P.op("dve", lambda e, ai=ai: e.reciprocal(out=accD[ai][:], in_=accD[ai][:]), reads=[bacc[ai]], writes=[bacc[ai]])
                P.op("dve", lambda e, ai=ai: e.tensor_tensor(out=yb[ai][:], in0=accN[ai][:], in1=accD[ai][:], op=ALU.mult),
                     reads=[bacc[ai]], writes=[byb[ai]])
                P.dma("sp", syb[ai], Y[4 + hs, :, lo:lo + ST], yb[ai][:], reads=[byb[ai]], writes=[dd_])


def st_ssd(K, SSDP, DTr, Y, S, prm):
    P, nc = K.P, K.nc
    XC = dram(K, "ssd_xc", [4, 128, S], F32)
    BCb = dram(K, "ssd_bc", [4, 128, S], BF16)
    DTs = dram(K, "ssd_dt", [8, S], F32)
    DAs = dram(K, "ssd_da", [8, S], F32)
    TS = 512
    with P.stage():
        cw = P.sbuf("cw", [128, 8, 4], F32)
        cbias = P.sbuf("cbs", [128, 8], F32)
        dtb = P.sbuf("dtb", [8, 1], F32)
        an = P.sbuf("an", [8, 1], F32)
        bp_ = P.buf()
        sp_ = P.dma_sem("sprm")
        P.dma("sp", sp_, cw[:], prm["conv_w"], writes=[bp_])
        P.dma("sp", sp_, cbias[:], prm["conv_b"], writes=[bp_])
        P.dma("sp", sp_, dtb[:], prm["dt_bias"], writes=[bp_])
        P.dma("sp", sp_, an[:], prm["a_log"], writes=[bp_])
        P.op("act", lambda e: e.activation(out=an[:], in_=an[:], func=AF.Exp), reads=[bp_], writes=[bp_])
        P.op("dve", lambda e: e.tensor_scalar(out=an[:], in0=an[:], scalar1=-1.0, scalar2=None, op0=ALU.mult), reads=[bp_], writes=[bp_])
        NB_ = 3
        raw = [P.sbuf("craw", [128, TS + 3], F32) for _ in range(NB_)]
        braw = P.bufs(NB_)
        sraw = [P.dma_sem(f"craw{i}") for i in range(NB_)]
        acc = [P.sbuf("cacc", [128, TS], F32) for _ in range(NB_)]
        bacc = P.bufs(NB_)
        of = [P.sbuf("cof", [128, TS], F32) for _ in range(NB_)]
        ob = [P.sbuf("cob", [128, TS], BF16) for _ in range(NB_)]
        bof, bob = P.bufs(NB_), P.bufs(NB_)
        sof = [P.dma_sem(f"cof{i}") for i in range(NB_)]
        sob = [P.dma_sem(f"cob{i}") for i in range(NB_)]
        dtt = [P.sbuf("cdt", [8, TS], F32) for _ in range(2)]
        dat = [P.sbuf("cda", [8, TS], F32) for _ in range(2)]
        bdt, bda = P.bufs(2), P.bufs(2)
        sdt = [P.dma_sem(f"cdt{i}") for i in range(2)]
        sda = [P.dma_sem(f"cda{i}") for i in range(2)]
        dd_ = P.buf()
        it = 0
        for ti, t0 in enumerate(range(0, S, TS)):
            for j in range(8):
                s = it % NB_
                it += 1
                if t0 == 0:
                    P.op("pool", lambda e, s=s: e.memset(raw[s][:, 0:3], 0.0), writes=[braw[s]])
                    P.dma("sp", sraw[s], raw[s][:, 3:3 + TS], SSDP[4 + j, :, 0:TS], writes=[braw[s]])
                else:
                    P.dma("sp", sraw[s], raw[s][:, :], SSDP[4 + j, :, t0 - 3:t0 + TS], writes=[braw[s]])
                P.op("act", lambda e, s=s, j=j: e.activation(out=acc[s][:], in_=raw[s][:, 3:3 + TS], func=AF.Identity,
                                                              scale=cw[:, j, 3:4], bias=cbias[:, j:j + 1]),
                     reads=[braw[s], bp_], writes=[bacc[s]])
                for k in range(3):
                    P.op("dve", lambda e, s=s, j=j, k=k: e.scalar_tensor_tensor(
                        out=acc[s][:], in0=raw[s][:, k:k + TS], scalar=cw[:, j, k:k + 1], in1=acc[s][:],
                        op0=ALU.mult, op1=ALU.add), reads=[braw[s], bacc[s], bp_], writes=[bacc[s]])
                if j < 4:
                    P.op("act", lambda e, s=s: e.activation(out=of[s][:], in_=acc[s][:], func=AF.Silu), reads=[bacc[s]], writes=[bof[s]])
                    P.dma("sp", sof[s], XC[j, :, t0:t0 + TS], of[s][:], reads=[bof[s]], writes=[dd_])
                else:
                    P.op("act", lambda e, s=s: e.activation(out=ob[s][:], in_=acc[s][:], func=AF.Silu), reads=[bacc[s]], writes=[bob[s]])
                    P.dma("sp", sob[s], BCb[j - 4, :, t0:t0 + TS], ob[s][:], reads=[bob[s]], writes=[dd_])
            d = ti % 2
            P.dma("sp", sdt[d], dtt[d][:], DTr[:, t0:t0 + TS], reads=[], writes=[bdt[d]])
            P.op("act", lambda e, d=d: e.activation(out=dtt[d][:], in_=dtt[d][:], func=AF.Exp, bias=dtb[:, 0:1]), reads=[bdt[d], bp_], writes=[bdt[d]])
            P.op("act", lambda e, d=d: e.activation(out=dtt[d][:], in_=dtt[d][:], func=AF.Ln, bias=1.0), reads=[bdt[d]], writes=[bdt[d]])
            P.op("dve", lambda e, d=d: e.tensor_scalar(out=dat[d][:], in0=dtt[d][:], scalar1=an[:, 0:1], scalar2=None, op0=ALU.mult),
                 reads=[bdt[d], bp_], writes=[bda[d]])
            P.dma("sp", sdt[d], DTs[:, t0:t0 + TS], dtt[d][:], reads=[bdt[d]], writes=[dd_])
            P.dma("sp", sda[d], DAs[:, t0:t0 + TS], dat[d][:], reads=[bda[d]], writes=[dd_])

    if getattr(K, 'ssd_lvl', 9) < 2:
        return
    with P.stage():
        dsk = P.sbuf("dsk", [128, 4], F32)
        nw = P.sbuf("nw", [128, 4], F32)
        bp_ = P.buf()
        sp_ = P.dma_sem("sprm2")
        P.dma("sp", sp_, dsk[:], prm["d_skip"], writes=[bp_])
        P.dma("sp", sp_, nw[:], prm["ssm_norm"], writes=[bp_])
        NL = 2
        xc = [P.sbuf("sxc", [128, 4, TS], F32) for _ in range(NL)]
        zt = [P.sbuf("sz", [128, 4, TS], F32) for _ in range(NL)]
        bcb = [P.sbuf("sbc", [128, 4, TS], BF16) for _ in range(NL)]
        dtda = [P.sbuf("sdtda", [16, TS], F32) for _ in range(NL)]
        bxc, bz, bbc, bdd = P.bufs(NL), P.bufs(NL), P.bufs(NL), P.bufs(NL)
        sxc = [P.dma_sem(f"sxc{i}") for i in range(NL)]
        sz = [P.dma_sem(f"sz{i}") for i in range(NL)]
        sbc = [P.dma_sem(f"sbc{i}") for i in range(NL)]
        sdd = [P.dma_sem(f"sdd{i}") for i in range(NL)]
        yt = [P.sbuf("syt", [128, 4, TS], BF16) for _ in range(NL)]
        byt = P.bufs(NL)
        syt = [P.dma_sem(f"syt{i}") for i in range(NL)]
        H = P.sbuf("sH", [128, 8, 64], F32)
        Hb = P.sbuf("sHb", [128, 8, 128], BF16)
        bH, bHb = P.buf(), P.buf()
        P.op("pool", lambda e: e.memset(H[:], 0.0), writes=[bH])
        P.op("pool", lambda e: e.memset(Hb[:], 0.0), writes=[bHb])
        Xp = [P.sbuf("sXp", [128, 8, 128], BF16) for _ in range(2)]
        Xpp = [P.sbuf("sXpp", [128, 8, 64], BF16) for _ in range(2)]
        bXp, bXpp = P.bufs(2), P.bufs(2)
        for i in range(2):
            P.op("pool", lambda e, i=i: e.memset(Xp[i][:], 0.0), writes=[bXp[i]])
        p_small = P.psum("ssm", [128, 512])
        b_small = P.buf()
        p_xT = P.psum("sxT", [128, 512])
        b_xT1 = P.buf()
        b_xT = [b_xT1] * 4
        p_bt = P.psum("sbt", [128, 512])
        b_bt1 = P.buf()
        b_bt = [b_bt1] * 2
        p_gs = P.psum("sgs", [128, 512])
        p_gt = p_gs[:, 0:256]
        p_ss = p_gs[:, 256:512]
        b_gs1 = P.buf()
        b_gt = [b_gs1] * 2
        p_ar = [P.psum("sar", [128, 512]) for _ in range(2)]
        b_ar2 = P.bufs(2)
        b_ar = [b_ar2[h // 4] for h in range(8)]
        p_y = P.psum("sy", [128, 512])
        b_y1 = P.buf()
        b_y = [b_y1] * 4
        p_S = P.psum("sS", [128, 512])
        b_S = P.buf()
        b_ss = [b_gs1] * 2
        dtT = P.sbuf("sdtT", [128, 16], F32)
        acT = P.sbuf("sacT", [128, 8], F32)
        nacT = P.sbuf("snacT", [128, 8], F32)
        dsT = P.sbuf("sdsT", [128, 8], F32)
        w2 = P.sbuf("sw2", [128, 8], F32)
        cdb = P.sbuf("scdb", [128, 8], F32)
        acF = P.sbuf("sacF", [8, 128], F32)
        alT = P.sbuf("salT", [128, 8], F32)
        b_alT = P.buf()
        arsb = [P.sbuf("sarsb", [128, 512], F32) for _ in range(2)]
        b_arsb = P.bufs(2)
        b_dtT, b_acT, b_nacT, b_dsT, b_w2, b_cdb, b_acF = P.bufs(7)
        Btok = P.sbuf("sBtok", [128, 2, 128], BF16)
        b_Btok = P.bufs(2)
        tmp4 = [P.sbuf("stmp", [128, 512], F32) for _ in range(2)]
        Ld4 = [P.sbuf("sLd", [128, 512], F32) for _ in range(2)]
        Er4 = [P.sbuf("sEr", [128, 512], F32) for _ in range(2)]
        b_tmp4, b_Ld4, b_Er4 = P.bufs(2), P.bufs(2), P.bufs(2)
        Mh = P.sbuf("sMh", [128, 8, 128], BF16)
        Cp = P.sbuf("sCp", [128, 8, 128], BF16)
        b_Mh, b_Cp = P.bufs(8), P.bufs(8)
        t1 = P.sbuf("st1", [128, 4, 128], F32)
        szs = P.sbuf("sszs", [128, 4, 128], F32)
        yg = P.sbuf("syg", [128, 4, 128], F32)
        sqy = P.sbuf("ssqy", [128, 4, 128], F32)
        b_t1, b_szs, b_yg, b_sqy = P.bufs(4), P.bufs(4), P.bufs(4), P.bufs(4)
        rstd = P.sbuf("srstd", [128, 2, 128], F32)
        b_rstd = P.bufs(2)
        dd_ = P.buf()
        hcnt = 0
        for ti, t0 in enumerate(range(0, S, TS)):
            L = ti % NL
            for c in range(4):
                P.dma("sp", sxc[L], xc[L][:, c, :], XC[c, :, t0:t0 + TS], writes=[bxc[L]])
                P.dma("sp", sz[L], zt[L][:, c, :], SSDP[c, :, t0:t0 + TS], writes=[bz[L]])
                P.dma("sp", sbc[L], bcb[L][:, c, :], BCb[c, :, t0:t0 + TS], writes=[bbc[L]])
            P.dma("sp", sdd[L], dtda[L][0:8, :], DTs[:, t0:t0 + TS], writes=[bdd[L]])
            P.dma("sp", sdd[L], dtda[L][8:16, :], DAs[:, t0:t0 + TS], writes=[bdd[L]])
            for zi in range(TS // 128):
                sl = slice(zi * 128, (zi + 1) * 128)
                par = (ti * 4 + zi) % 2
                P.op("pe", lambda e, L=L, sl=sl: e.matmul(p_small[:, 256:272], dtda[L][:, sl], K.ident32[0:16, 0:16], start=True, stop=True),
                     reads=[bdd[L], K.bc], writes=[b_small])
                P.op("dve", lambda e: e.tensor_copy(out=dtT[:], in_=p_small[:, 256:272]), reads=[b_small], writes=[b_dtT])

                def cum(e):
                    e.matmul(p_small[:, 0:8], K.tri, dtT[:, 8:16], start=True, stop=True)
                    e.matmul(p_small[:, 8:16], K.ones32, dtT[:, 8:16], start=True, stop=True)
                    return e.matmul(p_small[0:8, 128:256], dtT[:, 8:16], K.tri, start=True, stop=True)
                P.op("pe", cum, reads=[b_dtT, K.bc], writes=[b_small])
                P.op("dve", lambda e: e.tensor_copy(out=acT[:], in_=p_small[:, 0:8]), reads=[b_small], writes=[b_acT])
                P.op("dve", lambda e: e.tensor_scalar(out=nacT[:], in0=p_small[:, 0:8], scalar1=-1.0, scalar2=None, op0=ALU.mult),
                     reads=[b_small], writes=[b_nacT])
                P.op("dve", lambda e: e.tensor_copy(out=alT[:], in_=p_small[:, 8:16]), reads=[b_small], writes=[b_alT])
                P.op("dve", lambda e: e.tensor_copy(out=acF[:], in_=p_small[0:8, 128:256]), reads=[b_small], writes=[b_acF])
                P.op("dve", lambda e: e.tensor_tensor(out=dsT[:], in0=alT[:], in1=acT[:], op=ALU.subtract),
                     reads=[b_alT, b_acT], writes=[b_dsT])
                P.op("act", lambda e: e.activation(out=dsT[:], in_=dsT[:], func=AF.Exp), reads=[b_dsT], writes=[b_dsT])
                P.op("act", lambda e: e.activation(out=cdb[:], in_=alT[:], func=AF.Exp), reads=[b_alT], writes=[b_cdb])
                P.op("dve", lambda e: e.tensor_tensor(out=w2[:], in0=dtT[:, 0:8], in1=dsT[:], op=ALU.mult),
                     reads=[b_dtT, b_dsT], writes=[b_w2])
                if getattr(K, 'ssd_lvl', 9) < 3:
                    continue
                def xtr(e, L=L, sl=sl):
                    ins = None
                    for c in range(4):
                        ins = e.matmul(p_xT[:, c * 128:(c + 1) * 128], xc[L][:, c, sl], K.ident32, start=True, stop=True)
                    return ins
                P.op("pe", xtr, reads=[bxc[L], K.bc], writes=[b_xT1])
                xT4 = p_xT[:, 0:512].rearrange("p (c two q) -> p c two q", two=2, q=64)
                Xpv = Xp[par][:].rearrange("p (c two) q -> p c two q", two=2)
                dt4 = dtT[:, 0:8].rearrange("p (c two) -> p c two", two=2)
                for h2 in range(2):
                    P.op("dve", lambda e, h2=h2, xT4=xT4, Xpv=Xpv, dt4=dt4: e.tensor_tensor(
                        out=Xpv[:, :, h2, h2 * 64:(h2 + 1) * 64], in0=xT4[:, :, h2, :],
                        in1=dt4[:, :, h2].unsqueeze(2).broadcast_to([128, 4, 64]), op=ALU.mult),
                        reads=[b_xT1, b_dtT], writes=[bXp[par]])
                P.op("dve", lambda e, par=par: e.tensor_tensor(
                    out=Xpp[par][:], in0=p_xT[:, 0:512].rearrange("p (h q) -> p h q", q=64),
                    in1=w2[:, 0:8].unsqueeze(2).broadcast_to([128, 8, 64]), op=ALU.mult),
                    reads=[b_xT1, b_w2], writes=[bXpp[par]])
                if getattr(K, 'ssd_lvl', 9) < 4:
                    continue
                def btr(e, L=L, sl=sl):
                    e.matmul(p_bt[:, 0:128], bcb[L][:, 0, sl], K.identb, start=True, stop=True)
                    return e.matmul(p_bt[:, 128:256], bcb[L][:, 1, sl], K.identb, start=True, stop=True)
                P.op("pe", btr, reads=[bbc[L], K.bc], writes=[b_bt1])
                P.op("act", lambda e: e.activation(out=Btok[:, :, :], in_=p_bt[:, 0:256].rearrange("p (g n) -> p g n", g=2), func=AF.Copy),
                     reads=[b_bt1], writes=[b_Btok[0], b_Btok[1]])

                def gtm(e, L=L, sl=sl):
                    e.matmul(p_gt[:, 0:128], bcb[L][:, 0, sl], bcb[L][:, 2, sl], start=True, stop=True)
                    return e.matmul(p_gt[:, 128:256], bcb[L][:, 1, sl], bcb[L][:, 3, sl], start=True, stop=True)
                P.op("pe", gtm, reads=[bbc[L]], writes=[b_gs1])
                if getattr(K, 'ssd_lvl', 9) < 5:
                    continue
                for bk in range(2):
                    gl = bk

                    def arm(e, h0=4 * bk):
                        ins = None
                        for hh in range(h0, h0 + 4):
                            ins = e.matmul(p_ar[hh // 4][:, (hh % 4) * 128:(hh % 4 + 1) * 128], K.sel[0:8, hh, :], acF[:], start=True, stop=True)
                        return ins
                    P.op("pe", arm, reads=[b_acF, K.bc], writes=[b_ar2[bk]])
                    P.op("dve", lambda e, bk=bk: e.tensor_copy(out=arsb[bk][:], in_=p_ar[bk][:]), reads=[b_ar2[bk]], writes=[b_arsb[bk]])
                    a4 = arsb[bk][:].rearrange("p (h l) -> p h l", h=4)
                    t4 = tmp4[bk][:].rearrange("p (h l) -> p h l", h=4)
                    l4 = Ld4[bk][:].rearrange("p (h l) -> p h l", h=4)
                    e4 = Er4[bk][:].rearrange("p (h l) -> p h l", h=4)
                    P.op("pool", lambda e, a4=a4, t4=t4: e.tensor_tensor(out=t4, in0=a4, in1=K.maskneg.unsqueeze(1).broadcast_to([128, 4, 128]), op=ALU.add),
                         reads=[b_arsb[bk], K.bc], writes=[b_tmp4[bk]])
                    P.op("pool", lambda e, t4=t4, bk=bk: e.tensor_tensor(out=t4, in0=t4, in1=nacT[:, 4 * bk:4 * bk + 4].unsqueeze(2).broadcast_to([128, 4, 128]), op=ALU.add),
                         reads=[b_tmp4[bk], b_nacT], writes=[b_tmp4[bk]])
                    P.op("act", lambda e, bk=bk: e.activation(out=Ld4[bk][:], in_=tmp4[bk][:], func=AF.Exp), reads=[b_tmp4[bk]], writes=[b_Ld4[bk]])
                    P.op("act", lambda e, bk=bk: e.activation(out=Er4[bk][:], in_=arsb[bk][:], func=AF.Exp), reads=[b_arsb[bk]], writes=[b_Er4[bk]])
                    P.op("dve", lambda e, bk=bk, gl=gl, l4=l4: e.tensor_tensor(
                        out=Mh[:, 4 * bk:4 * bk + 4, :], in0=p_gt[:, gl * 128:(gl + 1) * 128].unsqueeze(1).broadcast_to([128, 4, 128]), in1=l4, op=ALU.mult),
                        reads=[b_gs1, b_Ld4[bk]], writes=b_Mh[4 * bk:4 * bk + 4])
                    P.op("pool", lambda e, bk=bk, gl=gl, e4=e4, L=L, sl=sl: e.tensor_tensor(
                        out=Cp[:, 4 * bk:4 * bk + 4, :], in0=bcb[L][:, 2 + gl, sl].unsqueeze(1).broadcast_to([128, 4, 128]), in1=e4, op=ALU.mult),
                        reads=[bbc[L], b_Er4[bk]], writes=b_Cp[4 * bk:4 * bk + 4])
                if getattr(K, 'ssd_lvl', 9) < 6:
                    continue
                def ymm(e, par=par):
                    ins = None
                    for c in range(4):
                        for h2 in range(2):
                            h = 2 * c + h2
                            e.matmul(p_y[:, c * 128:(c + 1) * 128], Xp[par][:, h, :], Mh[:, h, :], start=(h2 == 0), stop=False)
                            ins = e.matmul(p_y[:, c * 128:(c + 1) * 128], Hb[:, h, :], Cp[:, h, :], start=False, stop=(h2 == 1))
                    return ins
                P.op("pe", ymm, reads=[bXp[par], bHb] + b_Mh + b_Cp, writes=[b_y1])
                if getattr(K, 'ssd_lvl', 9) < 7:
                    continue
                def smm(e, par=par):
                    ins = None
                    for h in range(8):
                        ins = e.matmul(p_S[:, h * 64:(h + 1) * 64], Btok[:, h // 4, :], Xpp[par][:, h, :], start=True, stop=True)
                    return ins
                P.op("pe", smm, reads=[b_Btok[0], b_Btok[1], bXpp[par]], writes=[b_S])
                P.op("pool", lambda e: e.tensor_tensor(out=H[:], in0=H[:], in1=cdb[:].unsqueeze(2).broadcast_to([128, 8, 64]), op=ALU.mult),
                     reads=[bH, b_cdb], writes=[bH])
                P.op("dve", lambda e: e.tensor_tensor(out=H[:], in0=p_S[:].rearrange("p (h q) -> p h q", h=8), in1=H[:], op=ALU.add),
                     reads=[bH, b_S], writes=[bH])
                Hv = H[:].rearrange("p (c two) q -> p c two q", two=2)
                Hbv = Hb[:].rearrange("p (c two) q -> p c two q", two=2)
                P.op("act", lambda e, Hv=Hv, Hbv=Hbv: e.activation(func=AF.Copy, out=Hbv[:, :, 0, 0:64], in_=Hv[:, :, 0, :]), reads=[bH], writes=[bHb])
                P.op("act", lambda e, Hv=Hv, Hbv=Hbv: e.activation(func=AF.Copy, out=Hbv[:, :, 1, 64:128], in_=Hv[:, :, 1, :]), reads=[bH], writes=[bHb])
                if getattr(K, 'ssd_lvl', 9) < 8:
                    continue
                P.op("pool", lambda e, L=L, sl=sl: e.tensor_tensor(out=t1[:], in0=xc[L][:, :, sl], in1=dsk[:, 0:4].unsqueeze(2).broadcast_to([128, 4, 128]), op=ALU.mult),
                     reads=[bxc[L], bp_], writes=[b_t1[0]])
                P.op("dve", lambda e: e.tensor_tensor(out=t1[:], in0=p_y[:].rearrange("p (c l) -> p c l", c=4), in1=t1[:], op=ALU.add),
                     reads=[b_t1[0], b_y1], writes=[b_t1[0]])
                P.op("act", lambda e, L=L, sl=sl: e.activation(out=szs[:], in_=zt[L][:, :, sl], func=AF.Silu), reads=[bz[L]], writes=[b_szs[0]])
                P.op("pool", lambda e: e.tensor_tensor(out=yg[:], in0=t1[:], in1=szs[:], op=ALU.mult), reads=[b_t1[0], b_szs[0]], writes=[b_yg[0]])
                P.op("act", lambda e: e.activation(out=sqy[:], in_=yg[:], func=AF.Square), reads=[b_yg[0]], writes=[b_sqy[0]])

                def ssm(e):
                    ins = None
                    for gl in range(2):
                        e.matmul(p_ss[:, gl * 128:(gl + 1) * 128], K.ones32, sqy[:, 2 * gl, :], start=True, stop=False)
                        ins = e.matmul(p_ss[:, gl * 128:(gl + 1) * 128], K.ones32, sqy[:, 2 * gl + 1, :], start=False, stop=True)
                    return ins
                P.op("pe", ssm, reads=[b_sqy[0], K.bc], writes=[b_gs1])
                P.op("dve", lambda e: e.tensor_scalar(out=rstd[:], in0=p_ss.rearrange("p (g l) -> p g l", g=2), scalar1=1.0 / 256, scalar2=EPS,
                                                      op0=ALU.mult, op1=ALU.add), reads=[b_gs1], writes=[b_rstd[0]])
                P.op("act", lambda e: e.activation(out=rstd[:], in_=rstd[:], func=AF.Sqrt), reads=[b_rstd[0]], writes=[b_rstd[0]])
                P.op("dve", lambda e: e.reciprocal(out=rstd[:], in_=rstd[:]), reads=[b_rstd[0]], writes=[b_rstd[0]])
                P.op("pool", lambda e: e.tensor_tensor(out=yg[:], in0=yg[:], in1=nw[:, 0:4].unsqueeze(2).broadcast_to([128, 4, 128]), op=ALU.mult),
                     reads=[b_yg[0], bp_], writes=[b_yg[0]])
                P.op("dve", lambda e, L=L, sl=sl: e.tensor_tensor(
                    out=yt[L][:, :, sl].rearrange("p (g c) l -> p g c l", g=2), in0=yg[:].rearrange("p (g c) l -> p g c l", g=2),
                    in1=rstd[:].unsqueeze(2).broadcast_to([128, 2, 2, 128]), op=ALU.mult), reads=[b_yg[0], b_rstd[0]], writes=[byt[L]])
            for c in range(4):
                if getattr(K, 'ssd_lvl', 9) < 8:
                    break
                P.dma("sp", syt[L], Y[7 + c, :, t0:t0 + TS], yt[L][:, c, :], reads=[byt[L]], writes=[dd_])


def build_A(cfg, upto=4):
    nc = bass.Bass("TRN2", target_bir_lowering=False)
    K = Kx()
    K.nc, K.P = nc, Prog(nc)
    D, S, KC = cfg.D, cfg.S, cfg.KC
    X = nc.dram_tensor("x", [KC, 128, S], F32, kind="ExternalInput").ap()
    WA = nc.dram_tensor("wa", [D, cfg.NA], F32, kind="ExternalInput").ap()
    gmix = nc.dram_tensor("gmix", [128, KC], F32, kind="ExternalInput").ap()
    sink = nc.dram_tensor("sink", [128, 4], F32, kind="ExternalInput").ap()
    prm = dict(
        conv_w=nc.dram_tensor("conv_w", [128, 8, 4], F32, kind="ExternalInput").ap(),
        conv_b=nc.dram_tensor("conv_b", [128, 8], F32, kind="ExternalInput").ap(),
        dt_bias=nc.dram_tensor("dt_bias", [8, 1], F32, kind="ExternalInput").ap(),
        a_log=nc.dram_tensor("a_log", [8, 1], F32, kind="ExternalInput").ap(),
        d_skip=nc.dram_tensor("d_skip", [128, 4], F32, kind="ExternalInput").ap(),
        ssm_norm=nc.dram_tensor("ssm_norm", [128, 4], F32, kind="ExternalInput").ap(),
    )
    Y = nc.dram_tensor("y", [11, 128, S], BF16, kind="ExternalOutput").ap()
    load_consts(K)
    Hd = dram(K, "a_h", [KC, 128, S], BF16)
    QKV = dram(K, "a_qkv", [33, 128, S], BF16)
    SSDP = dram(K, "a_ssdp", [12, 128, S], F32)
    DTr = dram(K, "a_dt", [1, 8, S], F32)
    st_rmsnorm(K, X, Hd, gmix, KC, S, D)

    def route(un, nb):
        ch = (un["c0"] + nb * 128) // 128
        if ch < 33:
            return QKV, ch, 128, "b", None
        if ch < 45:
            return SSDP, ch - 33, 128, "f", None
        return DTr, 0, 8, "f", None
    if upto >= 2:
        st_linear(K, Hd, KC, WA, simple_units(cfg.NA, KC), S, 1024, epi_store_factory(K, route))
    if upto >= 3:
        st_attention(K, QKV, Y, S, sink)
    if upto >= 4:
        st_ssd(K, SSDP, DTr[0], Y, S, prm)
    K.P.emit()
    return nc


def pack_A_inputs(cfg, l, g, xT_b, inp, cf, cb):
    D = cfg.D
    w_in = inp["w_in"][l]
    cols = []
    A_Q, A_KV = 2048, 512
    cols += list(range(g * 512, g * 512 + 512))
    cols += list(range(A_Q + g * 128, A_Q + g * 128 + 128))
    cols += list(range(A_Q + A_KV + g * 128, A_Q + A_KV + g * 128 + 128))
    B0 = 3072
    for gi in range(3):
        for part in range(3):
            o = B0 + gi * 3 * 1536 + part * 1536 + g * 384
            cols += list(range(o, o + 384))
    C0 = 3072 + 13824
    cols += list(range(C0 + g * 512, C0 + g * 512 + 512))
    XB = C0 + 2048
    cols += list(range(XB + g * 512, XB + g * 512 + 512))
    cols += list(range(XB + 2048 + g * 256, XB + 2048 + g * 256 + 256))
    cols += list(range(XB + 2048 + 1024 + g * 256, XB + 2048 + 1024 + g * 256 + 256))
    DT0 = C0 + 2048 + 4096
    cols += list(range(DT0 + g * 8, DT0 + g * 8 + 8))
    cols = np.asarray(cols)
    assert cols.size == cfg.NA
    wa = np.ascontiguousarray(w_in[:, cols])
    conv_ch = np.concatenate([np.arange(g * 512, g * 512 + 512), 2048 + np.arange(g * 256, g * 256 + 256),
                              2048 + 1024 + np.arange(g * 256, g * 256 + 256)])
    cw = inp["conv_w"][l][:, conv_ch]
    cw = np.ascontiguousarray(cw.T.reshape(8, 128, 4).transpose(1, 0, 2))
    cbias = np.ascontiguousarray(inp["conv_b"][l][conv_ch].reshape(8, 128).T)
    heads = np.arange(g * 8, g * 8 + 8)
    dsk = np.repeat(inp["d_skip"][l][heads], 64).reshape(4, 128).T
    return {
        "x": xT_b, "wa": wa,
        "gmix": np.ascontiguousarray(inp["norm_mix"][l].reshape(cfg.KC, 128).T),
        "sink": np.ascontiguousarray(np.broadcast_to(inp["attn_sink"][l][g * 4:g * 4 + 4][None, :], (128, 4))),
        "conv_w": cw, "conv_b": cbias,
        "dt_bias": np.ascontiguousarray(inp["dt_bias"][l][heads].reshape(8, 1)),
        "a_log": np.ascontiguousarray(inp["a_log"][l][heads].reshape(8, 1)),
        "d_skip": np.ascontiguousarray(dsk),
        "ssm_norm": np.ascontiguousarray(inp["ssm_norm"][l][g * 512:g * 512 + 512].reshape(4, 128).T),
        "cst_f": cf, "cst_b": cb,
    }


def st_xattn(K, Qd, KTd, VFd, Od, NT):
    P = K.P
    scale = 128 ** -0.5
    with P.stage():
        kt = P.sbuf("xk", [128, 4, 256], BF16)
        vf = P.sbuf("xv", [128, 4, 256], BF16)
        bkv = P.buf()
        s_ = P.dma_sem("xkv")
        for h in range(4):
            P.dma("sp", s_, kt[:, h, :], KTd[h, :, :], writes=[bkv])
            P.dma("sp", s_, vf[:, h, :], VFd[h, :, :], writes=[bkv])
        vtok = P.sbuf("xvt", [128, 2, 4, 128], BF16)
        bvt = P.buf()
        pt = [P.psum("xpt", [128, 512]) for _ in range(2)]
        bpt = P.bufs(2)
        for h in range(4):
            tb = h % 2

            def xtr(e, h=h, tb=tb):
                e.matmul(pt[tb][:, 0:128], vf[:, h, 0:128], K.identb, start=True, stop=True)
                return e.matmul(pt[tb][:, 128:256], vf[:, h, 128:256], K.identb, start=True, stop=True)
            P.op("pe", xtr, reads=[bkv, K.bc], writes=[bpt[tb]])
            P.op("act", lambda e, h=h, tb=tb: e.activation(out=vtok[:, :, h, :], in_=pt[tb][:, 0:256].rearrange("p (m d) -> p m d", m=2), func=AF.Copy),
                 reads=[bpt[tb]], writes=[bvt])
        TQ = 512
        q = [P.sbuf("xq", [128, 4, TQ], BF16) for _ in range(2)]
        bq = P.bufs(2)
        sq = [P.dma_sem(f"xq{i}") for i in range(2)]
        sc = [[P.psum("xsc", [128, 512]) for _ in range(2)] for _ in range(2)]
        bsc = [P.bufs(2) for _ in range(2)]
        pn, pd = P.psum("xpn", [128, 512]), P.psum("xpd", [128, 512])
        bpn, bpd = P.buf(), P.buf()
        et = [P.sbuf("xet", [128, 2, TQ], BF16) for _ in range(2)]
        bet = P.bufs(2)
        rec = [P.sbuf("xrec", [128, TQ], F32) for _ in range(2)]
        brec = P.bufs(2)
        ot = [P.sbuf("xo", [128, TQ], BF16) for _ in range(2)]
        bot = P.bufs(2)
        so = [P.dma_sem(f"xo{i}") for i in range(2)]
        dd_ = P.buf()
        it = 0
        for ti, t0 in enumerate(range(0, NT, TQ)):
            s = ti % 2
            for h in range(4):
                P.dma("sp", sq[s], q[s][:, h, :], Qd[h, :, t0:t0 + TQ], writes=[bq[s]])
            for h in range(4):
                b = it % 2
                it += 1
                for mc in range(2):
                    P.op("pe", lambda e, b=b, mc=mc, h=h, s=s: e.matmul(sc[b][mc][:], kt[:, h, mc * 128:(mc + 1) * 128], q[s][:, h, :], start=True, stop=True),
                         reads=[bkv, bq[s]], writes=[bsc[b][mc]])
                    P.op("act", lambda e, b=b, mc=mc: e.activation(out=et[b][:, mc, :], in_=sc[b][mc][:], func=AF.Exp, scale=scale),
                         reads=[bsc[b][mc]], writes=[bet[b]])

                def pv(e, b=b, h=h):
                    e.matmul(pn[:], vtok[:, 0, h, :], et[b][:, 0, :], start=True, stop=False)
                    e.matmul(pn[:], vtok[:, 1, h, :], et[b][:, 1, :], start=False, stop=True)
                    e.matmul(pd[:], K.onesb, et[b][:, 0, :], start=True, stop=False)
                    return e.matmul(pd[:], K.onesb, et[b][:, 1, :], start=False, stop=True)
                P.op("pe", pv, reads=[bet[b], bvt, K.bc], writes=[bpn, bpd])
                P.op("dve", lambda e, b=b: e.reciprocal(out=rec[b][:], in_=pd[:]), reads=[bpd], writes=[brec[b]])
                P.op("dve", lambda e, b=b: e.tensor_tensor(out=ot[b][:], in0=pn[:], in1=rec[b][:], op=ALU.mult),
                     reads=[bpn, brec[b]], writes=[bot[b]])
                P.dma("sp", so[b], Od[h, :, t0:t0 + TQ], ot[b][:], reads=[bot[b]], writes=[dd_])


def epi_residual_factory(K, Xin, Xout):
    def factory(P):
        xi = [P.sbuf("eri", [128, 512], F32) for _ in range(4)]
        bxi = P.bufs(4)
        sxi = [P.dma_sem(f"eri{i}") for i in range(4)]
        so = Stager(P, "ero", F32)
        c = {"i": 0}

        def epi(u, un, t0, ps, bps):
            nbk = (un["nc"] + 127) // 128
            for nb in range(nbk):
                ch = (un["c0"] + nb * 128) // 128
                for th in range(len(ps[nb])):
                    j = c["i"] % 4
                    c["i"] += 1
                    tw = min(512, Xin.shape[2] - t0)
                    P.dma("sp", sxi[j], xi[j][:, 0:tw], Xin[ch, :, t0 + th * tw:t0 + (th + 1) * tw], writes=[bxi[j]])
                    k = so.next()
                    P.op("dve", lambda e, j=j, k=k, nb=nb, th=th, tw=tw: e.tensor_tensor(out=so.t[k][:, 0:tw], in0=ps[nb][th][:, 0:tw], in1=xi[j][:, 0:tw], op=ALU.add),
                         reads=[bps[nb][th], bxi[j]], writes=[so.b[k]])
                    P.dma("sp", so.s[k], Xout[ch, :, t0 + th * tw:t0 + (th + 1) * tw], so.t[k][:, 0:tw], reads=[so.b[k]], writes=[so.dd])
        return epi
    return factory


def epi_swiglu_factory(K, ACTd):
    def factory(P):
        sg = [P.sbuf("esg", [128, 512], F32) for _ in range(4)]
        bsg = P.bufs(4)
        so = Stager(P, "eso", BF16)
        c = {"i": 0}

        def epi(u, un, t0, ps, bps):
            jb = un["c0"] // 256
            for th in range(len(ps[0])):
                j = c["i"] % 4
                c["i"] += 1
                tw = min(512, ACTd.shape[2] - t0)
                P.op("act", lambda e, j=j, th=th, tw=tw: e.activation(out=sg[j][:, 0:tw], in_=ps[0][th][:, 0:tw], func=AF.Silu),
                     reads=[bps[0][th]], writes=[bsg[j]])
                k = so.next()
                P.op("dve", lambda e, j=j, k=k, th=th, tw=tw: e.tensor_tensor(out=so.t[k][:, 0:tw], in0=ps[1][th][:, 0:tw], in1=sg[j][:, 0:tw], op=ALU.mult),
                     reads=[bps[1][th], bsg[j]], writes=[so.b[k]])
                P.dma("sp", so.s[k], ACTd[jb, :, t0 + th * tw:t0 + (th + 1) * tw], so.t[k][:, 0:tw], reads=[so.b[k]], writes=[so.dd])
        return epi
    return factory


def epi_gated_factory(K, Gd, Md, KC):
    def factory(P):
        gt = [P.sbuf("egg", [128, 512], F32) for _ in range(4)]
        bgt = P.bufs(4)
        sgt = [P.dma_sem(f"egg{i}") for i in range(4)]
        acc = [P.sbuf("ega", [128, 2, 2, 512], F32) for _ in range(2)]
        bacc = [[P.bufs(2) for _ in range(2)] for _ in range(2)]
        so = Stager(P, "ego", BF16)
        c = {"i": 0, "u": 0}

        def epi(u, un, t0, ps, bps):
            br = un["br"]
            if br == 0:
                c["u"] += 1
            a = c["u"] % 2
            nbk = (un["nc"] + 127) // 128
            for nb in range(nbk):
                ch = (un["c0"] + nb * 128) // 128
                for th in range(len(ps[nb])):
                    j = c["i"] % 4
                    c["i"] += 1
                    tw = min(512, Gd.shape[2] - t0)
                    tt = t0 + th * tw
                    P.dma("sp", sgt[j], gt[j][:, 0:tw], Gd[br * KC + ch, :, tt:tt + tw], writes=[bgt[j]])
                    if br == 0:
                        P.op("dve", lambda e, a=a, nb=nb, th=th, j=j, tw=tw: e.tensor_tensor(out=acc[a][:, nb, th, 0:tw], in0=ps[nb][th][:, 0:tw], in1=gt[j][:, 0:tw], op=ALU.mult),
                             reads=[bps[nb][th], bgt[j]], writes=[bacc[a][nb][th]])
                    else:
                        P.op("dve", lambda e, nb=nb, th=th, j=j, tw=tw: e.tensor_tensor(out=gt[j][:, 0:tw], in0=ps[nb][th][:, 0:tw], in1=gt[j][:, 0:tw], op=ALU.mult),
                             reads=[bps[nb][th], bgt[j]], writes=[bgt[j]])
                        if br == 1:
                            P.op("pool", lambda e, a=a, nb=nb, th=th, j=j, tw=tw: e.tensor_tensor(out=acc[a][:, nb, th, 0:tw], in0=acc[a][:, nb, th, 0:tw], in1=gt[j][:, 0:tw], op=ALU.add),
                                 reads=[bacc[a][nb][th], bgt[j]], writes=[bacc[a][nb][th]])
                        else:
                            k = so.next()
                            P.op("pool", lambda e, a=a, nb=nb, th=th, j=j, k=k, tw=tw: e.tensor_tensor(out=so.t[k][:, 0:tw], in0=acc[a][:, nb, th, 0:tw], in1=gt[j][:, 0:tw], op=ALU.add),
                                 reads=[bacc[a][nb][th], bgt[j]], writes=[so.b[k]])
                            P.dma("sp", so.s[k], Md[ch, :, tt:tt + tw], so.t[k][:, 0:tw], reads=[so.b[k]], writes=[so.dd])
        return epi
    return factory


def build_B(cfg, last):
    nc = bass.Bass("TRN2", target_bir_lowering=False)
    K = Kx()
    K.nc, K.P = nc, Prog(nc)
    D, KC, NT, F, FC = cfg.D, cfg.KC, cfg.NT, cfg.F, cfg.FC

    def inp(name, shape, dt=F32):
        return nc.dram_tensor(name, list(shape), dt, kind="ExternalInput").ap()
    X = inp("x", [KC, 128, NT])
    Yd = inp("y", [44, 128, NT], BF16)
    MEMd = inp("mem", [KC, 128, 256])
    WG = inp("wg", [D, 3 * D])
    WBR = inp("wbr", [5632, D])
    WO = inp("wo", [D, D])
    WXQ, WXK, WXV = inp("wxq", [D, 512]), inp("wxk", [D, 512]), inp("wxv", [D, 512])
    WXO = inp("wxo", [512, D])
    WGU = inp("wgu", [D, 2 * F])
    WD = inp("wd", [F, D])
    gmix, gx, gmem, gffn = inp("gmix", [128, KC]), inp("gx", [128, KC]), inp("gmem", [128, KC]), inp("gffn", [128, KC])
    gfin = inp("gfin", [128, KC])
    OUT = nc.dram_tensor("out", [KC, 128, NT], F32, kind="ExternalOutput").ap()
    load_consts(K)
    Hd = dram(K, "b_h", [KC, 128, NT], BF16)
    Gd = dram(K, "b_g", [3 * KC, 128, NT], F32)
    Md = dram(K, "b_m", [KC, 128, NT], BF16)
    X1 = dram(K, "b_x1", [KC, 128, NT], F32)
    HX = dram(K, "b_hx", [KC, 128, NT], BF16)
    Qd = dram(K, "b_q", [4, 128, NT], BF16)
    HM = dram(K, "b_hm", [KC, 128, 256], BF16)
    KTd = dram(K, "b_kt", [4, 128, 256], BF16)
    VFd = dram(K, "b_vf", [4, 128, 256], BF16)
    Od = dram(K, "b_o", [4, 128, NT], BF16)
    X2 = dram(K, "b_x2", [KC, 128, NT], F32)
    HF = dram(K, "b_hf", [KC, 128, NT], BF16)
    ACTd = dram(K, "b_act", [FC, 128, NT], BF16)
    X3 = OUT if not last else dram(K, "b_x3", [KC, 128, NT], F32)

    def store_to(dst, dk, func=None):
        def route(un, nb):
            return dst, (un["c0"] + nb * 128) // 128, 128, dk, func
        return epi_store_factory(K, route)
    T1 = 1024 if KC * 1024 * 2 <= 65536 else 512
    st_rmsnorm(K, X, Hd, gmix, KC, NT, D)
    st_linear(K, Hd, KC, WG, simple_units(3 * D, KC), NT, T1, store_to(Gd, "f", AF.Sigmoid))
    units = []
    kofs = [0, 16, 28]
    knum = [16, 12, 16]
    for c0 in range(0, D, 256):
        for br in range(3):
            units.append(dict(c0=c0, nc=min(256, D - c0), k0=kofs[br], nk=knum[br], br=br))
    st_linear(K, Yd, 44, WBR, units, NT, 1024, epi_gated_factory(K, Gd, Md, KC))
    st_linear(K, Md, KC, WO, simple_units(D, KC), NT, T1, epi_residual_factory(K, X, X1))
    st_rmsnorm(K, X1, HX, gx, KC, NT, D)
    st_linear(K, HX, KC, WXQ, simple_units(512, KC), NT, T1, store_to(Qd, "b"))
    st_rmsnorm(K, MEMd, HM, gmem, KC, 256, D)
    st_linear(K, HM, KC, WXK, simple_units(512, KC), 256, 256, store_to(KTd, "b"))
    st_linear(K, HM, KC, WXV, simple_units(512, KC), 256, 256, store_to(VFd, "b"))
    st_xattn(K, Qd, KTd, VFd, Od, NT)
    st_linear(K, Od, 4, WXO, simple_units(D, 4), NT, 1024, epi_residual_factory(K, X1, X2))
    st_rmsnorm(K, X2, HF, gffn, KC, NT, D)
    st_linear(K, HF, KC, WGU, simple_units(2 * F, KC), NT, T1, epi_swiglu_factory(K, ACTd))
    Xh = dram(K, "b_xh", [KC, 128, NT], F32)
    F1 = FC // 2
    st_linear(K, ACTd, F1, WD, simple_units(D, F1, 0), NT, 1024, epi_residual_factory(K, X2, Xh), kbase=0)
    st_linear(K, ACTd, FC - F1, WD, simple_units(D, FC - F1, F1), NT, 1024, epi_residual_factory(K, Xh, X3), kbase=F1)
    if last:
        st_rmsnorm(K, X3, OUT, gfin, KC, NT, D, out_dt=F32)
    K.P.emit()
    return nc


def pack_B_weights(cfg, l, inp, cf, cb):
    D, F = cfg.D, cfg.F
    G0 = 3072 + 13824 + 6176
    wg = np.ascontiguousarray(inp["w_in"][l][:, G0:G0 + 3 * D])
    fg = inp["ffn_gate"][l].reshape(D, F // 128, 128)
    fu = inp["ffn_up"][l].reshape(D, F // 128, 128)
    wgu = np.ascontiguousarray(np.stack([fg, fu], axis=2).reshape(D, 2 * F))

    def gl(v):
        return np.ascontiguousarray(v.reshape(cfg.KC, 128).T)
    return {
        "wg": wg, "wbr": inp["w_branch"][l], "wo": inp["w_out"][l],
        "wxq": inp["xattn_q"][l], "wxk": inp["xattn_k"][l], "wxv": inp["xattn_v"][l], "wxo": inp["xattn_o"][l],
        "wgu": wgu, "wd": inp["ffn_down"][l],
        "gmix": gl(inp["norm_mix"][l]), "gx": gl(inp["norm_x"][l]), "gmem": gl(inp["norm_mem"][l]),
        "gffn": gl(inp["norm_ffn"][l]), "gfin": gl(inp["norm_final"]),
        "cst_f": cf, "cst_b": cb,
    }


_PROGS = {}
DEBUG_STORE = {}


def get_prog(key, builder):
    if key not in _PROGS:
        _PROGS[key] = builder()
    return _PROGS[key]


def run_model(cfg, inp):
    cf, cb = host_consts()
    B, S, D, KC, NT = cfg.B, cfg.S, cfg.D, cfg.KC, cfg.NT
    assert B * 4 == NCORES
    xT = [np.ascontiguousarray(np.asarray(inp["x"][b], np.float32).T).reshape(KC, 128, S) for b in range(B)]
    memT = [np.ascontiguousarray(np.asarray(inp["mem"][b], np.float32).T).reshape(KC, 128, 256) for b in range(B)]
    inp = {k: np.asarray(v, np.float32) for k, v in inp.items()}
    for l in range(2):
        ncA = get_prog(("A", cfg.D, cfg.S), lambda: build_A(cfg))
        in_maps = [pack_A_inputs(cfg, l, c % 4, xT[c // 4], inp, cf, cb) for c in range(NCORES)]
        resA = run_bass_kernel_spmd(ncA, in_maps, core_ids=list(range(NCORES))).results
        del in_maps
        Yb = []
        for b in range(B):
            ya = np.concatenate([resA[b * 4 + g]["y"][0:4] for g in range(4)], axis=0)
            yb = np.concatenate([resA[b * 4 + g]["y"][4:7] for g in range(4)], axis=0)
            yc = np.concatenate([resA[b * 4 + g]["y"][7:11] for g in range(4)], axis=0)
            Yb.append(np.concatenate([ya, yb, yc], axis=0))
        DEBUG_STORE[f'Y{l}'] = Yb
        last = (l == 1)
        ncB = get_prog(("B", cfg.D, cfg.S, last), lambda: build_B(cfg, last))
        wts = pack_B_weights(cfg, l, inp, cf, cb)
        in_maps = []
        for c in range(NCORES):
            b, qd = c // 4, c % 4
            m = dict(wts)
            m["x"] = np.ascontiguousarray(xT[b][:, :, qd * NT:(qd + 1) * NT])
            m["y"] = np.ascontiguousarray(Yb[b][:, :, qd * NT:(qd + 1) * NT])
            m["mem"] = memT[b]
            in_maps.append(m)
        resB = run_bass_kernel_spmd(ncB, in_maps, core_ids=list(range(NCORES))).results
        del in_maps, wts
        xT = [np.concatenate([resB[b * 4 + qd]["out"] for qd in range(4)], axis=2) for b in range(B)]
        DEBUG_STORE[f'X{l}'] = xT
    out = np.stack([xT[b].reshape(D, S).T for b in range(B)], axis=0)
    return np.ascontiguousarray(out.astype(np.float32))


def kernel(**inputs):
    cfg = Cfg()
    return run_model(cfg, inputs)
```
